# Optimizing a Trainium2 kernel written in Bass

```python
import jax, jax.numpy as jnp
from jax import lax
import numpy as np

D_MODEL = 1024
BATCH = 8
SEQ = 4096
DEPTH = 4

POOL_WINDOWS = (2, 4, 8, 16)
POOL_GROUPS = len(POOL_WINDOWS)
POOL_WIDTH = D_MODEL // 2
POOL_GW = POOL_WIDTH // POOL_GROUPS
N_HEADS = 16
HEAD_DIM = 64
N_KV_GROUPS = 2
HPG = N_HEADS // N_KV_GROUPS
NSA_WIDTH = N_HEADS * HEAD_DIM
KV_WIDTH = N_KV_GROUPS * HEAD_DIM
CMP_LEN = 32
CMP_STRIDE = 16
CMP_HIDDEN = 256
SEL_BLOCK = 64
N_SEL = 16
WINDOW = 512
Q_BLOCK = 64
SEL_BONUS = 1e4
NEG_INF = -1e30
ROPE_THETA = 10000.0
D_FF = 4 * D_MODEL
RMS_EPS = 1e-6
IN_SIZES = (POOL_WIDTH, NSA_WIDTH, KV_WIDTH, KV_WIDTH, KV_WIDTH, KV_WIDTH, KV_WIDTH, KV_WIDTH, 3 * N_HEADS, 2 * D_MODEL)
N_IN = sum(IN_SIZES)
IN_SPLITS = tuple(int(v) for v in np.cumsum(IN_SIZES)[:-1])

kernel_name = 'hybrid_pool_nsa_gated_trunk'


def rms_norm(x, g):
    xf = x.astype(jnp.float32)
    y = xf * lax.rsqrt(jnp.mean(xf * xf, axis=-1, keepdims=True) + RMS_EPS)
    return (y * g.astype(jnp.float32)).astype(x.dtype)


def rope_tables(pos):
    inv = ROPE_THETA ** (-jnp.arange(0, HEAD_DIM, 2, dtype=jnp.float32) / HEAD_DIM)
    ang = pos.astype(jnp.float32)[:, None] * inv[None, :]
    ang = jnp.concatenate([ang, ang], axis=-1)
    return jnp.cos(ang), jnp.sin(ang)


def apply_rope(x, cos, sin):
    x1, x2 = jnp.split(x, 2, axis=-1)
    rot = jnp.concatenate([-x2, x1], axis=-1)
    y = x.astype(jnp.float32) * cos[:, None, :] + rot.astype(jnp.float32) * sin[:, None, :]
    return y.astype(x.dtype)


def masked_softmax(s, mask):
    s = jnp.where(mask, s.astype(jnp.float32), NEG_INF)
    return jnp.where(mask, jax.nn.softmax(s, axis=-1), 0.0)


def pool_mixer(u, w_pool, pool_scale):
    B, S, _ = u.shape
    uf = u.astype(jnp.float32)
    csum = jnp.pad(jnp.cumsum(uf, axis=1), ((0, 0), (1, 0), (0, 0)))
    t = jnp.arange(S)
    diffs = []
    for g, w in enumerate(POOL_WINDOWS):
        c = csum[:, :, g * POOL_GW:(g + 1) * POOL_GW]
        upper = c[:, 1:]
        lower = jnp.pad(c, ((0, 0), (w - 1, 0), (0, 0)))[:, :S]
        count = jnp.minimum(t + 1, w).astype(jnp.float32)[None, :, None]
        diffs.append((upper - lower) / count - uf[:, :, g * POOL_GW:(g + 1) * POOL_GW])
    d = jnp.stack(diffs, axis=2)
    y = jnp.einsum('bsgc,gcd->bsgd', d, w_pool.astype(jnp.float32)).reshape(B, S, POOL_WIDTH)
    return (y * pool_scale.astype(jnp.float32)).astype(u.dtype)


def compress(k, pe, w1, w2):
    B, S, G, dh = k.shape
    r = CMP_LEN // CMP_STRIDE
    n_chunks = S // CMP_STRIDE
    n_cmp = n_chunks - r + 1
    ch = k.reshape(B, n_chunks, CMP_STRIDE, G, dh)
    blocks = jnp.concatenate([ch[:, j:j + n_cmp] for j in range(r)], axis=2)
    blocks = blocks + pe[None, None, :, None, :]
    flat = blocks.transpose(0, 1, 3, 2, 4).reshape(B, n_cmp, G, CMP_LEN * dh)
    return jax.nn.gelu(flat @ w1) @ w2


def nsa_mixer(q, kc, vc, ks, vs, kw, vw, g_nsa, pe_k, pe_v, w_ck1, w_ck2, w_cv1, w_cv2):
    B, S = q.shape[:2]
    G, dh = N_KV_GROUPS, HEAD_DIM
    cos, sin = rope_tables(jnp.arange(S))
    q = apply_rope(q.reshape(B, S, N_HEADS, dh), cos, sin).reshape(B, S, G, HPG, dh)
    ks = apply_rope(ks.reshape(B, S, G, dh), cos, sin)
    kw = apply_rope(kw.reshape(B, S, G, dh), cos, sin)
    vs = vs.reshape(B, S, G, dh)
    vw = vw.reshape(B, S, G, dh)
    k_cmp = compress(kc.reshape(B, S, G, dh), pe_k, w_ck1, w_ck2)
    v_cmp = compress(vc.reshape(B, S, G, dh), pe_v, w_cv1, w_cv2)
    n_cmp = k_cmp.shape[1]
    cmp_start = jnp.arange(n_cmp) * CMP_STRIDE
    cmp_end = cmp_start + CMP_LEN - 1
    ccos, csin = rope_tables(cmp_end)
    k_cmp = apply_rope(k_cmp, ccos, csin)
    n_slc = S // SEL_BLOCK
    k_blk = ks.reshape(B, n_slc, SEL_BLOCK, G, dh).transpose(0, 3, 1, 2, 4)
    v_blk = vs.reshape(B, n_slc, SEL_BLOCK, G, dh).transpose(0, 3, 1, 2, 4)
    slc_start = jnp.arange(n_slc) * SEL_BLOCK
    slc_idx = jnp.arange(n_slc)
    overlap = ((cmp_start[:, None] <= slc_start[None, :] + SEL_BLOCK - 1)
               & (cmp_end[:, None] >= slc_start[None, :])).astype(jnp.float32)
    n_pick = min(N_SEL, n_slc)
    gather_blocks = jax.vmap(jax.vmap(lambda kb, i: kb[i]))
    win_len = Q_BLOCK + WINDOW - 1
    kw_pad = jnp.pad(kw, ((0, 0), (WINDOW - 1, 0), (0, 0), (0, 0)))
    vw_pad = jnp.pad(vw, ((0, 0), (WINDOW - 1, 0), (0, 0), (0, 0)))
    scale = HEAD_DIM ** -0.5
    gates = jax.nn.sigmoid(g_nsa.astype(jnp.float32)).reshape(B, S, G, HPG, 3)

    def block(s0):
        t = s0 + jnp.arange(Q_BLOCK)
        qb = lax.dynamic_slice_in_dim(q, s0, Q_BLOCK, axis=1)
        gb = lax.dynamic_slice_in_dim(gates, s0, Q_BLOCK, axis=1)
        s_c = jnp.einsum('bqghd,bngd->bghqn', qb, k_cmp) * scale
        p_c = masked_softmax(s_c, cmp_end[None, :] <= t[:, None])
        o_c = jnp.einsum('bghqn,bngd->bqghd', p_c, v_cmp)
        imp = jnp.einsum('bghqn,nj->bgqj', p_c, overlap)
        causal = slc_start[None, :] <= t[:, None]
        cur = t // SEL_BLOCK
        forced = causal & ((slc_idx[None, :] == 0) | (slc_idx[None, :] >= cur[:, None] - 1))
        score = jnp.where(forced, SEL_BONUS, jnp.where(causal, imp, NEG_INF))
        top_v, idx = lax.top_k(score, n_pick)
        sel_ok = top_v > 0.5 * NEG_INF
        idx_flat = idx.reshape(B, G, Q_BLOCK * n_pick)
        k_sel = gather_blocks(k_blk, idx_flat).reshape(B, G, Q_BLOCK, n_pick * SEL_BLOCK, dh)
        v_sel = gather_blocks(v_blk, idx_flat).reshape(B, G, Q_BLOCK, n_pick * SEL_BLOCK, dh)
        key_pos = idx[..., None] * SEL_BLOCK + jnp.arange(SEL_BLOCK)
        ok = (sel_ok[..., None] & (key_pos <= t[:, None, None])).reshape(B, G, Q_BLOCK, n_pick * SEL_BLOCK)
        s_s = jnp.einsum('bqghd,bgqkd->bghqk', qb, k_sel) * scale
        p_s = masked_softmax(s_s, ok[:, :, None])
        o_s = jnp.einsum('bghqk,bgqkd->bqghd', p_s, v_sel)
        kwb = lax.dynamic_slice_in_dim(kw_pad, s0, win_len, axis=1)
        vwb = lax.dynamic_slice_in_dim(vw_pad, s0, win_len, axis=1)
        kpos = s0 - (WINDOW - 1) + jnp.arange(win_len)
        wmask = (kpos[None, :] <= t[:, None]) & (kpos[None, :] > t[:, None] - WINDOW) & (kpos[None, :] >= 0)
        s_w = jnp.einsum('bqghd,bkgd->bghqk', qb, kwb) * scale
        p_w = masked_softmax(s_w, wmask)
        o_w = jnp.einsum('bghqk,bkgd->bqghd', p_w, vwb)
        o = gb[..., 0:1] * o_c + gb[..., 1:2] * o_s + gb[..., 2:3] * o_w
        return o.reshape(B, Q_BLOCK, NSA_WIDTH)

    out = lax.map(block, jnp.arange(S // Q_BLOCK) * Q_BLOCK)
    return out.transpose(1, 0, 2, 3).reshape(B, S, NSA_WIDTH)


def setup_inputs(seed: int = 0) -> dict:
    key = jax.random.key(seed)
    ks = jax.random.split(key, 20)
    f32 = jnp.float32

    def nrm(k, shape, fan_in):
        return jax.random.normal(k, shape, f32) * (fan_in ** -0.5)

    return {
        'x': jax.random.normal(ks[0], (BATCH, SEQ, D_MODEL), f32),
        'norm_mix': 1.0 + 0.05 * jax.random.normal(ks[1], (DEPTH, D_MODEL), f32),
        'w_in': nrm(ks[2], (DEPTH, D_MODEL, N_IN), D_MODEL),
        'w_pool': nrm(ks[3], (DEPTH, POOL_GROUPS, POOL_GW, POOL_GW), POOL_GW),
        'pool_scale': 1.0 + 0.1 * jax.random.normal(ks[4], (DEPTH, POOL_WIDTH), f32),
        'pe_k': 0.1 * jax.random.normal(ks[5], (DEPTH, CMP_LEN, HEAD_DIM), f32),
        'pe_v': 0.1 * jax.random.normal(ks[6], (DEPTH, CMP_LEN, HEAD_DIM), f32),
        'w_ck1': nrm(ks[7], (DEPTH, CMP_LEN * HEAD_DIM, CMP_HIDDEN), CMP_LEN * HEAD_DIM),
        'w_ck2': nrm(ks[8], (DEPTH, CMP_HIDDEN, HEAD_DIM), CMP_HIDDEN),
        'w_cv1': nrm(ks[9], (DEPTH, CMP_LEN * HEAD_DIM, CMP_HIDDEN), CMP_LEN * HEAD_DIM),
        'w_cv2': nrm(ks[10], (DEPTH, CMP_HIDDEN, HEAD_DIM), CMP_HIDDEN),
        'w_proj_pool': nrm(ks[11], (DEPTH, POOL_WIDTH, D_MODEL), POOL_WIDTH),
        'w_proj_nsa': nrm(ks[12], (DEPTH, NSA_WIDTH, D_MODEL), NSA_WIDTH),
        'w_out': nrm(ks[13], (DEPTH, D_MODEL, D_MODEL), D_MODEL),
        'norm_mlp': 1.0 + 0.05 * jax.random.normal(ks[14], (DEPTH, D_MODEL), f32),
        'w_ff1': nrm(ks[15], (DEPTH, D_MODEL, D_FF), D_MODEL),
        'w_ff2': nrm(ks[16], (DEPTH, D_FF, D_MODEL), D_FF),
        'norm_final': 1.0 + 0.05 * jax.random.normal(ks[17], (D_MODEL,), f32),
    }


def reference(x, norm_mix, w_in, w_pool, pool_scale, pe_k, pe_v, w_ck1, w_ck2, w_cv1, w_cv2,
              w_proj_pool, w_proj_nsa, w_out, norm_mlp, w_ff1, w_ff2, norm_final):
    for l in range(DEPTH):
        h = rms_norm(x, norm_mix[l])
        proj = h @ w_in[l]
        u, q, kc, vc, ksl, vsl, kwn, vwn, g_nsa, g_merge = jnp.split(proj, IN_SPLITS, axis=-1)
        y_pool = pool_mixer(u, w_pool[l], pool_scale[l])
        y_nsa = nsa_mixer(q, kc, vc, ksl, vsl, kwn, vwn, g_nsa, pe_k[l], pe_v[l],
                          w_ck1[l], w_ck2[l], w_cv1[l], w_cv2[l]).astype(x.dtype)
        g_a, g_b = jnp.split(jax.nn.sigmoid(g_merge.astype(jnp.float32)).astype(x.dtype), 2, axis=-1)
        merged = g_a * (y_pool @ w_proj_pool[l]) + g_b * (y_nsa @ w_proj_nsa[l])
        x = x + merged @ w_out[l]
        h = rms_norm(x, norm_mlp[l])
        x = x + jnp.square(jax.nn.relu(h @ w_ff1[l])) @ w_ff2[l]
    return rms_norm(x, norm_final)
```

```python
import numpy as np
import collections
from contextlib import ExitStack
import ml_dtypes
import concourse.bass as bass
import concourse.mybir as mybir
from concourse.bass_utils import run_bass_kernel_spmd

F32 = mybir.dt.float32
BF16 = mybir.dt.bfloat16
ALU = mybir.AluOpType
AF = mybir.ActivationFunctionType
AX = mybir.AxisListType

NDS = 20
LIMIT = None
NDS_SW = 6


class Buf:
    __slots__ = ("w", "r", "excl")

    def __init__(self, excl=False):
        self.w = None
        self.r = {}
        self.excl = excl


class Sched:
    def __init__(self, nc, es, same_engine_sync=True):
        self.nc = nc
        self.es = es
        self.eng = {"pe": nc.tensor, "act": nc.scalar, "dve": nc.vector, "pool": nc.gpsimd, "sp": nc.sync}
        self.same = same_engine_sync
        self.gen = 0
        self.dsem = [es.enter_context(nc.semaphore(f"dq{i}")) for i in range(NDS)]
        self.dcnt = [0] * NDS
        self.dnext = {False: 0, True: 0}
        self.seen = {k: {} for k in self.eng}
        self.total = 0
        self.limit = None
        self.new_epoch()

    def new_epoch(self):
        self.sem = {k: self.es.enter_context(self.nc.semaphore(f"e{self.gen}_{k}")) for k in self.eng}
        self.cnt = {k: 0 for k in self.eng}
        self.gen += 1

    def _wait(self, e, ev):
        sem, val = ev
        if sem is self.sem[e] and (e == "pe" or not self.same):
            return
        if self.seen[e].get(sem, 0) >= val:
            return
        self.seen[e][sem] = val
        self.eng[e].wait_ge(sem, val)

    def _deps(self, e, reads, writes):
        for b in reads:
            if b.w is not None:
                self._wait(e, b.w)
        for b in writes:
            if b.w is not None:
                self._wait(e, b.w)
            for sem, val in b.r.items():
                self._wait(e, (sem, val))

    def _commit(self, ev, reads, writes):
        for b in reads:
            if b.r.get(ev[0], 0) < ev[1]:
                b.r[ev[0]] = ev[1]
        for b in writes:
            b.w = ev
            b.r = {}

    def op(self, e, fn, reads=(), writes=()):
        self.total += 1
        if self.limit is not None and self.total > self.limit:
            return
        if any(b.excl for b in reads):
            writes = list(writes) + [b for b in reads if b.excl]
            reads = [b for b in reads if not b.excl]
        self._deps(e, reads, writes)
        ins = fn(self.eng[e])
        self.cnt[e] += 1
        ins.then_inc(self.sem[e], 1)
        self._commit((self.sem[e], self.cnt[e]), reads, writes)

    def dma(self, e, out, in_, reads=(), writes=(), **kw):
        self.total += 1
        if self.limit is not None and self.total > self.limit:
            return
        lo, n = (0, NDS_SW) if e == "pool" else (NDS_SW, NDS - NDS_SW)
        k = lo + self.dnext[e == "pool"] % n
        self.dnext[e == "pool"] += 1
        if self.dcnt[k] > 0:
            self._wait(e, (self.dsem[k], 16 * self.dcnt[k]))
        self._deps(e, reads, writes)
        self.eng[e].dma_start(out=out, in_=in_, **kw).then_inc(self.dsem[k], 16)
        self.dcnt[k] += 1
        self._commit((self.dsem[k], 16 * self.dcnt[k]), reads, writes)

    def barrier(self):
        evs = [(self.sem[k], self.cnt[k]) for k in self.eng if self.cnt[k] > 0]
        evs += [(self.dsem[i], 16 * self.dcnt[i]) for i in range(NDS) if self.dcnt[i] > 0]
        for e in self.eng:
            for ev in evs:
                if ev[0] is self.sem[e]:
                    continue
                self._wait(e, ev)


D = 1024
NIN = 4400
C_Q = 512
C_KC, C_VC, C_KS, C_VS, C_KW, C_VW, C_GN, C_GM = 1536, 1664, 1792, 1920, 2048, 2176, 2304, 2352
POOLW = (2, 4, 8, 16)
GELU_C = 1.5957691216057308


def host_consts(S):
    NQ, NCP, NJ = S // 128, S // 16, S // 64
    NCH = NCP // 128
    bf = ml_dtypes.bfloat16
    c = {}
    c["ident"] = np.eye(128, dtype=np.float32).astype(bf)
    Rm = np.zeros((128, 128), np.float32)
    for hb in (0, 64):
        for d in range(32):
            Rm[hb + d + 32, hb + d] = -1.0
            Rm[hb + d, hb + d + 32] = 1.0
    c["Rm"] = Rm.astype(bf)
    inv = (10000.0 ** (-np.arange(0, 64, 2, dtype=np.float32) / 64)).astype(np.float32)
    inv64 = np.concatenate([inv, inv])
    inv128 = np.concatenate([inv64, inv64])
    pos = np.arange(S, dtype=np.float32)
    ang = (pos[None, :] * inv128[:, None]).astype(np.float32)
    c["cosT"] = np.cos(ang).astype(np.float32)
    c["sinT"] = np.sin(ang).astype(np.float32)
    cpos = (np.arange(NCP, dtype=np.float32) * 16 + 31)
    cang = (cpos[None, :] * inv128[:, None]).astype(np.float32)
    c["ccos"] = np.cos(cang).astype(np.float32)
    c["csin"] = np.sin(cang).astype(np.float32)
    key = np.arange(S)
    c["Emat"] = (key[None, :] // 64 == np.arange(64)[:, None]).astype(np.float32).astype(bf)
    r = np.arange(128)
    c["mdiag"] = (r[:, None] <= r[None, :]).astype(np.float32).astype(bf)
    c["mfar"] = (r[:, None] > r[None, :]).astype(np.float32).astype(bf)
    nd = np.where(r[:, None] > r[None, :], -30000.0, 0.0).astype(np.float32)
    c["negdiag4"] = np.repeat(nd[:, None, :], 4, axis=1).astype(bf)
    n = np.arange(NCP)
    j = np.arange(64)
    ovl = ((16 * n[:, None] <= 64 * j[None, :] + 63) & (16 * n[:, None] + 31 >= 64 * j[None, :])).astype(np.float32)
    ovl[NCP - 1, :] = 0.0
    c["ovl"] = ovl.reshape(NCH, 128, 64).transpose(1, 0, 2).copy().astype(bf)
    t = np.arange(S)
    cm = (16 * n[:, None] + 31 <= t[None, :]).astype(np.float32)
    cm[NCP - 1, :] = 0.0
    c["cmask"] = cm.reshape(NCH, 128, NQ, 128).transpose(2, 1, 0, 3).copy().astype(bf)
    cur = t // 64
    causal = (64 * j[None, :] <= t[:, None])
    forced = causal & ((j[None, :] == 0) | (j[None, :] >= cur[:, None] - 1))
    A = (causal & ~forced).astype(np.float32)
    Bt = np.where(forced, 100.0 + j[None, :], np.where(causal, 0.0, -1.0 - j[None, :])).astype(np.float32)
    AB = np.stack([A, A, Bt, Bt], axis=1)
    c["tkAB"] = AB.reshape(NQ, 128, 4, 64).astype(np.float32)
    corr = np.ones((128, 4, 16), np.float32)
    for g, w in enumerate(POOLW):
        for tt in range(16):
            corr[:, g, tt] = w / min(tt + 1, w)
    c["pcorr"] = corr
    return c


def build(S=4096, depth=4, dbg=False, stop_after=None, same_sync=True):
    nc = bass.Bass("TRN2", target_bir_lowering=False)
    NT, NQ, NCP, NJ = S // 512, S // 128, S // 16, S // 64
    NCMP = NCP - 1
    NCH = NCP // 128
    NT2 = S // 256

    def din(name, shape, dt=F32):
        return nc.dram_tensor(name, shape, dt, kind="ExternalInput").ap()

    def dscr(name, shape, dt):
        return nc.dram_tensor(name, shape, dt, kind=("ExternalOutput" if dbg else "Internal")).ap()

    x_in = din("x", [S, D])
    norm_mix = din("norm_mix", [depth, D])
    w_in = din("w_in", [depth, D, NIN])
    w_pool = din("w_pool", [depth, 4, 128, 128])
    pool_scale = din("pool_scale", [depth, 512])
    pe_k = din("pe_k", [depth, 32, 64])
    pe_v = din("pe_v", [depth, 32, 64])
    w_ck1 = din("w_ck1", [depth, 2048, 256])
    w_ck2 = din("w_ck2", [depth, 256, 64])
    w_cv1 = din("w_cv1", [depth, 2048, 256])
    w_cv2 = din("w_cv2", [depth, 256, 64])
    w_pp = din("w_proj_pool", [depth, 512, D])
    w_pn = din("w_proj_nsa", [depth, D, D])
    w_out = din("w_out", [depth, D, D])
    norm_mlp = din("norm_mlp", [depth, D])
    w_ff1 = din("w_ff1", [depth, D, 4096])
    w_ff2 = din("w_ff2", [depth, 4096, D])
    norm_final = din("norm_final", [D])
    c_ident = din("ident", [128, 128], BF16)
    c_Rm = din("Rm", [128, 128], BF16)
    c_cos = din("cosT", [128, S])
    c_sin = din("sinT", [128, S])
    c_ccos = din("ccos", [128, NCP])
    c_csin = din("csin", [128, NCP])
    c_E = din("Emat", [64, S], BF16)
    c_mdiag = din("mdiag", [128, 128], BF16)
    c_mfar = din("mfar", [128, 128], BF16)
    c_negdiag4 = din("negdiag4", [128, 4, 128], BF16)
    c_ovl = din("ovl", [128, NCH, 64], BF16)
    c_cmask = din("cmask", [NQ, 128, NCH, 128], BF16)
    c_tkAB = din("tkAB", [NQ, 128, 4, 64])
    c_pcorr = din("pcorr", [128, 4, 16])
    out = nc.dram_tensor("out", [S, D], F32, kind="ExternalOutput").ap()

    xres = dscr("xres", [S, D], F32)
    qT = dscr("qT", [D, S], BF16)
    kk = dscr("kk", [4, 128, S], BF16)
    kcvc = dscr("kcvc", [2, 128, S], BF16)
    vtok = dscr("vtok", [2, S, 130], BF16)
    gatesd = dscr("gatesd", [S, 48], F32)
    gmT = dscr("gmT", [2048, S], BF16)
    ypT = dscr("ypT", [512, S], BF16)
    ynT = dscr("ynT", [D, S], BF16)
    dbg_imp = dscr("dbg_imp", [S, 2, 64], F32) if dbg else None
    dbg_sel = dscr("dbg_sel", [S, 2, 64], F32) if dbg else None
    dbg_kcm = dscr("dbg_kcm", [128, 2, NCP], F32) if dbg else None
    dbg_vcm = dscr("dbg_vcm", [128, NCH, 2, 129], F32) if dbg else None

    top = ExitStack()
    with top:
        Sx = Sched(nc, top, same_engine_sync=same_sync)
        Sx.limit = LIMIT

        gcnt = [0]

        def mkT(es):
            cnt = gcnt

            def T(shape, dt, name=None):
                cnt[0] += 1
                nm = f"{name or 't'}_{cnt[0]}"
                return es.enter_context(nc.sbuf_tensor(nm, shape, dt))
            return T

        T0 = mkT(top)
        pbank = [top.enter_context(nc.psum_tensor(f"pb{i}", [128, 512], F32)) for i in range(8)]
        pbuf = [Buf(True) for _ in range(8)]

        class Rot:
            def __init__(self, idxs):
                self.idxs = list(idxs)
                self.i = 0

            def next(self):
                k = self.idxs[self.i % len(self.idxs)]
                self.i += 1
                return pbank[k], pbuf[k]

        class Ring:
            def __init__(self, T, n, shape, dt, name):
                self.t = [T(shape, dt, name) for _ in range(n)]
                self.b = [Buf() for _ in range(n)]
                self.i = 0

            def next(self):
                k = self.i % len(self.t)
                self.i += 1
                return self.t[k], self.b[k]

        ident = T0([128, 128], BF16, "ident"); b_ident = Buf()
        Rm = T0([128, 128], BF16, "Rm"); b_Rm = Buf()
        mdiag = T0([128, 128], BF16, "mdiag"); b_md = Buf()
        mfar = T0([128, 128], BF16, "mfar"); b_mf = Buf()
        epst = T0([128, 1], F32, "eps"); b_eps = Buf()
        gmix = T0([128, depth, 8], F32, "gmix"); b_gmix = Buf()
        gmlp = T0([128, depth, 8], F32, "gmlp"); b_gmlp = Buf()
        pscl = T0([128, depth, 4], F32, "pscl"); b_pscl = Buf()
        Sx.dma("sp", ident[:], c_ident, [], [b_ident])
        Sx.dma("sp", Rm[:], c_Rm, [], [b_Rm])
        Sx.dma("sp", mdiag[:], c_mdiag, [], [b_md])
        Sx.dma("sp", mfar[:], c_mfar, [], [b_mf])
        negdiag4 = T0([128, 4, 128], BF16, "negdiag4"); b_nd4 = Buf()
        Sx.dma("sp", negdiag4[:], c_negdiag4, [], [b_nd4])
        Sx.op("pool", lambda e: e.memset(epst[:], 1e-6), [], [b_eps])
        for l in range(depth):
            Sx.dma("sp", gmix[:, l, :], norm_mix[l].rearrange("(c p) -> p c", p=128), [], [b_gmix], allow_slow_non_contiguous=True)
            Sx.dma("sp", gmlp[:, l, :], norm_mlp[l].rearrange("(c p) -> p c", p=128), [], [b_gmlp], allow_slow_non_contiguous=True)
            Sx.dma("sp", pscl[:, l, :], pool_scale[l].rearrange("(c p) -> p c", p=128), [], [b_pscl], allow_slow_non_contiguous=True)

        b_xres = [Buf() for _ in range(NT2)]
        b_q = [Buf() for _ in range(NT)]
        b_kk = [Buf() for _ in range(NT)]
        b_kcvc = [Buf() for _ in range(NT)]
        b_vtok = [Buf() for _ in range(NT)]
        b_gates = [Buf() for _ in range(NT)]
        b_gm = [Buf() for _ in range(NT)]
        b_yp = [Buf() for _ in range(NT)]
        b_yn = [Buf() for _ in range(NT)]

        def xres_bufs(t0, n):
            return b_xres[t0 // 256:(t0 + n) // 256]

        def norm_stats(xt, bxt, ntj, hn, bhn, ss, bss):
            for j in range(ntj):
                Sx.op("act", lambda e, j=j: e.activation(out=hn[:, j, :], in_=xt[:, j, :], func=AF.Square, accum_out=ss[:, j:j + 1]),
                      [bxt], [bhn, bss])
            Sx.op("act", lambda e: e.activation(out=ss[:, 4:4 + ntj], in_=ss[:, 0:ntj], func=AF.Sqrt, scale=1.0 / D, bias=epst[:, 0:1]),
                  [bss, b_eps], [bss])
            Sx.op("dve", lambda e: e.reciprocal(out=ss[:, 4:4 + ntj], in_=ss[:, 4:4 + ntj]), [bss], [bss])
            for j in range(ntj):
                Sx.op("dve", lambda e, j=j: e.tensor_scalar(out=hn[:, j, :], in0=xt[:, j, :], scalar1=ss[:, 4 + j:5 + j], scalar2=None, op0=ALU.mult),
                      [bxt, bss], [bhn])

        def norm_tr(ntj, gcol, bg, hn, bhn, hT, bhT, rot):
            for c in range(8):
                pt, pb = rot.next()
                pv = pt[:].bitcast(BF16)
                for j in range(ntj):
                    Sx.op("pe", lambda e, j=j, c=c, pv=pv: e.transpose(out=pv[:, j * 128:(j + 1) * 128], in_=hn[:, j, c * 128:(c + 1) * 128], identity=ident[:]),
                          [bhn, b_ident], [pb])
                Sx.op("dve", lambda e, c=c, pv=pv: e.tensor_scalar(out=hT[:, c, :], in0=pv[:, 0:ntj * 128], scalar1=gcol[:, c:c + 1], scalar2=None, op0=ALU.mult),
                      [pb, bg], [bhT])

        def phase_A(l, xsrc):
            with ExitStack() as es:
                T = mkT(es)
                rotA = Rot([0, 1, 2, 3])
                rotR = Rot([4, 5])
                rotM = Rot([6, 7])
                wA = T([128, 8, NIN], BF16, "wA"); b_w = [Buf() for _ in range(8)]
                for k in range(8):
                    Sx.dma("pool", wA[:, k, :], w_in[l, k * 128:(k + 1) * 128, :], [], [b_w[k]])
                wdup = T([128, 8, 4, 128], BF16, "wdup"); b_wdup = Buf()
                for i, cb in enumerate((C_KS, C_KS + 64, C_KW, C_KW + 64)):
                    Sx.op("pool", lambda e, i=i, cb=cb: e.tensor_copy(
                        out=wdup[:, :, i, :].rearrange("p k (two d) -> p k two d", two=2),
                        in_=wA[:, :, cb:cb + 64].unsqueeze(2).to_broadcast([128, 8, 2, 64])), b_w, [b_wdup])
                wpl = T([128, 4, 128], BF16, "wpl"); b_wpl = Buf()
                Sx.dma("pool", wpl[:], w_pool[l].rearrange("g c d -> c g d"), [], [b_wpl])
                pcorr = T([128, 4, 16], F32, "pcorr"); b_pc = Buf()
                Sx.dma("sp", pcorr[:], c_pcorr, [], [b_pc])
                xt = T([128, 4, D], F32, "xt"); b_xt = Buf()
                hn = T([128, 4, D], BF16, "hn"); b_hn = Buf()
                hT = T([128, 8, 512], BF16, "hT"); b_hT = Buf()
                ss = T([128, 8], F32, "ss"); b_ss = Buf()
                csr = Ring(T, 2, [128, 2, 512], F32, "cs")
                ubuf = T([128, 4, 528], F32, "ubuf"); b_u = [Buf() for _ in range(4)]
                ptmp = Ring(T, 2, [128, 528], F32, "ptmp")
                dTr = Ring(T, 2, [128, 512], BF16, "dT")
                xbr = Ring(T, 2, [128, 512], BF16, "xb")
                t1r = Ring(T, 2, [128, 512], F32, "t1")
                t2r = Ring(T, 2, [128, 512], F32, "t2")
                qst = T([128, 8, 512], BF16, "qst"); b_qst = Buf()
                kst = T([128, 4, 512], BF16, "kst"); b_kst = Buf()
                cst = T([128, 2, 512], BF16, "cst"); b_cst = Buf()
                gst = T([128, 16, 512], BF16, "gst"); b_gst = Buf()
                yst = T([128, 4, 512], BF16, "yst"); b_yst = Buf()
                vst = T([128, 4, 2, 130], BF16, "vst"); b_vst = Buf()
                gts = T([128, 4, 48], F32, "gts"); b_gts = Buf()
                Sx.op("pool", lambda e: e.memset(vst[:], 1.0), [], [b_vst])
                Sx.op("pool", lambda e: e.memset(ubuf[:], 0.0), [], b_u)

                def rope_to(pt, pb, dest, bdest, cs, b_cs):
                    xb, bxb = xbr.next()
                    Sx.op("act", lambda e: e.activation(out=xb[:], in_=pt[:, 0:512], func=AF.Copy), [pb], [bxb])

                    def part2():
                        p2, pb2 = rotR.next()
                        Sx.op("pe", lambda e: e.matmul(p2[:, 0:512], lhsT=Rm[:], rhs=xb[:], start=True, stop=True), [bxb, b_Rm], [pb2])
                        t1, bt1 = t1r.next()
                        t2, bt2 = t2r.next()
                        Sx.op("dve", lambda e: e.tensor_tensor(out=t1[:], in0=pt[:, 0:512], in1=cs[:, 0, :], op=ALU.mult), [pb, b_cs], [bt1])
                        Sx.op("dve", lambda e: e.tensor_tensor(out=t2[:], in0=p2[:, 0:512], in1=cs[:, 1, :], op=ALU.mult), [pb2, b_cs], [bt2])
                        Sx.op("pool", lambda e: e.tensor_tensor(out=dest, in0=t1[:], in1=t2[:], op=ALU.add), [bt1, bt2], [bdest])
                    return part2

                def prepA(tt):
                    tsl_ = slice(tt * 512, tt * 512 + 512)
                    Sx.dma("sp", xt[:], xsrc[tsl_].rearrange("(j p) f -> p j f", p=128), xres_bufs(tt * 512, 512) if xsrc is xres else [], [b_xt])
                    norm_stats(xt, b_xt, 4, hn, b_hn, ss, b_ss)

                prepA(0)
                for tt in range(NT):
                    t0 = tt * 512
                    tsl = slice(t0, t0 + 512)
                    cs, b_cs = csr.next()
                    Sx.dma("sp", cs[:, 0, :], c_cos[:, tsl], [], [b_cs])
                    Sx.dma("sp", cs[:, 1, :], c_sin[:, tsl], [], [b_cs])
                    norm_tr(4, gmix[:, l, :], b_gmix, hn, b_hn, hT, b_hT, rotM)

                    def fm_chunk(lhs_fn, wbufs):
                        pt, pb = rotA.next()
                        for k in range(8):
                            Sx.op("pe", lambda e, k=k: e.matmul(pt[:, 0:512], lhsT=lhs_fn(k), rhs=hT[:, k, :], start=(k == 0), stop=(k == 7)),
                                  [b_hT] + wbufs, [pb])
                        return pt, pb

                    for g in range(4):
                        pt, pb = fm_chunk(lambda k, g=g: wA[:, k, g * 128:(g + 1) * 128], b_w)
                        Sx.op("act", lambda e, g=g, pt=pt: e.activation(out=ubuf[:, g, 16:528], in_=pt[:, 0:512], func=AF.Copy), [pb], [b_u[g]])
                    rp = None
                    for c in range(8):
                        pt, pb = fm_chunk(lambda k, c=c: wA[:, k, C_Q + c * 128:C_Q + (c + 1) * 128], b_w)
                        if rp is not None:
                            rp()
                        rp = rope_to(pt, pb, qst[:, c, :], b_qst, cs, b_cs)
                    if tt + 1 < NT:
                        prepA(tt + 1)
                    for i in range(4):
                        pt, pb = fm_chunk(lambda k, i=i: wdup[:, k, i, :], [b_wdup])
                        rp()
                        rp = rope_to(pt, pb, kst[:, i, :], b_kst, cs, b_cs)
                    for i, cb in enumerate((C_KC, C_VC)):
                        pt, pb = fm_chunk(lambda k, cb=cb: wA[:, k, cb:cb + 128], b_w)
                        if rp is not None:
                            rp()
                            rp = None
                        Sx.op("act", lambda e, i=i, pt=pt: e.activation(out=cst[:, i, :], in_=pt[:, 0:512], func=AF.Copy), [pb], [b_cst])
                    for g in range(4):
                        cur = ubuf[:, g, :]
                        curb = b_u[g]
                        sh = 1
                        v0 = 0
                        for step in range(g + 1):
                            nx, bnx = ptmp.next()
                            Sx.op("pool", lambda e, cur=cur, nx=nx, sh=sh, v0=v0: e.tensor_tensor(out=nx[:, v0 + sh:528], in0=cur[:, v0 + sh:528], in1=cur[:, v0:528 - sh], op=ALU.add),
                                  [curb], [bnx])
                            cur, curb = nx, bnx
                            v0 += sh
                            sh *= 2
                        if tt == 0:
                            Sx.op("pool", lambda e, cur=cur, g=g: e.tensor_tensor(out=cur[:, 16:32], in0=cur[:, 16:32], in1=pcorr[:, g, :], op=ALU.mult),
                                  [curb, b_pc], [curb])
                        dT, bdT = dTr.next()
                        Sx.op("dve", lambda e, cur=cur, g=g, dT=dT: e.scalar_tensor_tensor(out=dT[:], in0=cur[:, 16:528], scalar=1.0 / POOLW[g], in1=ubuf[:, g, 16:528],
                                                                                         op0=ALU.mult, op1=ALU.subtract), [curb, b_u[g]], [bdT])
                        pt, pb = rotA.next()
                        Sx.op("pe", lambda e, g=g, dT=dT, pt=pt: e.matmul(pt[:, 0:512], lhsT=wpl[:, g, :], rhs=dT[:], start=True, stop=True), [bdT, b_wpl], [pb])
                        Sx.op("act", lambda e, g=g, pt=pt: e.activation(out=yst[:, g, :], in_=pt[:, 0:512], func=AF.Copy, scale=pscl[:, l, g:g + 1]), [pb, b_pscl], [b_yst])
                        Sx.op("pool", lambda e, g=g: e.tensor_copy(out=ubuf[:, g, 0:16], in_=ubuf[:, g, 512:528]), [b_u[g]], [b_u[g]])
                    for c in range(16):
                        pt, pb = fm_chunk(lambda k, c=c: wA[:, k, C_GM + c * 128:C_GM + (c + 1) * 128], b_w)
                        Sx.op("act", lambda e, c=c, pt=pt: e.activation(out=gst[:, c, :], in_=pt[:, 0:512], func=AF.Sigmoid), [pb], [b_gst])
                    for j in range(4):
                        pt, pb = rotA.next()
                        for (cb, wdt, o0) in ((C_VS, 128, 0), (C_VW, 128, 128), (C_GN, 48, 256)):
                            for k in range(8):
                                Sx.op("pe", lambda e, k=k, j=j, cb=cb, wdt=wdt, o0=o0, pt=pt: e.matmul(pt[:, o0:o0 + wdt], lhsT=hT[:, k, j * 128:(j + 1) * 128],
                                                                                                     rhs=wA[:, k, cb:cb + wdt], start=(k == 0), stop=(k == 7)),
                                      [b_hT] + b_w, [pb])
                        Sx.op("act", lambda e, j=j, pt=pt: e.activation(
                            out=vst[:, j, :, :].rearrange("p b (g c) -> p b g c", g=2)[:, :, :, 0:64],
                            in_=pt[:, 0:256].rearrange("p (b g d) -> p b g d", b=2, g=2), func=AF.Copy), [pb], [b_vst])
                        Sx.op("act", lambda e, j=j, pt=pt: e.activation(out=gts[:, j, :], in_=pt[:, 256:304], func=AF.Sigmoid), [pb], [b_gts])
                    Sx.dma("sp", qT.rearrange("(c p) s -> p c s", p=128)[:, :, tsl], qst[:], [b_qst], [b_q[tt]])
                    Sx.dma("sp", kk.rearrange("i p s -> p i s")[:, :, tsl], kst[:], [b_kst], [b_kk[tt]])
                    Sx.dma("sp", kcvc.rearrange("i p s -> p i s")[:, :, tsl], cst[:], [b_cst], [b_kcvc[tt]])
                    Sx.dma("sp", gmT.rearrange("(c p) s -> p c s", p=128)[:, :, tsl], gst[:], [b_gst], [b_gm[tt]])
                    Sx.dma("sp", ypT.rearrange("(c p) s -> p c s", p=128)[:, :, tsl], yst[:], [b_yst], [b_yp[tt]])
                    for bb in range(2):
                        Sx.dma("sp", vtok[bb].rearrange("(n p) c -> p n c", p=128)[:, tt * 4:(tt + 1) * 4], vst[:, :, bb, :], [b_vst], [b_vtok[tt]])
                    Sx.dma("sp", gatesd.rearrange("(n p) c -> p n c", p=128)[:, tt * 4:(tt + 1) * 4], gts[:], [b_gts], [b_gates[tt]])
                Sx.barrier()

        def phase_BC(l):
            with ExitStack() as es:
                T = mkT(es)
                kcm = T([128, 2, 2, NCP], BF16, "kcm"); b_kcm = Buf()
                vcm = T([128, NCH, 2, 129], BF16, "vcm"); b_vcm = Buf()
                ovl = T([128, NCH, 64], BF16, "ovl"); b_ovl = Buf()
                Sx.dma("sp", ovl[:], c_ovl, [], [b_ovl])
                Sx.op("pool", lambda e: e.memset(kcm[:], 0.0), [], [b_kcm])
                Sx.op("pool", lambda e: e.memset(vcm[:], 0.0), [], [b_vcm])
                with ExitStack() as esb:
                    Tb = mkT(esb)
                    rotB = Rot([0, 1, 2, 3])
                    kv = Tb([128, 2, S], BF16, "kv"); b_kv = Buf()
                    for i in range(2):
                        Sx.dma("sp", kv[:, i, :], kcvc[i], b_kcvc, [b_kv])
                    w1 = Tb([128, 2, 32, 256], BF16, "w1"); b_w1 = Buf()
                    for i, wsrc in enumerate((w_ck1, w_cv1)):
                        for hh in range(2):
                            for lq in range(4):
                                Sx.dma("pool", w1[hh * 64:(hh + 1) * 64, i, lq * 8:(lq + 1) * 8, :],
                                       wsrc[l, lq * 512:(lq + 1) * 512, :].rearrange("(l d) h -> d l h", d=64), [], [b_w1])
                    w2k = Tb([128, 2, 128], BF16, "w2k"); b_w2k = Buf()
                    w2v = Tb([128, 2, 64], BF16, "w2v"); b_w2v = Buf()
                    for hh in range(2):
                        Sx.dma("pool", w2k[:, :, hh * 64:(hh + 1) * 64], w_ck2[l].rearrange("(c p) d -> p c d", p=128), [], [b_w2k])
                    Sx.dma("pool", w2v[:], w_cv2[l].rearrange("(c p) d -> p c d", p=128), [], [b_w2v])
                    pef = Tb([64, 2, 32], F32, "pef"); b_pef = Buf()
                    peb = Tb([64, 2, 32], BF16, "peb"); b_peb = Buf()
                    Sx.dma("sp", pef[:, 0, :], pe_k[l].rearrange("l d -> d l"), [], [b_pef], allow_slow_non_contiguous=True)
                    Sx.dma("sp", pef[:, 1, :], pe_v[l].rearrange("l d -> d l"), [], [b_pef], allow_slow_non_contiguous=True)
                    Sx.op("act", lambda e: e.activation(out=peb[:], in_=pef[:], func=AF.Copy), [b_pef], [b_peb])
                    ccs = Tb([128, 2, NCP], F32, "ccs"); b_ccs = Buf()
                    Sx.dma("sp", ccs[:, 0, :], c_ccos, [], [b_ccs])
                    Sx.dma("sp", ccs[:, 1, :], c_csin, [], [b_ccs])
                    bias = Tb([128, 2, 2], F32, "bias"); b_bias = Buf()
                    for i in range(2):
                        for hc in range(2):
                            pt, pb = rotB.next()
                            for lq in range(32):
                                Sx.op("pe", lambda e, i=i, hc=hc, lq=lq, pt=pt: e.matmul(pt[:, 0:1], lhsT=w1[0:64, i, lq, hc * 128:(hc + 1) * 128], rhs=peb[:, i, lq:lq + 1],
                                                                                       start=(lq == 0), stop=(lq == 31)), [b_w1, b_peb], [pb])
                            Sx.op("act", lambda e, i=i, hc=hc, pt=pt: e.activation(out=bias[:, i, hc:hc + 1], in_=pt[:, 0:1], func=AF.Copy), [pb], [b_bias])
                    hT2 = Tb([128, 2, 2, 2, NCP], BF16, "hT2"); b_h2 = Buf()
                    xh = Ring(Tb, 2, [128, NCP], F32, "xh")
                    x2 = Ring(Tb, 2, [128, NCP], F32, "x2")
                    sg = Ring(Tb, 2, [128, NCP], F32, "sg")
                    for i in range(2):
                        for hc in range(2):
                            for g in range(2):
                                pt, pb = rotB.next()
                                for lq in range(32):
                                    Sx.op("pe", lambda e, i=i, hc=hc, g=g, lq=lq, pt=pt: e.matmul(
                                        pt[:, 0:NCMP], lhsT=w1[g * 64:(g + 1) * 64, i, lq, hc * 128:(hc + 1) * 128],
                                        rhs=kv[g * 64:(g + 1) * 64, i, lq:lq + 16 * (NCMP - 1) + 1:16], start=(lq == 0), stop=(lq == 31)), [b_w1, b_kv], [pb])
                                a, ba = xh.next()
                                b2, bb2 = x2.next()
                                c2, bc2 = sg.next()
                                N = NCMP
                                Sx.op("act", lambda e, a=a, pt=pt, i=i, hc=hc: e.activation(out=a[:, 0:N], in_=pt[:, 0:N], func=AF.Identity, bias=bias[:, i, hc:hc + 1]), [pb, b_bias], [ba])
                                Sx.op("dve", lambda e, a=a, b2=b2: e.tensor_tensor(out=b2[:, 0:N], in0=a[:, 0:N], in1=a[:, 0:N], op=ALU.mult), [ba], [bb2])
                                Sx.op("dve", lambda e, b2=b2: e.tensor_scalar(out=b2[:, 0:N], in0=b2[:, 0:N], scalar1=0.044715, scalar2=1.0, op0=ALU.mult, op1=ALU.add), [bb2], [bb2])
                                Sx.op("dve", lambda e, a=a, b2=b2: e.tensor_tensor(out=b2[:, 0:N], in0=b2[:, 0:N], in1=a[:, 0:N], op=ALU.mult), [bb2, ba], [bb2])
                                Sx.op("act", lambda e, b2=b2, c2=c2: e.activation(out=c2[:, 0:N], in_=b2[:, 0:N], func=AF.Sigmoid, scale=GELU_C), [bb2], [bc2])
                                Sx.op("dve", lambda e, a=a, c2=c2, i=i, hc=hc, g=g: e.tensor_tensor(out=hT2[:, i, hc, g, 0:N], in0=a[:, 0:N], in1=c2[:, 0:N], op=ALU.mult), [ba, bc2], [b_h2])
                    xbk = Tb([128, NCP], BF16, "xbk"); b_xbk = Buf()
                    tk1 = Tb([128, NCP], F32, "tk1"); b_tk1 = Buf()
                    tk2 = Tb([128, NCP], F32, "tk2"); b_tk2 = Buf()
                    for g in range(2):
                        pt, pb = rotB.next()
                        for hc in range(2):
                            Sx.op("pe", lambda e, g=g, hc=hc, pt=pt: e.matmul(pt[:, 0:NCMP], lhsT=w2k[:, hc, :], rhs=hT2[:, 0, hc, g, 0:NCMP], start=(hc == 0), stop=(hc == 1)),
                                  [b_w2k, b_h2], [pb])
                        Sx.op("act", lambda e, pt=pt: e.activation(out=xbk[:, 0:NCMP], in_=pt[:, 0:NCMP], func=AF.Copy), [pb], [b_xbk])
                        p2, pb2 = rotB.next()
                        Sx.op("pe", lambda e, p2=p2: e.matmul(p2[:, 0:NCMP], lhsT=Rm[:], rhs=xbk[:, 0:NCMP], start=True, stop=True), [b_xbk, b_Rm], [pb2])
                        Sx.op("dve", lambda e, pt=pt: e.tensor_tensor(out=tk1[:, 0:NCMP], in0=pt[:, 0:NCMP], in1=ccs[:, 0, 0:NCMP], op=ALU.mult), [pb, b_ccs], [b_tk1])
                        Sx.op("dve", lambda e, p2=p2: e.tensor_tensor(out=tk2[:, 0:NCMP], in0=p2[:, 0:NCMP], in1=ccs[:, 1, 0:NCMP], op=ALU.mult), [pb2, b_ccs], [b_tk2])
                        for hp in range(2):
                            Sx.op("dve", lambda e, g=g, hp=hp: e.tensor_tensor(out=kcm[hp * 64:(hp + 1) * 64, g, hp, 0:NCMP], in0=tk1[hp * 64:(hp + 1) * 64, 0:NCMP],
                                                                             in1=tk2[hp * 64:(hp + 1) * 64, 0:NCMP], op=ALU.add), [b_tk1, b_tk2], [b_kcm])
                    for nch in range(NCH):
                        nn = min(128, NCMP - nch * 128)
                        for g in range(2):
                            pt, pb = rotB.next()
                            for hc in range(2):
                                Sx.op("pe", lambda e, g=g, hc=hc, nch=nch, nn=nn, pt=pt: e.matmul(pt[0:nn, 0:64], lhsT=hT2[:, 1, hc, g, nch * 128:nch * 128 + nn], rhs=w2v[:, hc, :],
                                                                                                start=(hc == 0), stop=(hc == 1)), [b_w2v, b_h2], [pb])
                            Sx.op("act", lambda e, g=g, nch=nch, nn=nn, pt=pt: e.activation(out=vcm[0:nn, nch, g, 0:64], in_=pt[0:nn, 0:64], func=AF.Copy), [pb], [b_vcm])
                        for g in range(2):
                            Sx.op("pool", lambda e, g=g, nch=nch, nn=nn: e.memset(vcm[0:nn, nch, g, 64:65], 1.0), [], [b_vcm])
                            Sx.op("pool", lambda e, g=g, nch=nch: e.tensor_copy(out=vcm[:, nch, g, 65:129], in_=ovl[:, nch, :]), [b_ovl], [b_vcm])
                    if dbg:
                        dk = Tb([128, 2, NCP], F32, "dk"); b_dk = Buf()
                        dv = Tb([128, NCH, 2, 129], F32, "dv"); b_dv = Buf()
                        Sx.op("dve", lambda e: e.tensor_tensor(out=dk[:], in0=kcm[:, :, 0, :], in1=kcm[:, :, 1, :], op=ALU.add), [b_kcm], [b_dk])
                        Sx.op("dve", lambda e: e.tensor_copy(out=dv[:], in_=vcm[:]), [b_vcm], [b_dv])
                        Sx.dma("sp", dbg_kcm, dk[:], [b_dk], [])
                        Sx.dma("sp", dbg_vcm, dv[:], [b_dv], [])
                    Sx.barrier()
                if stop_after == "B":
                    return
                rotS = Rot([0, 1, 2, 3])
                rotAcc = Rot([4, 5])
                rotI = Rot([6])
                rotX = Rot([7])
                ks2 = T([128, 2, 2, S], BF16, "ks2"); b_ks2 = Buf()
                kw2 = T([128, 2, 2, S], BF16, "kw2"); b_kw2 = Buf()
                for hp in range(2):
                    oh = 1 - hp
                    Sx.op("pool", lambda e: e.memset(ks2[oh * 64:(oh + 1) * 64, :, hp, :], 0.0), [], [b_ks2])
                    Sx.op("pool", lambda e: e.memset(kw2[oh * 64:(oh + 1) * 64, :, hp, :], 0.0), [], [b_kw2])
                for g in range(2):
                    for hp in range(2):
                        Sx.dma("sp", ks2[hp * 64:(hp + 1) * 64, g, hp, :], kk[g, hp * 64:(hp + 1) * 64, :], b_kk, [b_ks2])
                        Sx.dma("sp", kw2[hp * 64:(hp + 1) * 64, g, hp, :], kk[2 + g, hp * 64:(hp + 1) * 64, :], b_kk, [b_kw2])
                vs1 = T([128, NQ, 2, 65], BF16, "vs1"); b_vs1 = Buf()
                vw1 = T([128, NQ, 2, 65], BF16, "vw1"); b_vw1 = Buf()
                for n0 in range(0, NQ, 8):
                    Sx.dma("sp", vs1[:, n0:n0 + 8].rearrange("p n g c -> p n (g c)"), vtok[0].rearrange("(n p) c -> p n c", p=128)[:, n0:n0 + 8], b_vtok, [b_vs1])
                    Sx.dma("sp", vw1[:, n0:n0 + 8].rearrange("p n g c -> p n (g c)"), vtok[1].rearrange("(n p) c -> p n c", p=128)[:, n0:n0 + 8], b_vtok, [b_vw1])
                gat = T([128, NQ, 48], F32, "gat"); b_gat = Buf()
                Sx.dma("sp", gat[:], gatesd.rearrange("(n p) c -> p n c", p=128), b_gates, [b_gat])
                Esb = T([128, S], BF16, "Esb"); b_E = Buf()
                Sx.op("pool", lambda e: e.memset(Esb[64:128, :], 0.0), [], [b_E])
                Sx.dma("sp", Esb[0:64, :], c_E, [], [b_E])
                qbr = Ring(T, 2, [128, 8, 512], BF16, "qb")
                ynst = T([128, 8, 512], BF16, "ynst"); b_ynst = Buf()
                ptr = Ring(T, 4, [128, 512], BF16, "pt")
                obf = T([128, D], BF16, "obf"); b_obf = Buf()
                tmp4r = Ring(T, 2, [128, 4, 64], F32, "tmp4")
                tmpIr = Ring(T, 2, [128, 4, 64], F32, "tmpI")
                rinv = Ring(T, 2, [128, 8], F32, "rinv")
                impp = T([128, 4, 64], F32, "impp"); b_impp = Buf()
                imp = T([128, 2, 64], F32, "imp"); b_imp = Buf()
                score = T([128, 2, 64], F32, "score"); b_score = Buf()
                m8 = T([128, 2, 8], F32, "m8"); b_m8 = Buf()
                s1 = T([128, 2, 64], F32, "s1"); b_s1 = Buf()
                s2 = T([128, 2, 64], F32, "s2"); b_s2 = Buf()
                selq = T([128, 2, 64], BF16, "selq"); b_selq = Buf()
                self32 = T([128, 2, 64], F32, "self32"); b_self32 = Buf()
                abr = Ring(T, 2, [128, 4, 64], F32, "ab")
                cmr = Ring(T, 2, [128, NCH, 128], BF16, "cm")

                bank_owner = {}
                side = collections.deque()

                def side_push(fns, grp=None):
                    for fn in fns:
                        side.append((fn, grp))
                        if grp is not None:
                            grp["pend"] = grp.get("pend", 0) + 1

                def side_pop(k):
                    for _ in range(k):
                        if not side:
                            return
                        fn, grp = side.popleft()
                        fn()
                        if grp is not None:
                            grp["pend"] -= 1

                def side_drain_group(grp):
                    while grp is not None and grp.get("pend", 0) > 0:
                        side_pop(1)

                def side_drain_all():
                    while side:
                        side_pop(1)

                def evac_ops(grp):
                    tl = grp["tile"]
                    qi, g, p, b = tl["qi"], grp["g"], grp["p"], grp["b"]
                    acc, bacc = grp["acc"]
                    osb, b_osb = tl["osb"]
                    av = acc[:, 0:260].rearrange("p (c e) -> p c e", e=65)
                    h0 = (8 * g + p) * 3 + b
                    ov = osb[:].rearrange("p (h d) -> p h d", d=64)[:, 8 * g + p:8 * g + p + 7:2, :]
                    R = {}

                    def o1():
                        R["rv"], R["brv"] = rinv.next()
                        Sx.op("dve", lambda e: e.tensor_scalar(out=R["rv"][:, 0:4], in0=av[:, :, 64], scalar1=1e-30, scalar2=None, op0=ALU.max), [bacc], [R["brv"]])

                    def o2():
                        Sx.op("dve", lambda e: e.reciprocal(out=R["rv"][:, 0:4], in_=R["rv"][:, 0:4]), [R["brv"]], [R["brv"]])

                    def o3():
                        Sx.op("dve", lambda e: e.tensor_tensor(out=R["rv"][:, 4:8], in0=R["rv"][:, 0:4], in1=gat[:, qi, h0:h0 + 19:6], op=ALU.mult), [R["brv"], b_gat], [R["brv"]])
                    ops = [o1, o2, o3]
                    if b == 0:
                        accI, baccI = grp["accI"]

                        def o4():
                            Sx.op("dve", lambda e: e.tensor_tensor(out=ov, in0=av[:, :, 0:64], in1=R["rv"][:, 4:8].unsqueeze(2).to_broadcast([128, 4, 64]), op=ALU.mult),
                                  [bacc, R["brv"]], [b_osb])

                        def o5():
                            R["tI"], R["btI"] = tmpIr.next()
                            Sx.op("dve", lambda e: e.tensor_tensor(out=R["tI"][:], in0=accI[:, 0:256].rearrange("p (c j) -> p c j", j=64),
                                                                   in1=R["rv"][:, 0:4].unsqueeze(2).to_broadcast([128, 4, 64]), op=ALU.mult), [baccI, R["brv"]], [R["btI"]])

                        def o6():
                            Sx.op("dve", lambda e: e.tensor_reduce(out=impp[:, 2 * g + p, :], in_=R["tI"][:].rearrange("p c j -> p j c"), axis=AX.X, op=ALU.add),
                                  [R["btI"]], [b_impp])
                        ops += [o4, o5, o6]
                        if g == 1 and p == 1:
                            ops += topk_ops(tl)
                    else:
                        def o4():
                            R["t4"], R["bt4"] = tmp4r.next()
                            Sx.op("dve", lambda e: e.tensor_tensor(out=R["t4"][:], in0=av[:, :, 0:64], in1=R["rv"][:, 4:8].unsqueeze(2).to_broadcast([128, 4, 64]), op=ALU.mult),
                                  [bacc, R["brv"]], [R["bt4"]])

                        def o5():
                            Sx.op("pool", lambda e: e.tensor_tensor(out=ov, in0=ov, in1=R["t4"][:], op=ALU.add), [R["bt4"], b_osb], [b_osb])
                        ops += [o4, o5]
                    return ops

                def topk_ops(tl):
                    qi = tl["qi"]
                    ops = []

                    def mk(eng, fn, rd, wr):
                        ops.append(lambda: Sx.op(eng, fn, rd, wr))
                    ab_ = lambda: tl["ab"]
                    ops.append(lambda: Sx.op("pool", lambda e: e.tensor_tensor(out=imp[:], in0=impp[:, 0:4:2, :], in1=impp[:, 1:4:2, :], op=ALU.add), [b_impp], [b_imp]))
                    ops.append(lambda: Sx.op("pool", lambda e: e.tensor_tensor(out=score[:], in0=imp[:], in1=ab_()[0][:, 0:2, :], op=ALU.mult), [b_imp, ab_()[1]], [b_score]))
                    ops.append(lambda: Sx.op("pool", lambda e: e.tensor_tensor(out=score[:], in0=score[:], in1=ab_()[0][:, 2:4, :], op=ALU.add), [b_score, ab_()[1]], [b_score]))
                    for g in range(2):
                        mk("dve", lambda e, g=g: e.max(out=m8[:, g, :], in_=score[:, g, :]), [b_score], [b_m8])
                        mk("dve", lambda e, g=g: e.match_replace(out=s1[:, g, :], in_to_replace=m8[:, g, :], in_values=score[:, g, :], imm_value=-1e9), [b_score, b_m8], [b_s1])
                        mk("dve", lambda e, g=g: e.max(out=m8[:, g, :], in_=s1[:, g, :]), [b_s1], [b_m8])
                        mk("dve", lambda e, g=g: e.match_replace(out=s2[:, g, :], in_to_replace=m8[:, g, :], in_values=s1[:, g, :], imm_value=-1e9), [b_s1, b_m8], [b_s2])
                    mk("pool", lambda e: e.tensor_single_scalar(out=s2[:], in_=s2[:], scalar=-1e8, op=ALU.is_lt), [b_s2], [b_s2])
                    mk("dve", lambda e: e.scalar_tensor_tensor(out=self32[:], in0=score[:], scalar=-0.5, in1=s2[:], op0=ALU.is_gt, op1=ALU.mult),
                       [b_score, b_s2], [b_self32])
                    mk("pool", lambda e: e.tensor_scalar(out=selq[:], in0=self32[:], scalar1=30000.0, scalar2=-30000.0, op0=ALU.mult, op1=ALU.add), [b_self32], [b_selq])
                    if dbg:
                        ops.append(lambda: Sx.dma("sp", dbg_imp[qi * 128:(qi + 1) * 128], imp[:], [b_imp], []))
                        ops.append(lambda: Sx.dma("sp", dbg_sel[qi * 128:(qi + 1) * 128], self32[:], [b_self32], []))
                    ops += topk_T_ops(tl)
                    return ops

                def topk_T_ops(tl):
                    R = {}

                    def pe_part():
                        R["sT"], R["bsT"] = selTr.next()
                        R["px"], R["pbx"] = rotX.next()
                        pxv = R["px"][:].bitcast(BF16)
                        for g in range(2):
                            Sx.op("pe", lambda e, g=g: e.transpose(out=pxv[0:64, g * 128:(g + 1) * 128], in_=selq[:, g, :], identity=ident[:]), [b_selq, b_ident], [R["pbx"]])

                    def act_part():
                        pxv = R["px"][:].bitcast(BF16)
                        Sx.op("act", lambda e: e.activation(out=R["sT"][0:64], in_=pxv[0:64, 0:256].rearrange("p (g q) -> p g q", g=2).unsqueeze(2).to_broadcast([64, 2, 4, 128]), func=AF.Copy),
                              [R["pbx"]], [R["bsT"]])
                        tl["selT"] = (R["sT"], R["bsT"])
                    nop = lambda: None
                    return [pe_part, nop, nop, nop, act_part]

                def build_selmask(tl, k0):
                    qi = tl["qi"]
                    if "selT" not in tl:
                        side_drain_all()
                    sT, bsT = tl["selT"]
                    nk = min(2, qi + 1 - k0)
                    px, pbx = rotI.next()
                    side_drain_group(bank_owner.get(id(pbx)))
                    bsm = b_smp[k0 // 2]
                    for kk_ in range(nk):
                        kc = k0 + kk_
                        Sx.op("pe", lambda e, kc=kc, kk_=kk_: e.matmul(px[:, kk_ * 256:(kk_ + 1) * 256], lhsT=Esb[:, kc * 128:(kc + 1) * 128],
                                                                      rhs=sT[:].rearrange("p g q -> p (g q)"), start=True, stop=True), [b_E, bsT], [pbx])

                    def act_part():
                        Sx.op("act", lambda e: e.activation(out=selmask[:, k0:k0 + nk].rearrange("p k g q -> p (k g q)"), in_=px[:, 0:nk * 256], func=AF.Copy),
                              [pbx], [bsm])
                        if k0 <= qi < k0 + nk:
                            Sx.op("pool", lambda e: e.tensor_tensor(out=selmask[:, qi], in0=selmask[:, qi], in1=mdiag[:].unsqueeze(1).to_broadcast([128, 2, 128]), op=ALU.mult),
                                  [bsm, b_md], [bsm])
                    return act_part

                def tile_pre(tl):
                    qi = tl["qi"]
                    if qi % 4 == 0:
                        qcur[0] = qbr.next()
                        qb, bq = qcur[0]
                        Sx.dma("sp", qb[:], qT.rearrange("(c p) s -> p c s", p=128)[:, :, (qi // 4) * 512:(qi // 4 + 1) * 512], [b_q[qi // 4]], [bq])
                    tl["qb"] = qcur[0]
                    tl["ab"] = abr.next()
                    tl["cm"] = cmr.next()
                    tl["osb"] = osbr.next()
                    Sx.dma("sp", tl["ab"][0][:], c_tkAB[qi], [], [tl["ab"][1]])
                    Sx.dma("sp", tl["cm"][0][:], c_cmask[qi], [], [tl["cm"][1]])

                def tile_tail_a(tl):
                    osb, b_osb = tl["osb"]
                    Sx.op("act", lambda e: e.activation(out=obf[:], in_=osb[:], func=AF.Copy), [b_osb], [b_obf])

                def tile_tail_b_ops(tl):
                    qi = tl["qi"]
                    q0 = (qi % 4) * 128
                    ops = []
                    nop = lambda: None
                    for c2 in range(2):
                        R = {}

                        def pe_part(c2=c2, R=R):
                            R["px"], R["pbx"] = rotX.next()
                            pxv = R["px"][:].bitcast(BF16)
                            for c in range(4):
                                cc = c2 * 4 + c
                                Sx.op("pe", lambda e, c=c, cc=cc: e.transpose(out=pxv[:, c * 128:(c + 1) * 128], in_=obf[:, cc * 128:(cc + 1) * 128], identity=ident[:]),
                                      [b_obf, b_ident], [R["pbx"]])

                        def act_part(c2=c2, R=R):
                            pxv = R["px"][:].bitcast(BF16)
                            Sx.op("act", lambda e: e.activation(out=ynst[:, c2 * 4:(c2 + 1) * 4, q0:q0 + 128], in_=pxv[:, 0:512].rearrange("p (c q) -> p c q", c=4), func=AF.Copy),
                                  [R["pbx"]], [b_ynst])
                            if c2 == 1 and qi % 4 == 3:
                                Sx.dma("sp", ynT.rearrange("(c p) s -> p c s", p=128)[:, :, (qi // 4) * 512:(qi // 4 + 1) * 512], ynst[:], [b_ynst], [b_yn[qi // 4]])
                        ops += [pe_part, nop, nop, nop, act_part]
                    return ops

                def emit_S(st):
                    grp = st["grp"]
                    tl = grp["tile"]
                    if st["tile_first"]:
                        tile_pre(tl)
                    b, g, p, kc = grp["b"], grp["g"], grp["p"], st["kc"]
                    qb, bq = tl["qb"]
                    q0 = (tl["qi"] % 4) * 128
                    rhs_q = qb[:, 4 * g:4 * g + 4, q0:q0 + 128]
                    ps, pbs = rotS.next()
                    if b == 0:
                        lhs, blhs = kcm[:, g, p, kc * 128:(kc + 1) * 128], b_kcm
                    elif b == 1:
                        lhs, blhs = ks2[:, g, p, kc * 128:(kc + 1) * 128], b_ks2
                    else:
                        lhs, blhs = kw2[:, g, p, kc * 128:(kc + 1) * 128], b_kw2
                    if b != 1:
                        Sx.op("pe", lambda e: e.matmul(ps[:, 0:512], lhsT=lhs, rhs=rhs_q, start=True, stop=True), [blhs, bq], [pbs])
                    else:
                        if "selT" not in tl:
                            side_drain_all()
                        nT, bnT = tl["selT"]
                        qi = tl["qi"]
                        Sx.op("pe", lambda e: e.matmul(ps[:, 0:512], lhsT=lhs, rhs=rhs_q, start=True, stop=False), [blhs, bq], [pbs])
                        Sx.op("pe", lambda e: e.matmul(ps[:, 0:512], lhsT=Esb[:, kc * 128:(kc + 1) * 128], rhs=nT[:, g, :, :], start=False, stop=(kc != qi)), [b_E, bnT], [pbs])
                        if kc == qi:
                            Sx.op("pe", lambda e: e.matmul(ps[:, 0:512], lhsT=ident[:], rhs=negdiag4[:], start=False, stop=True), [b_ident, b_nd4], [pbs])
                    st["ps"], st["pbs"] = ps, pbs

                def emit_mid(st):
                    grp = st["grp"]
                    tl = grp["tile"]
                    qi = tl["qi"]
                    b, g, kc = grp["b"], grp["g"], st["kc"]
                    ps, pbs = st["ps"], st["pbs"]
                    pt, bpt = ptr.next()
                    Sx.op("act", lambda e: e.activation(out=pt[:], in_=ps[:, 0:512], func=AF.Exp, scale=0.125), [pbs], [bpt])
                    ptv = pt[:].rearrange("p (c q) -> p c q", c=4)
                    mk = None
                    if b == 0:
                        mk, bmk = tl["cm"][0][:, kc, :], tl["cm"][1]
                    elif b == 1:
                        mk = None
                    elif kc == qi:
                        mk, bmk = mdiag[:], b_md
                    elif kc == qi - 4:
                        mk, bmk = mfar[:], b_mf
                    if mk is not None:
                        Sx.op("dve", lambda e: e.tensor_tensor(out=ptv, in0=ptv, in1=mk.unsqueeze(1).to_broadcast([128, 4, 128]), op=ALU.mult), [bpt, bmk], [bpt])
                    st["pt"], st["bpt"] = pt, bpt

                def emit_PV(st):
                    grp = st["grp"]
                    b, g, kc = grp["b"], grp["g"], st["kc"]
                    pt, bpt = st["pt"], st["bpt"]
                    if st["first"]:
                        grp["acc"] = rotAcc.next()
                        side_drain_group(bank_owner.get(id(grp["acc"][1])))
                        bank_owner[id(grp["acc"][1])] = grp
                        if b == 0:
                            grp["accI"] = rotI.next()
                            side_drain_group(bank_owner.get(id(grp["accI"][1])))
                            bank_owner[id(grp["accI"][1])] = grp
                    acc, bacc = grp["acc"]
                    for c in range(4):
                        if b == 0:
                            accI, baccI = grp["accI"]
                            Sx.op("pe", lambda e, c=c: e.matmul(acc[:, c * 65:(c + 1) * 65], lhsT=pt[:, c * 128:(c + 1) * 128], rhs=vcm[:, kc, g, 0:65],
                                                                start=(st["first"] and c == 0), stop=False, skip_group_check=True), [bpt, b_vcm], [bacc])
                            Sx.op("pe", lambda e, c=c: e.matmul(accI[:, c * 64:(c + 1) * 64], lhsT=pt[:, c * 128:(c + 1) * 128], rhs=vcm[:, kc, g, 65:129],
                                                                start=(st["first"] and c == 0), stop=False, skip_group_check=True), [bpt, b_vcm], [baccI])
                        else:
                            vT, bvT = (vs1, b_vs1) if b == 1 else (vw1, b_vw1)
                            Sx.op("pe", lambda e, c=c: e.matmul(acc[:, c * 65:(c + 1) * 65], lhsT=pt[:, c * 128:(c + 1) * 128], rhs=vT[:, kc, g, :],
                                                                start=(st["first"] and c == 0), stop=False, skip_group_check=True), [bpt, bvT], [bacc])

                LA = 3
                EV_DELAY = 2
                qcur = [None]
                osbr = Ring(T, 3, [128, D], F32, "osb")
                selTr = Ring(T, 2, [128, 2, 4, 128], BF16, "negT4")
                for sT_ in selTr.t:
                    Sx.op("pool", lambda e, sT_=sT_: e.memset(sT_[:], 0.0), [], [selTr.b[selTr.t.index(sT_)]])
                b_smp = [Buf() for _ in range((NQ + 1) // 2)]
                steps = []
                tiles = [{"qi": qi} for qi in range(NQ)]

                def add_group(tl, b, g, p, first_of_tile=False, tile_last=False):
                    qi = tl["qi"]
                    if b == 0:
                        kcs = list(range(NCH))
                    elif b == 1:
                        kcs = list(range(qi + 1))
                    else:
                        kcs = list(range(max(0, qi - 4), qi + 1))
                    grp = {"b": b, "g": g, "p": p, "acc": None, "tile": tl, "tile_last": tile_last}
                    for ii, kc in enumerate(kcs):
                        steps.append({"grp": grp, "kc": kc, "first": ii == 0, "last": ii == len(kcs) - 1, "tile_first": first_of_tile and ii == 0})

                def add_sel(tl):
                    for g in range(2):
                        for p in range(2):
                            add_group(tl, 1, g, p, tile_last=(g == 1 and p == 1))

                for qi in range(NQ):
                    tl = tiles[qi]
                    for g in range(2):
                        for p in range(2):
                            add_group(tl, 0, g, p, first_of_tile=(g == 0 and p == 0))
                            add_group(tl, 2, g, p)
                    if qi >= 1:
                        add_sel(tiles[qi - 1])
                add_sel(tiles[NQ - 1])
                n = len(steps)
                SIDE_RATE = 2

                for i in range(min(LA, n)):
                    emit_S(steps[i])
                for i in range(n):
                    st = steps[i]
                    for st_ in steps[i:i + 2]:
                        f_ = st_.pop("sm_act", None)
                        if f_ is not None:
                            f_()
                    emit_mid(st)
                    side_pop(SIDE_RATE)
                    emit_PV(st)
                    if st["last"]:
                        grp = st["grp"]
                        side_push(evac_ops(grp), grp)
                        if grp["tile_last"]:
                            side_push([lambda tl=grp["tile"]: tile_tail_a(tl), lambda: None, lambda: None, lambda: None] + tile_tail_b_ops(grp["tile"]), None)
                    if i + LA < n:
                        emit_S(steps[i + LA])
                side_drain_all()
                Sx.barrier()

        def phase_D(l, xsrc, after_weights):
            with ExitStack() as es:
                T = mkT(es)
                rotP = Rot([0, 1, 2, 3])
                rotO = Rot([4, 5, 6, 7])
                wpp = T([128, 4, D], BF16, "wpp"); b_wpp = Buf()
                wpn = T([128, 8, D], BF16, "wpn"); b_wpn = Buf()
                wo = T([128, 8, D], BF16, "wo"); b_wo = Buf()
                for k in range(4):
                    Sx.dma("pool", wpp[:, k, :], w_pp[l, k * 128:(k + 1) * 128, :], [], [b_wpp])
                for k in range(8):
                    Sx.dma("pool", wpn[:, k, :], w_pn[l, k * 128:(k + 1) * 128, :], [], [b_wpn])
                    Sx.dma("pool", wo[:, k, :], w_out[l, k * 128:(k + 1) * 128, :], [], [b_wo])
                after_weights()
                ypt = T([128, 4, 512], BF16, "ypt"); b_ypt = Buf()
                ynt = T([128, 8, 512], BF16, "ynt"); b_ynt = Buf()
                gmt = T([128, 16, 512], BF16, "gmt"); b_gmt = Buf()
                xtr = Ring(T, 2, [128, 4, D], F32, "xtD")
                mg = T([128, 8, 512], BF16, "mg"); b_mg = Buf()
                t1r = Ring(T, 2, [128, 512], F32, "t1D")
                t2r = Ring(T, 2, [128, 512], F32, "t2D")

                def loadD(tt, which):
                    tsl_ = slice(tt * 512, tt * 512 + 512)
                    if which == 0:
                        Sx.dma("sp", ypt[:], ypT.rearrange("(c p) s -> p c s", p=128)[:, :, tsl_], [b_yp[tt]], [b_ypt])
                        Sx.dma("sp", ynt[:], ynT.rearrange("(c p) s -> p c s", p=128)[:, :, tsl_], [b_yn[tt]], [b_ynt])
                    elif which == 1:
                        Sx.dma("sp", gmt[:], gmT.rearrange("(c p) s -> p c s", p=128)[:, :, tsl_], [b_gm[tt]], [b_gmt])
                    else:
                        xt_, bxt_ = xtr.next()
                        Sx.dma("sp", xt_[:], xsrc[tsl_].rearrange("(j p) f -> p j f", p=128), xres_bufs(tt * 512, 512) if xsrc is xres else [], [bxt_])
                        return xt_, bxt_

                loadD(0, 0)
                loadD(0, 1)
                cur = loadD(0, 2)
                for tt in range(NT):
                    t0 = tt * 512
                    tsl = slice(t0, t0 + 512)
                    xt, b_xt = cur
                    for oc in range(8):
                        pa, pba = rotP.next()
                        for k in range(4):
                            Sx.op("pe", lambda e, k=k: e.matmul(pa[:, 0:512], lhsT=wpp[:, k, oc * 128:(oc + 1) * 128], rhs=ypt[:, k, :], start=(k == 0), stop=(k == 3)),
                                  [b_wpp, b_ypt], [pba])
                        pn, pbn = rotP.next()
                        for k in range(8):
                            Sx.op("pe", lambda e, k=k: e.matmul(pn[:, 0:512], lhsT=wpn[:, k, oc * 128:(oc + 1) * 128], rhs=ynt[:, k, :], start=(k == 0), stop=(k == 7)),
                                  [b_wpn, b_ynt], [pbn])
                        t1, bt1 = t1r.next()
                        t2, bt2 = t2r.next()
                        Sx.op("dve", lambda e: e.tensor_tensor(out=t1[:], in0=pa[:, 0:512], in1=gmt[:, oc, :], op=ALU.mult), [pba, b_gmt], [bt1])
                        Sx.op("dve", lambda e: e.tensor_tensor(out=t2[:], in0=pn[:, 0:512], in1=gmt[:, 8 + oc, :], op=ALU.mult), [pbn, b_gmt], [bt2])
                        Sx.op("dve", lambda e: e.tensor_tensor(out=mg[:, oc, :], in0=t1[:], in1=t2[:], op=ALU.add), [bt1, bt2], [b_mg])
                    if tt + 1 < NT:
                        loadD(tt + 1, 0)
                        loadD(tt + 1, 1)
                        cur = loadD(tt + 1, 2)
                    for j in range(4):
                        for hf in range(2):
                            po, pbo = rotO.next()
                            for k in range(8):
                                Sx.op("pe", lambda e, k=k: e.matmul(po[:, 0:512], lhsT=mg[:, k, j * 128:(j + 1) * 128], rhs=wo[:, k, hf * 512:(hf + 1) * 512], start=(k == 0), stop=(k == 7)),
                                      [b_mg, b_wo], [pbo])
                            Sx.op("dve", lambda e: e.tensor_tensor(out=xt[:, j, hf * 512:(hf + 1) * 512], in0=po[:, 0:512], in1=xt[:, j, hf * 512:(hf + 1) * 512], op=ALU.add),
                                  [pbo, b_xt], [b_xt])
                    Sx.dma("sp", xres[tsl].rearrange("(j p) f -> p j f", p=128), xt[:], [b_xt], xres_bufs(t0, 512))
                Sx.barrier()

        def phase_E(l, last, wf1, b_wf1):
            with ExitStack() as es:
                T = mkT(es)
                rotF = Rot([0, 1, 2, 3])
                rotO = Rot([4, 5, 6, 7])
                wf2 = T([128, 32, D], BF16, "wf2"); b_wf2 = Buf()
                for k8 in range(4):
                    Sx.dma("pool", wf2[:, k8 * 8:(k8 + 1) * 8, :], w_ff2[l, k8 * 1024:(k8 + 1) * 1024, :].rearrange("(k p) f -> p k f", p=128), [], [b_wf2])
                xtr = Ring(T, 2, [128, 2, D], F32, "xtE")
                hn = T([128, 2, D], BF16, "hnE"); b_hn = Buf()
                hT = T([128, 8, 256], BF16, "hTE"); b_hT = Buf()
                ss = T([128, 8], F32, "ssE"); b_ss = Buf()
                actT = T([128, 32, 256], BF16, "actT"); b_act = Buf()
                rlr = Ring(T, 3, [128, 256], F32, "rl")
                if last:
                    nf = T([128, D], F32, "nf"); b_nf = Buf()
                    Sx.dma("sp", nf[:], norm_final.partition_broadcast(128), [], [b_nf])
                    junk = T([128, D], BF16, "junk"); b_junk = Buf()
                    s2 = T([128, 8], F32, "ssF"); b_s2 = Buf()

                def prepE(tt):
                    xt, b_xt = xtr.next()
                    Sx.dma("sp", xt[:], xres[tt * 256:(tt + 1) * 256].rearrange("(j p) f -> p j f", p=128), xres_bufs(tt * 256, 256), [b_xt])
                    norm_stats(xt, b_xt, 2, hn, b_hn, ss, b_ss)
                    return xt, b_xt

                cur = prepE(0)
                for tt in range(NT2):
                    t0 = tt * 256
                    tsl = slice(t0, t0 + 256)
                    xt, b_xt = cur
                    if tt == 0:
                        norm_tr(2, gmlp[:, l, :], b_gmlp, hn, b_hn, hT, b_hT, rotF)
                    for fc in range(32):
                        pt, pb = rotF.next()
                        for k in range(8):
                            Sx.op("pe", lambda e, k=k: e.matmul(pt[:, 0:256], lhsT=wf1[:, k, fc * 128:(fc + 1) * 128], rhs=hT[:, k, :], start=(k == 0), stop=(k == 7)),
                                  [b_wf1, b_hT], [pb])
                        rl, brl = rlr.next()
                        Sx.op("act", lambda e: e.activation(out=rl[:], in_=pt[:, 0:256], func=AF.Relu), [pb], [brl])
                        Sx.op("dve", lambda e: e.tensor_tensor(out=actT[:, fc, :], in0=rl[:], in1=rl[:], op=ALU.mult), [brl], [b_act])
                        if fc == 6 and tt + 1 < NT2:
                            cur = prepE(tt + 1)
                    if tt + 1 < NT2:
                        norm_tr(2, gmlp[:, l, :], b_gmlp, hn, b_hn, hT, b_hT, rotF)
                    for j in range(2):
                        for hf in range(2):
                            po, pbo = rotO.next()
                            for k in range(32):
                                Sx.op("pe", lambda e, k=k: e.matmul(po[:, 0:512], lhsT=actT[:, k, j * 128:(j + 1) * 128], rhs=wf2[:, k, hf * 512:(hf + 1) * 512], start=(k == 0), stop=(k == 31)),
                                      [b_act, b_wf2], [pbo])
                            Sx.op("dve", lambda e: e.tensor_tensor(out=xt[:, j, hf * 512:(hf + 1) * 512], in0=po[:, 0:512], in1=xt[:, j, hf * 512:(hf + 1) * 512], op=ALU.add),
                                  [pbo, b_xt], [b_xt])
                    if not last:
                        Sx.dma("sp", xres[tsl].rearrange("(j p) f -> p j f", p=128), xt[:], [b_xt], xres_bufs(t0, 256))
                    else:
                        for j in range(2):
                            Sx.op("act", lambda e, j=j: e.activation(out=junk[:], in_=xt[:, j, :], func=AF.Square, accum_out=s2[:, j:j + 1]), [b_xt], [b_junk, b_s2])
                        Sx.op("act", lambda e: e.activation(out=s2[:, 4:6], in_=s2[:, 0:2], func=AF.Sqrt, scale=1.0 / D, bias=epst[:, 0:1]), [b_s2, b_eps], [b_s2])
                        Sx.op("dve", lambda e: e.reciprocal(out=s2[:, 4:6], in_=s2[:, 4:6]), [b_s2], [b_s2])
                        for j in range(2):
                            Sx.op("dve", lambda e, j=j: e.tensor_scalar(out=xt[:, j, :], in0=xt[:, j, :], scalar1=s2[:, 4 + j:5 + j], scalar2=None, op0=ALU.mult), [b_xt, b_s2], [b_xt])
                            Sx.op("pool", lambda e, j=j: e.tensor_tensor(out=xt[:, j, :], in0=xt[:, j, :], in1=nf[:], op=ALU.mult), [b_xt, b_nf], [b_xt])
                        Sx.dma("sp", out[tsl].rearrange("(j p) f -> p j f", p=128), xt[:], [b_xt], [])
                Sx.barrier()

        for l in range(depth):
            xsrc = x_in if l == 0 else xres
            phase_A(l, xsrc)
            if stop_after == "A":
                break
            Sx.new_epoch()
            phase_BC(l)
            if stop_after in ("B", "C"):
                break
            Sx.new_epoch()
            with ExitStack() as esw:
                wf1 = mkT(esw)([128, 8, 4096], BF16, "wf1"); b_wf1 = Buf()

                def load_wf1():
                    for k2 in range(4):
                        Sx.dma("pool", wf1[:, k2 * 2:(k2 + 1) * 2, :], w_ff1[l, k2 * 256:(k2 + 1) * 256, :].rearrange("(k p) f -> p k f", p=128), [], [b_wf1])
                phase_D(l, xsrc, load_wf1)
                if stop_after == "D":
                    break
                phase_E(l, l == depth - 1, wf1, b_wf1)
            if l < depth - 1:
                Sx.new_epoch()
        Sx.barrier()
    return nc


WKEYS = ["norm_mix", "w_in", "w_pool", "pool_scale", "pe_k", "pe_v", "w_ck1", "w_ck2", "w_cv1", "w_cv2",
         "w_proj_pool", "w_proj_nsa", "w_out", "norm_mlp", "w_ff1", "w_ff2", "norm_final"]


def kernel(**inputs):
    S, depth, B = 4096, 4, 8
    x = np.asarray(inputs["x"], dtype=np.float32)
    wts = {k: np.ascontiguousarray(np.asarray(inputs[k], dtype=np.float32)) for k in WKEYS}
    consts = host_consts(S)
    nc = build(S, depth)
    in_maps = []
    for b in range(B):
        m = dict(wts)
        m.update(consts)
        m["x"] = np.ascontiguousarray(x[b])
        in_maps.append(m)
    res = run_bass_kernel_spmd(nc, in_maps, core_ids=list(range(B)))
    return np.stack([np.asarray(r["out"], dtype=np.float32) for r in res.results], axis=0)
```

```python
import numpy as np
import collections
from contextlib import ExitStack
import ml_dtypes
import concourse.bass as bass
import concourse.mybir as mybir
from concourse.bass_utils import run_bass_kernel_spmd

F32 = mybir.dt.float32
BF16 = mybir.dt.bfloat16
ALU = mybir.AluOpType
AF = mybir.ActivationFunctionType
AX = mybir.AxisListType

NDS = 20
LIMIT = None
NDS_SW = 6


class Buf:
    __slots__ = ("w", "r", "excl")

    def __init__(self, excl=False):
        self.w = None
        self.r = {}
        self.excl = excl


class Sched:
    def __init__(self, nc, es, same_engine_sync=True):
        self.nc = nc
        self.es = es
        self.eng = {"pe": nc.tensor, "act": nc.scalar, "dve": nc.vector, "pool": nc.gpsimd, "sp": nc.sync}
        self.same = same_engine_sync
        self.gen = 0
        self.dsem = [es.enter_context(nc.semaphore(f"dq{i}")) for i in range(NDS)]
        self.dcnt = [0] * NDS
        self.dnext = {False: 0, True: 0}
        self.seen = {k: {} for k in self.eng}
        self.total = 0
        self.limit = None
        self.new_epoch()

    def new_epoch(self):
        self.sem = {k: self.es.enter_context(self.nc.semaphore(f"e{self.gen}_{k}")) for k in self.eng}
        self.cnt = {k: 0 for k in self.eng}
        self.gen += 1

    def _wait(self, e, ev):
        sem, val = ev
        if sem is self.sem[e] and (e == "pe" or not self.same):
            return
        if self.seen[e].get(sem, 0) >= val:
            return
        self.seen[e][sem] = val
        self.eng[e].wait_ge(sem, val)

    def _deps(self, e, reads, writes):
        for b in reads:
            if b.w is not None:
                self._wait(e, b.w)
        for b in writes:
            if b.w is not None:
                self._wait(e, b.w)
            for sem, val in b.r.items():
                self._wait(e, (sem, val))

    def _commit(self, ev, reads, writes):
        for b in reads:
            if b.r.get(ev[0], 0) < ev[1]:
                b.r[ev[0]] = ev[1]
        for b in writes:
            b.w = ev
            b.r = {}

    def op(self, e, fn, reads=(), writes=()):
        self.total += 1
        if self.limit is not None and self.total > self.limit:
            return
        if any(b.excl for b in reads):
            writes = list(writes) + [b for b in reads if b.excl]
            reads = [b for b in reads if not b.excl]
        self._deps(e, reads, writes)
        ins = fn(self.eng[e])
        self.cnt[e] += 1
        ins.then_inc(self.sem[e], 1)
        self._commit((self.sem[e], self.cnt[e]), reads, writes)

    def dma(self, e, out, in_, reads=(), writes=(), **kw):
        self.total += 1
        if self.limit is not None and self.total > self.limit:
            return
        lo, n = (0, NDS_SW) if e == "pool" else (NDS_SW, NDS - NDS_SW)
        k = lo + self.dnext[e == "pool"] % n
        self.dnext[e == "pool"] += 1
        if self.dcnt[k] > 0:
            self._wait(e, (self.dsem[k], 16 * self.dcnt[k]))
        self._deps(e, reads, writes)
        self.eng[e].dma_start(out=out, in_=in_, **kw).then_inc(self.dsem[k], 16)
        self.dcnt[k] += 1
        self._commit((self.dsem[k], 16 * self.dcnt[k]), reads, writes)

    def barrier(self):
        evs = [(self.sem[k], self.cnt[k]) for k in self.eng if self.cnt[k] > 0]
        evs += [(self.dsem[i], 16 * self.dcnt[i]) for i in range(NDS) if self.dcnt[i] > 0]
        for e in self.eng:
            for ev in evs:
                if ev[0] is self.sem[e]:
                    continue
                self._wait(e, ev)


D = 1024
NIN = 4400
C_Q = 512
C_KC, C_VC, C_KS, C_VS, C_KW, C_VW, C_GN, C_GM = 1536, 1664, 1792, 1920, 2048, 2176, 2304, 2352
POOLW = (2, 4, 8, 16)
GELU_C = 1.5957691216057308


def host_consts(S):
    NQ, NCP, NJ = S // 128, S // 16, S // 64
    NCH = NCP // 128
    bf = ml_dtypes.bfloat16
    c = {}
    c["ident"] = np.eye(128, dtype=np.float32).astype(bf)
    Rm = np.zeros((128, 128), np.float32)
    for hb in (0, 64):
        for d in range(32):
            Rm[hb + d + 32, hb + d] = -1.0
            Rm[hb + d, hb + d + 32] = 1.0
    c["Rm"] = Rm.astype(bf)
    inv = (10000.0 ** (-np.arange(0, 64, 2, dtype=np.float32) / 64)).astype(np.float32)
    inv64 = np.concatenate([inv, inv])
    inv128 = np.concatenate([inv64, inv64])
    pos = np.arange(S, dtype=np.float32)
    ang = (pos[None, :] * inv128[:, None]).astype(np.float32)
    c["cosT"] = np.cos(ang).astype(np.float32)
    c["sinT"] = np.sin(ang).astype(np.float32)
    cpos = (np.arange(NCP, dtype=np.float32) * 16 + 31)
    cang = (cpos[None, :] * inv128[:, None]).astype(np.float32)
    c["ccos"] = np.cos(cang).astype(np.float32)
    c["csin"] = np.sin(cang).astype(np.float32)
    key = np.arange(S)
    c["Emat"] = (key[None, :] // 64 == np.arange(64)[:, None]).astype(np.float32).astype(bf)
    r = np.arange(128)
    c["mdiag"] = (r[:, None] <= r[None, :]).astype(np.float32).astype(bf)
    c["mfar"] = (r[:, None] > r[None, :]).astype(np.float32).astype(bf)
    nd = np.where(r[:, None] > r[None, :], -30000.0, 0.0).astype(np.float32)
    c["negdiag4"] = np.repeat(nd[:, None, :], 4, axis=1).astype(bf)
    n = np.arange(NCP)
    j = np.arange(64)
    ovl = ((16 * n[:, None] <= 64 * j[None, :] + 63) & (16 * n[:, None] + 31 >= 64 * j[None, :])).astype(np.float32)
    ovl[NCP - 1, :] = 0.0
    c["ovl"] = ovl.reshape(NCH, 128, 64).transpose(1, 0, 2).copy().astype(bf)
    t = np.arange(S)
    cm = (16 * n[:, None] + 31 <= t[None, :]).astype(np.float32)
    cm[NCP - 1, :] = 0.0
    c["cmask"] = cm.reshape(NCH, 128, NQ, 128).transpose(2, 1, 0, 3).copy().astype(bf)
    cur = t // 64
    causal = (64 * j[None, :] <= t[:, None])
    forced = causal & ((j[None, :] == 0) | (j[None, :] >= cur[:, None] - 1))
    A = (causal & ~forced).astype(np.float32)
    Bt = np.where(forced, 100.0 + j[None, :], np.where(causal, 0.0, -1.0 - j[None, :])).astype(np.float32)
    AB = np.stack([A, A, Bt, Bt], axis=1)
    c["tkAB"] = AB.reshape(NQ, 128, 4, 64).astype(np.float32)
    corr = np.ones((128, 4, 16), np.float32)
    for g, w in enumerate(POOLW):
        for tt in range(16):
            corr[:, g, tt] = w / min(tt + 1, w)
    c["pcorr"] = corr
    return c


def build(S=4096, depth=4, dbg=False, stop_after=None, same_sync=True):
    nc = bass.Bass("TRN2", target_bir_lowering=False)
    NT, NQ, NCP, NJ = S // 512, S // 128, S // 16, S // 64
    NCMP = NCP - 1
    NCH = NCP // 128
    NT2 = S // 256

    def din(name, shape, dt=F32):
        return nc.dram_tensor(name, shape, dt, kind="ExternalInput").ap()

    def dscr(name, shape, dt):
        return nc.dram_tensor(name, shape, dt, kind=("ExternalOutput" if dbg else "Internal")).ap()

    x_in = din("x", [S, D])
    norm_mix = din("norm_mix", [depth, D])
    w_in = din("w_in", [depth, D, NIN])
    w_pool = din("w_pool", [depth, 4, 128, 128])
    pool_scale = din("pool_scale", [depth, 512])
    pe_k = din("pe_k", [depth, 32, 64])
    pe_v = din("pe_v", [depth, 32, 64])
    w_ck1 = din("w_ck1", [depth, 2048, 256])
    w_ck2 = din("w_ck2", [depth, 256, 64])
    w_cv1 = din("w_cv1", [depth, 2048, 256])
    w_cv2 = din("w_cv2", [depth, 256, 64])
    w_pp = din("w_proj_pool", [depth, 512, D])
    w_pn = din("w_proj_nsa", [depth, D, D])
    w_out = din("w_out", [depth, D, D])
    norm_mlp = din("norm_mlp", [depth, D])
    w_ff1 = din("w_ff1", [depth, D, 4096])
    w_ff2 = din("w_ff2", [depth, 4096, D])
    norm_final = din("norm_final", [D])
    c_ident = din("ident", [128, 128], BF16)
    c_Rm = din("Rm", [128, 128], BF16)
    c_cos = din("cosT", [128, S])
    c_sin = din("sinT", [128, S])
    c_ccos = din("ccos", [128, NCP])
    c_csin = din("csin", [128, NCP])
    c_E = din("Emat", [64, S], BF16)
    c_mdiag = din("mdiag", [128, 128], BF16)
    c_mfar = din("mfar", [128, 128], BF16)
    c_negdiag4 = din("negdiag4", [128, 4, 128], BF16)
    c_ovl = din("ovl", [128, NCH, 64], BF16)
    c_cmask = din("cmask", [NQ, 128, NCH, 128], BF16)
    c_tkAB = din("tkAB", [NQ, 128, 4, 64])
    c_pcorr = din("pcorr", [128, 4, 16])
    out = nc.dram_tensor("out", [S, D], F32, kind="ExternalOutput").ap()

    xres = dscr("xres", [S, D], F32)
    qT = dscr("qT", [D, S], BF16)
    kk = dscr("kk", [4, 128, S], BF16)
    kcvc = dscr("kcvc", [2, 128, S], BF16)
    vtok = dscr("vtok", [2, S, 130], BF16)
    gatesd = dscr("gatesd", [S, 48], F32)
    gmT = dscr("gmT", [2048, S], BF16)
    ypT = dscr("ypT", [512, S], BF16)
    ynT = dscr("ynT", [D, S], BF16)
    dbg_imp = dscr("dbg_imp", [S, 2, 64], F32) if dbg else None
    dbg_sel = dscr("dbg_sel", [S, 2, 64], F32) if dbg else None
    dbg_kcm = dscr("dbg_kcm", [128, 2, NCP], F32) if dbg else None
    dbg_vcm = dscr("dbg_vcm", [128, NCH, 2, 129], F32) if dbg else None

    top = ExitStack()
    with top:
        Sx = Sched(nc, top, same_engine_sync=same_sync)
        Sx.limit = LIMIT

        gcnt = [0]

        def mkT(es):
            cnt = gcnt

            def T(shape, dt, name=None):
                cnt[0] += 1
                nm = f"{name or 't'}_{cnt[0]}"
                return es.enter_context(nc.sbuf_tensor(nm, shape, dt))
            return T

        T0 = mkT(top)
        pbank = [top.enter_context(nc.psum_tensor(f"pb{i}", [128, 512], F32)) for i in range(8)]
        pbuf = [Buf(True) for _ in range(8)]

        class Rot:
            def __init__(self, idxs):
                self.idxs = list(idxs)
                self.i = 0

            def next(self):
                k = self.idxs[self.i % len(self.idxs)]
                self.i += 1
                return pbank[k], pbuf[k]

        class Ring:
            def __init__(self, T, n, shape, dt, name):
                self.t = [T(shape, dt, name) for _ in range(n)]
                self.b = [Buf() for _ in range(n)]
                self.i = 0

            def next(self):
                k = self.i % len(self.t)
                self.i += 1
                return self.t[k], self.b[k]

        ident = T0([128, 128], BF16, "ident"); b_ident = Buf()
        Rm = T0([128, 128], BF16, "Rm"); b_Rm = Buf()
        mdiag = T0([128, 128], BF16, "mdiag"); b_md = Buf()
        mfar = T0([128, 128], BF16, "mfar"); b_mf = Buf()
        epst = T0([128, 1], F32, "eps"); b_eps = Buf()
        gmix = T0([128, depth, 8], F32, "gmix"); b_gmix = Buf()
        gmlp = T0([128, depth, 8], F32, "gmlp"); b_gmlp = Buf()
        pscl = T0([128, depth, 4], F32, "pscl"); b_pscl = Buf()
        Sx.dma("sp", ident[:], c_ident, [], [b_ident])
        Sx.dma("sp", Rm[:], c_Rm, [], [b_Rm])
        Sx.dma("sp", mdiag[:], c_mdiag, [], [b_md])
        Sx.dma("sp", mfar[:], c_mfar, [], [b_mf])
        negdiag4 = T0([128, 4, 128], BF16, "negdiag4"); b_nd4 = Buf()
        Sx.dma("sp", negdiag4[:], c_negdiag4, [], [b_nd4])
        Sx.op("pool", lambda e: e.memset(epst[:], 1e-6), [], [b_eps])
        for l in range(depth):
            Sx.dma("sp", gmix[:, l, :], norm_mix[l].rearrange("(c p) -> p c", p=128), [], [b_gmix], allow_slow_non_contiguous=True)
            Sx.dma("sp", gmlp[:, l, :], norm_mlp[l].rearrange("(c p) -> p c", p=128), [], [b_gmlp], allow_slow_non_contiguous=True)
            Sx.dma("sp", pscl[:, l, :], pool_scale[l].rearrange("(c p) -> p c", p=128), [], [b_pscl], allow_slow_non_contiguous=True)

        b_xres = [Buf() for _ in range(NT2)]
        b_q = [Buf() for _ in range(NT)]
        b_kk = [Buf() for _ in range(NT)]
        b_kcvc = [Buf() for _ in range(NT)]
        b_vtok = [Buf() for _ in range(NT)]
        b_gates = [Buf() for _ in range(NT)]
        b_gm = [Buf() for _ in range(NT)]
        b_yp = [Buf() for _ in range(NT)]
        b_yn = [Buf() for _ in range(NT)]

        def xres_bufs(t0, n):
            return b_xres[t0 // 256:(t0 + n) // 256]

        def norm_stats(xt, bxt, ntj, hn, bhn, ss, bss):
            for j in range(ntj):
                Sx.op("act", lambda e, j=j: e.activation(out=hn[:, j, :], in_=xt[:, j, :], func=AF.Square, accum_out=ss[:, j:j + 1]),
                      [bxt], [bhn, bss])
            Sx.op("act", lambda e: e.activation(out=ss[:, 4:4 + ntj], in_=ss[:, 0:ntj], func=AF.Sqrt, scale=1.0 / D, bias=epst[:, 0:1]),
                  [bss, b_eps], [bss])
            Sx.op("dve", lambda e: e.reciprocal(out=ss[:, 4:4 + ntj], in_=ss[:, 4:4 + ntj]), [bss], [bss])
            for j in range(ntj):
                Sx.op("dve", lambda e, j=j: e.tensor_scalar(out=hn[:, j, :], in0=xt[:, j, :], scalar1=ss[:, 4 + j:5 + j], scalar2=None, op0=ALU.mult),
                      [bxt, bss], [bhn])

        def norm_tr(ntj, gcol, bg, hn, bhn, hT, bhT, rot):
            for c in range(8):
                pt, pb = rot.next()
                pv = pt[:].bitcast(BF16)
                for j in range(ntj):
                    Sx.op("pe", lambda e, j=j, c=c, pv=pv: e.transpose(out=pv[:, j * 128:(j + 1) * 128], in_=hn[:, j, c * 128:(c + 1) * 128], identity=ident[:]),
                          [bhn, b_ident], [pb])
                Sx.op("dve", lambda e, c=c, pv=pv: e.tensor_scalar(out=hT[:, c, :], in0=pv[:, 0:ntj * 128], scalar1=gcol[:, c:c + 1], scalar2=None, op0=ALU.mult),
                      [pb, bg], [bhT])

        def phase_A(l, xsrc):
            with ExitStack() as es:
                T = mkT(es)
                rotA = Rot([0, 1, 2, 3])
                rotR = Rot([4, 5])
                rotM = Rot([6, 7])
                wA = T([128, 8, NIN], BF16, "wA"); b_w = [Buf() for _ in range(8)]
                for k in range(8):
                    Sx.dma("pool", wA[:, k, :], w_in[l, k * 128:(k + 1) * 128, :], [], [b_w[k]])
                wdup = T([128, 8, 4, 128], BF16, "wdup"); b_wdup = Buf()
                for i, cb in enumerate((C_KS, C_KS + 64, C_KW, C_KW + 64)):
                    Sx.op("pool", lambda e, i=i, cb=cb: e.tensor_copy(
                        out=wdup[:, :, i, :].rearrange("p k (two d) -> p k two d", two=2),
                        in_=wA[:, :, cb:cb + 64].unsqueeze(2).to_broadcast([128, 8, 2, 64])), b_w, [b_wdup])
                wpl = T([128, 4, 128], BF16, "wpl"); b_wpl = Buf()
                Sx.dma("pool", wpl[:], w_pool[l].rearrange("g c d -> c g d"), [], [b_wpl])
                pcorr = T([128, 4, 16], F32, "pcorr"); b_pc = Buf()
                Sx.dma("sp", pcorr[:], c_pcorr, [], [b_pc])
                xt = T([128, 4, D], F32, "xt"); b_xt = Buf()
                hn = T([128, 4, D], BF16, "hn"); b_hn = Buf()
                hT = T([128, 8, 512], BF16, "hT"); b_hT = Buf()
                ss = T([128, 8], F32, "ss"); b_ss = Buf()
                csr = Ring(T, 2, [128, 2, 512], F32, "cs")
                ubuf = T([128, 4, 528], F32, "ubuf"); b_u = [Buf() for _ in range(4)]
                ptmp = Ring(T, 2, [128, 528], F32, "ptmp")
                dTr = Ring(T, 4, [128, 512], BF16, "dT")
                xbr = Ring(T, 2, [128, 512], BF16, "xb")
                t1r = Ring(T, 2, [128, 512], F32, "t1")
                t2r = Ring(T, 2, [128, 512], F32, "t2")
                qst = T([128, 8, 512], BF16, "qst"); b_qst = Buf()
                kst = T([128, 4, 512], BF16, "kst"); b_kst = Buf()
                cst = T([128, 2, 512], BF16, "cst"); b_cst = Buf()
                gst = T([128, 16, 512], BF16, "gst"); b_gst = Buf()
                yst = T([128, 4, 512], BF16, "yst"); b_yst = Buf()
                vst = T([128, 4, 2, 130], BF16, "vst"); b_vst = Buf()
                gts = T([128, 4, 48], F32, "gts"); b_gts = Buf()
                Sx.op("pool", lambda e: e.memset(vst[:], 1.0), [], [b_vst])
                Sx.op("pool", lambda e: e.memset(ubuf[:], 0.0), [], b_u)

                def rope_to(pt, pb, dest, bdest, cs, b_cs):
                    xb, bxb = xbr.next()
                    Sx.op("act", lambda e: e.activation(out=xb[:], in_=pt[:, 0:512], func=AF.Copy), [pb], [bxb])

                    def part2():
                        p2, pb2 = rotR.next()
                        Sx.op("pe", lambda e: e.matmul(p2[:, 0:512], lhsT=Rm[:], rhs=xb[:], start=True, stop=True), [bxb, b_Rm], [pb2])
                        t1, bt1 = t1r.next()
                        t2, bt2 = t2r.next()
                        Sx.op("dve", lambda e: e.tensor_tensor(out=t1[:], in0=pt[:, 0:512], in1=cs[:, 0, :], op=ALU.mult), [pb, b_cs], [bt1])
                        Sx.op("dve", lambda e: e.tensor_tensor(out=t2[:], in0=p2[:, 0:512], in1=cs[:, 1, :], op=ALU.mult), [pb2, b_cs], [bt2])
                        Sx.op("pool", lambda e: e.tensor_tensor(out=dest, in0=t1[:], in1=t2[:], op=ALU.add), [bt1, bt2], [bdest])
                    return part2

                def prepA(tt):
                    tsl_ = slice(tt * 512, tt * 512 + 512)
                    Sx.dma("sp", xt[:], xsrc[tsl_].rearrange("(j p) f -> p j f", p=128), xres_bufs(tt * 512, 512) if xsrc is xres else [], [b_xt])
                    norm_stats(xt, b_xt, 4, hn, b_hn, ss, b_ss)

                prepA(0)
                for tt in range(NT):
                    t0 = tt * 512
                    tsl = slice(t0, t0 + 512)
                    cs, b_cs = csr.next()
                    Sx.dma("sp", cs[:, 0, :], c_cos[:, tsl], [], [b_cs])
                    Sx.dma("sp", cs[:, 1, :], c_sin[:, tsl], [], [b_cs])
                    norm_tr(4, gmix[:, l, :], b_gmix, hn, b_hn, hT, b_hT, rotM)

                    def fm_chunk(lhs_fn, wbufs):
                        pt, pb = rotA.next()
                        for k in range(8):
                            Sx.op("pe", lambda e, k=k: e.matmul(pt[:, 0:512], lhsT=lhs_fn(k), rhs=hT[:, k, :], start=(k == 0), stop=(k == 7)),
                                  [b_hT] + wbufs, [pb])
                        return pt, pb

                    for g in range(4):
                        pt, pb = fm_chunk(lambda k, g=g: wA[:, k, g * 128:(g + 1) * 128], b_w)
                        Sx.op("act", lambda e, g=g, pt=pt: e.activation(out=ubuf[:, g, 16:528], in_=pt[:, 0:512], func=AF.Copy), [pb], [b_u[g]])
                    rp = None
                    for c in range(8):
                        pt, pb = fm_chunk(lambda k, c=c: wA[:, k, C_Q + c * 128:C_Q + (c + 1) * 128], b_w)
                        if rp is not None:
                            rp()
                        rp = rope_to(pt, pb, qst[:, c, :], b_qst, cs, b_cs)
                    if tt + 1 < NT:
                        prepA(tt + 1)
                    for i in range(4):
                        pt, pb = fm_chunk(lambda k, i=i: wdup[:, k, i, :], [b_wdup])
                        rp()
                        rp = rope_to(pt, pb, kst[:, i, :], b_kst, cs, b_cs)
                    for i, cb in enumerate((C_KC, C_VC)):
                        pt, pb = fm_chunk(lambda k, cb=cb: wA[:, k, cb:cb + 128], b_w)
                        if rp is not None:
                            rp()
                            rp = None
                        Sx.op("act", lambda e, i=i, pt=pt: e.activation(out=cst[:, i, :], in_=pt[:, 0:512], func=AF.Copy), [pb], [b_cst])
                    pool_mms = []
                    for g in range(4):
                        cur = ubuf[:, g, :]
                        curb = b_u[g]
                        sh = 1
                        v0 = 0
                        for step in range(g + 1):
                            nx, bnx = ptmp.next()
                            Sx.op("pool", lambda e, cur=cur, nx=nx, sh=sh, v0=v0: e.tensor_tensor(out=nx[:, v0 + sh:528], in0=cur[:, v0 + sh:528], in1=cur[:, v0:528 - sh], op=ALU.add),
                                  [curb], [bnx])
                            cur, curb = nx, bnx
                            v0 += sh
                            sh *= 2
                        if tt == 0:
                            Sx.op("pool", lambda e, cur=cur, g=g: e.tensor_tensor(out=cur[:, 16:32], in0=cur[:, 16:32], in1=pcorr[:, g, :], op=ALU.mult),
                                  [curb, b_pc], [curb])
                        dT, bdT = dTr.next()
                        Sx.op("dve", lambda e, cur=cur, g=g, dT=dT: e.scalar_tensor_tensor(out=dT[:], in0=cur[:, 16:528], scalar=1.0 / POOLW[g], in1=ubuf[:, g, 16:528],
                                                                                         op0=ALU.mult, op1=ALU.subtract), [curb, b_u[g]], [bdT])
                        def pool_mm(g=g, dT=dT, bdT=bdT):
                            pt, pb = rotA.next()
                            Sx.op("pe", lambda e: e.matmul(pt[:, 0:512], lhsT=wpl[:, g, :], rhs=dT[:], start=True, stop=True), [bdT, b_wpl], [pb])
                            Sx.op("act", lambda e: e.activation(out=yst[:, g, :], in_=pt[:, 0:512], func=AF.Copy, scale=pscl[:, l, g:g + 1]), [pb, b_pscl], [b_yst])
                        pool_mms.append(pool_mm)
                        Sx.op("pool", lambda e, g=g: e.tensor_copy(out=ubuf[:, g, 0:16], in_=ubuf[:, g, 512:528]), [b_u[g]], [b_u[g]])
                    for c in range(16):
                        pt, pb = fm_chunk(lambda k, c=c: wA[:, k, C_GM + c * 128:C_GM + (c + 1) * 128], b_w)
                        Sx.op("act", lambda e, c=c, pt=pt: e.activation(out=gst[:, c, :], in_=pt[:, 0:512], func=AF.Sigmoid), [pb], [b_gst])
                        if c >= 8 and c % 2 == 0 and pool_mms:
                            pool_mms.pop(0)()
                    while pool_mms:
                        pool_mms.pop(0)()
                    for j in range(4):
                        pt, pb = rotA.next()
                        for (cb, wdt, o0) in ((C_VS, 128, 0), (C_VW, 128, 128), (C_GN, 48, 256)):
                            for k in range(8):
                                Sx.op("pe", lambda e, k=k, j=j, cb=cb, wdt=wdt, o0=o0, pt=pt: e.matmul(pt[:, o0:o0 + wdt], lhsT=hT[:, k, j * 128:(j + 1) * 128],
                                                                                                     rhs=wA[:, k, cb:cb + wdt], start=(k == 0), stop=(k == 7)),
                                      [b_hT] + b_w, [pb])
                        Sx.op("act", lambda e, j=j, pt=pt: e.activation(
                            out=vst[:, j, :, :].rearrange("p b (g c) -> p b g c", g=2)[:, :, :, 0:64],
                            in_=pt[:, 0:256].rearrange("p (b g d) -> p b g d", b=2, g=2), func=AF.Copy), [pb], [b_vst])
                        Sx.op("act", lambda e, j=j, pt=pt: e.activation(out=gts[:, j, :], in_=pt[:, 256:304], func=AF.Sigmoid), [pb], [b_gts])
                    Sx.dma("sp", qT.rearrange("(c p) s -> p c s", p=128)[:, :, tsl], qst[:], [b_qst], [b_q[tt]])
                    Sx.dma("sp", kk.rearrange("i p s -> p i s")[:, :, tsl], kst[:], [b_kst], [b_kk[tt]])
                    Sx.dma("sp", kcvc.rearrange("i p s -> p i s")[:, :, tsl], cst[:], [b_cst], [b_kcvc[tt]])
                    Sx.dma("sp", gmT.rearrange("(c p) s -> p c s", p=128)[:, :, tsl], gst[:], [b_gst], [b_gm[tt]])
                    Sx.dma("sp", ypT.rearrange("(c p) s -> p c s", p=128)[:, :, tsl], yst[:], [b_yst], [b_yp[tt]])
                    for bb in range(2):
                        Sx.dma("sp", vtok[bb].rearrange("(n p) c -> p n c", p=128)[:, tt * 4:(tt + 1) * 4], vst[:, :, bb, :], [b_vst], [b_vtok[tt]])
                    Sx.dma("sp", gatesd.rearrange("(n p) c -> p n c", p=128)[:, tt * 4:(tt + 1) * 4], gts[:], [b_gts], [b_gates[tt]])
                Sx.barrier()

        def phase_BC(l):
            with ExitStack() as es:
                T = mkT(es)
                kcm = T([128, 2, 2, NCP], BF16, "kcm"); b_kcm = Buf()
                vcm = T([128, NCH, 2, 129], BF16, "vcm"); b_vcm = Buf()
                ovl = T([128, NCH, 64], BF16, "ovl"); b_ovl = Buf()
                Sx.dma("sp", ovl[:], c_ovl, [], [b_ovl])
                Sx.op("pool", lambda e: e.memset(kcm[:], 0.0), [], [b_kcm])
                Sx.op("pool", lambda e: e.memset(vcm[:], 0.0), [], [b_vcm])
                with ExitStack() as esb:
                    Tb = mkT(esb)
                    rotB = Rot([0, 1, 2, 3])
                    kv = Tb([128, 2, S], BF16, "kv"); b_kv = Buf()
                    for i in range(2):
                        Sx.dma("sp", kv[:, i, :], kcvc[i], b_kcvc, [b_kv])
                    w1 = Tb([128, 2, 32, 256], BF16, "w1"); b_w1 = Buf()
                    for i, wsrc in enumerate((w_ck1, w_cv1)):
                        for hh in range(2):
                            for lq in range(4):
                                Sx.dma("pool", w1[hh * 64:(hh + 1) * 64, i, lq * 8:(lq + 1) * 8, :],
                                       wsrc[l, lq * 512:(lq + 1) * 512, :].rearrange("(l d) h -> d l h", d=64), [], [b_w1])
                    w2k = Tb([128, 2, 128], BF16, "w2k"); b_w2k = Buf()
                    w2v = Tb([128, 2, 64], BF16, "w2v"); b_w2v = Buf()
                    for hh in range(2):
                        Sx.dma("pool", w2k[:, :, hh * 64:(hh + 1) * 64], w_ck2[l].rearrange("(c p) d -> p c d", p=128), [], [b_w2k])
                    Sx.dma("pool", w2v[:], w_cv2[l].rearrange("(c p) d -> p c d", p=128), [], [b_w2v])
                    pef = Tb([64, 2, 32], F32, "pef"); b_pef = Buf()
                    peb = Tb([64, 2, 32], BF16, "peb"); b_peb = Buf()
                    Sx.dma("sp", pef[:, 0, :], pe_k[l].rearrange("l d -> d l"), [], [b_pef], allow_slow_non_contiguous=True)
                    Sx.dma("sp", pef[:, 1, :], pe_v[l].rearrange("l d -> d l"), [], [b_pef], allow_slow_non_contiguous=True)
                    Sx.op("act", lambda e: e.activation(out=peb[:], in_=pef[:], func=AF.Copy), [b_pef], [b_peb])
                    ccs = Tb([128, 2, NCP], F32, "ccs"); b_ccs = Buf()
                    Sx.dma("sp", ccs[:, 0, :], c_ccos, [], [b_ccs])
                    Sx.dma("sp", ccs[:, 1, :], c_csin, [], [b_ccs])
                    bias = Tb([128, 2, 2], F32, "bias"); b_bias = Buf()
                    for i in range(2):
                        for hc in range(2):
                            pt, pb = rotB.next()
                            for lq in range(32):
                                Sx.op("pe", lambda e, i=i, hc=hc, lq=lq, pt=pt: e.matmul(pt[:, 0:1], lhsT=w1[0:64, i, lq, hc * 128:(hc + 1) * 128], rhs=peb[:, i, lq:lq + 1],
                                                                                       start=(lq == 0), stop=(lq == 31)), [b_w1, b_peb], [pb])
                            Sx.op("act", lambda e, i=i, hc=hc, pt=pt: e.activation(out=bias[:, i, hc:hc + 1], in_=pt[:, 0:1], func=AF.Copy), [pb], [b_bias])
                    hT2 = Tb([128, 2, 2, 2, NCP], BF16, "hT2"); b_h2 = Buf()
                    xh = Ring(Tb, 2, [128, NCP], F32, "xh")
                    x2 = Ring(Tb, 2, [128, NCP], F32, "x2")
                    sg = Ring(Tb, 2, [128, NCP], F32, "sg")
                    for i in range(2):
                        for hc in range(2):
                            for g in range(2):
                                pt, pb = rotB.next()
                                for lq in range(32):
                                    Sx.op("pe", lambda e, i=i, hc=hc, g=g, lq=lq, pt=pt: e.matmul(
                                        pt[:, 0:NCMP], lhsT=w1[g * 64:(g + 1) * 64, i, lq, hc * 128:(hc + 1) * 128],
                                        rhs=kv[g * 64:(g + 1) * 64, i, lq:lq + 16 * (NCMP - 1) + 1:16], start=(lq == 0), stop=(lq == 31)), [b_w1, b_kv], [pb])
                                a, ba = xh.next()
                                b2, bb2 = x2.next()
                                c2, bc2 = sg.next()
                                N = NCMP
                                Sx.op("act", lambda e, a=a, pt=pt, i=i, hc=hc: e.activation(out=a[:, 0:N], in_=pt[:, 0:N], func=AF.Identity, bias=bias[:, i, hc:hc + 1]), [pb, b_bias], [ba])
                                Sx.op("dve", lambda e, a=a, b2=b2: e.tensor_tensor(out=b2[:, 0:N], in0=a[:, 0:N], in1=a[:, 0:N], op=ALU.mult), [ba], [bb2])
                                Sx.op("dve", lambda e, b2=b2: e.tensor_scalar(out=b2[:, 0:N], in0=b2[:, 0:N], scalar1=0.044715, scalar2=1.0, op0=ALU.mult, op1=ALU.add), [bb2], [bb2])
                                Sx.op("dve", lambda e, a=a, b2=b2: e.tensor_tensor(out=b2[:, 0:N], in0=b2[:, 0:N], in1=a[:, 0:N], op=ALU.mult), [bb2, ba], [bb2])
                                Sx.op("act", lambda e, b2=b2, c2=c2: e.activation(out=c2[:, 0:N], in_=b2[:, 0:N], func=AF.Sigmoid, scale=GELU_C), [bb2], [bc2])
                                Sx.op("dve", lambda e, a=a, c2=c2, i=i, hc=hc, g=g: e.tensor_tensor(out=hT2[:, i, hc, g, 0:N], in0=a[:, 0:N], in1=c2[:, 0:N], op=ALU.mult), [ba, bc2], [b_h2])
                    xbk = Tb([128, NCP], BF16, "xbk"); b_xbk = Buf()
                    tk1 = Tb([128, NCP], F32, "tk1"); b_tk1 = Buf()
                    tk2 = Tb([128, NCP], F32, "tk2"); b_tk2 = Buf()
                    for g in range(2):
                        pt, pb = rotB.next()
                        for hc in range(2):
                            Sx.op("pe", lambda e, g=g, hc=hc, pt=pt: e.matmul(pt[:, 0:NCMP], lhsT=w2k[:, hc, :], rhs=hT2[:, 0, hc, g, 0:NCMP], start=(hc == 0), stop=(hc == 1)),
                                  [b_w2k, b_h2], [pb])
                        Sx.op("act", lambda e, pt=pt: e.activation(out=xbk[:, 0:NCMP], in_=pt[:, 0:NCMP], func=AF.Copy), [pb], [b_xbk])
                        p2, pb2 = rotB.next()
                        Sx.op("pe", lambda e, p2=p2: e.matmul(p2[:, 0:NCMP], lhsT=Rm[:], rhs=xbk[:, 0:NCMP], start=True, stop=True), [b_xbk, b_Rm], [pb2])
                        Sx.op("dve", lambda e, pt=pt: e.tensor_tensor(out=tk1[:, 0:NCMP], in0=pt[:, 0:NCMP], in1=ccs[:, 0, 0:NCMP], op=ALU.mult), [pb, b_ccs], [b_tk1])
                        Sx.op("dve", lambda e, p2=p2: e.tensor_tensor(out=tk2[:, 0:NCMP], in0=p2[:, 0:NCMP], in1=ccs[:, 1, 0:NCMP], op=ALU.mult), [pb2, b_ccs], [b_tk2])
                        for hp in range(2):
                            Sx.op("dve", lambda e, g=g, hp=hp: e.tensor_tensor(out=kcm[hp * 64:(hp + 1) * 64, g, hp, 0:NCMP], in0=tk1[hp * 64:(hp + 1) * 64, 0:NCMP],
                                                                             in1=tk2[hp * 64:(hp + 1) * 64, 0:NCMP], op=ALU.add), [b_tk1, b_tk2], [b_kcm])
                    for nch in range(NCH):
                        nn = min(128, NCMP - nch * 128)
                        for g in range(2):
                            pt, pb = rotB.next()
                            for hc in range(2):
                                Sx.op("pe", lambda e, g=g, hc=hc, nch=nch, nn=nn, pt=pt: e.matmul(pt[0:nn, 0:64], lhsT=hT2[:, 1, hc, g, nch * 128:nch * 128 + nn], rhs=w2v[:, hc, :],
                                                                                                start=(hc == 0), stop=(hc == 1)), [b_w2v, b_h2], [pb])
                            Sx.op("act", lambda e, g=g, nch=nch, nn=nn, pt=pt: e.activation(out=vcm[0:nn, nch, g, 0:64], in_=pt[0:nn, 0:64], func=AF.Copy), [pb], [b_vcm])
                        for g in range(2):
                            Sx.op("pool", lambda e, g=g, nch=nch, nn=nn: e.memset(vcm[0:nn, nch, g, 64:65], 1.0), [], [b_vcm])
                            Sx.op("pool", lambda e, g=g, nch=nch: e.tensor_copy(out=vcm[:, nch, g, 65:129], in_=ovl[:, nch, :]), [b_ovl], [b_vcm])
                    if dbg:
                        dk = Tb([128, 2, NCP], F32, "dk"); b_dk = Buf()
                        dv = Tb([128, NCH, 2, 129], F32, "dv"); b_dv = Buf()
                        Sx.op("dve", lambda e: e.tensor_tensor(out=dk[:], in0=kcm[:, :, 0, :], in1=kcm[:, :, 1, :], op=ALU.add), [b_kcm], [b_dk])
                        Sx.op("dve", lambda e: e.tensor_copy(out=dv[:], in_=vcm[:]), [b_vcm], [b_dv])
                        Sx.dma("sp", dbg_kcm, dk[:], [b_dk], [])
                        Sx.dma("sp", dbg_vcm, dv[:], [b_dv], [])
                    Sx.barrier()
                if stop_after == "B":
                    return
                rotS = Rot([0, 1, 2, 3])
                rotAcc = Rot([4, 5])
                rotI = Rot([6])
                rotX = Rot([7])
                ks2 = T([128, 2, 2, S], BF16, "ks2"); b_ks2 = Buf()
                kw2 = T([128, 2, 2, S], BF16, "kw2"); b_kw2 = Buf()
                for hp in range(2):
                    oh = 1 - hp
                    Sx.op("pool", lambda e: e.memset(ks2[oh * 64:(oh + 1) * 64, :, hp, :], 0.0), [], [b_ks2])
                    Sx.op("pool", lambda e: e.memset(kw2[oh * 64:(oh + 1) * 64, :, hp, :], 0.0), [], [b_kw2])
                for g in range(2):
                    for hp in range(2):
                        Sx.dma("sp", ks2[hp * 64:(hp + 1) * 64, g, hp, :], kk[g, hp * 64:(hp + 1) * 64, :], b_kk, [b_ks2])
                        Sx.dma("sp", kw2[hp * 64:(hp + 1) * 64, g, hp, :], kk[2 + g, hp * 64:(hp + 1) * 64, :], b_kk, [b_kw2])
                vs1 = T([128, NQ, 2, 65], BF16, "vs1"); b_vs1 = Buf()
                vw1 = T([128, NQ, 2, 65], BF16, "vw1"); b_vw1 = Buf()
                for n0 in range(0, NQ, 8):
                    Sx.dma("sp", vs1[:, n0:n0 + 8].rearrange("p n g c -> p n (g c)"), vtok[0].rearrange("(n p) c -> p n c", p=128)[:, n0:n0 + 8], b_vtok, [b_vs1])
                    Sx.dma("sp", vw1[:, n0:n0 + 8].rearrange("p n g c -> p n (g c)"), vtok[1].rearrange("(n p) c -> p n c", p=128)[:, n0:n0 + 8], b_vtok, [b_vw1])
                gat = T([128, NQ, 48], F32, "gat"); b_gat = Buf()
                Sx.dma("sp", gat[:], gatesd.rearrange("(n p) c -> p n c", p=128), b_gates, [b_gat])
                Esb = T([128, S], BF16, "Esb"); b_E = Buf()
                Sx.op("pool", lambda e: e.memset(Esb[64:128, :], 0.0), [], [b_E])
                Sx.dma("sp", Esb[0:64, :], c_E, [], [b_E])
                qbr = Ring(T, 2, [128, 8, 512], BF16, "qb")
                ynst = T([128, 8, 512], BF16, "ynst"); b_ynst = Buf()
                ptr = Ring(T, 4, [128, 512], BF16, "pt")
                obf = T([128, D], BF16, "obf"); b_obf = Buf()
                tmp4r = Ring(T, 2, [128, 4, 64], F32, "tmp4")
                tmpIr = Ring(T, 2, [128, 4, 64], F32, "tmpI")
                rinv = Ring(T, 2, [128, 8], F32, "rinv")
                impp = T([128, 4, 64], F32, "impp"); b_impp = Buf()
                imp = T([128, 2, 64], F32, "imp"); b_imp = Buf()
                score = T([128, 2, 64], F32, "score"); b_score = Buf()
                m8 = T([128, 2, 8], F32, "m8"); b_m8 = Buf()
                s1 = T([128, 2, 64], F32, "s1"); b_s1 = Buf()
                s2 = T([128, 2, 64], F32, "s2"); b_s2 = Buf()
                selq = T([128, 2, 64], BF16, "selq"); b_selq = Buf()
                self32 = T([128, 2, 64], F32, "self32"); b_self32 = Buf()
                abr = Ring(T, 2, [128, 4, 64], F32, "ab")
                cmr = Ring(T, 2, [128, NCH, 128], BF16, "cm")

                bank_owner = {}
                side = collections.deque()

                def side_push(fns, grp=None):
                    for fn in fns:
                        side.append((fn, grp))
                        if grp is not None:
                            grp["pend"] = grp.get("pend", 0) + 1

                def side_pop(k):
                    for _ in range(k):
                        if not side:
                            return
                        fn, grp = side.popleft()
                        fn()
                        if grp is not None:
                            grp["pend"] -= 1

                def side_drain_group(grp):
                    while grp is not None and grp.get("pend", 0) > 0:
                        side_pop(1)

                def side_drain_all():
                    while side:
                        side_pop(1)

                def evac_ops(grp):
                    tl = grp["tile"]
                    qi, g, p, b = tl["qi"], grp["g"], grp["p"], grp["b"]
                    acc, bacc = grp["acc"]
                    osb, b_osb = tl["osb"]
                    av = acc[:, 0:260].rearrange("p (c e) -> p c e", e=65)
                    h0 = (8 * g + p) * 3 + b
                    ov = osb[:].rearrange("p (h d) -> p h d", d=64)[:, 8 * g + p:8 * g + p + 7:2, :]
                    R = {}

                    def o1():
                        R["rv"], R["brv"] = rinv.next()
                        Sx.op("dve", lambda e: e.tensor_scalar(out=R["rv"][:, 0:4], in0=av[:, :, 64], scalar1=1e-30, scalar2=None, op0=ALU.max), [bacc], [R["brv"]])

                    def o2():
                        Sx.op("dve", lambda e: e.reciprocal(out=R["rv"][:, 0:4], in_=R["rv"][:, 0:4]), [R["brv"]], [R["brv"]])

                    def o3():
                        Sx.op("dve", lambda e: e.tensor_tensor(out=R["rv"][:, 4:8], in0=R["rv"][:, 0:4], in1=gat[:, qi, h0:h0 + 19:6], op=ALU.mult), [R["brv"], b_gat], [R["brv"]])
                    ops = [o1, o2, o3]
                    if b == 0:
                        accI, baccI = grp["accI"]

                        def o4():
                            Sx.op("dve", lambda e: e.tensor_tensor(out=ov, in0=av[:, :, 0:64], in1=R["rv"][:, 4:8].unsqueeze(2).to_broadcast([128, 4, 64]), op=ALU.mult),
                                  [bacc, R["brv"]], [b_osb])

                        def o5():
                            R["tI"], R["btI"] = tmpIr.next()
                            Sx.op("dve", lambda e: e.tensor_tensor(out=R["tI"][:], in0=accI[:, 0:256].rearrange("p (c j) -> p c j", j=64),
                                                                   in1=R["rv"][:, 0:4].unsqueeze(2).to_broadcast([128, 4, 64]), op=ALU.mult), [baccI, R["brv"]], [R["btI"]])

                        def o6():
                            Sx.op("dve", lambda e: e.tensor_reduce(out=impp[:, 2 * g + p, :], in_=R["tI"][:].rearrange("p c j -> p j c"), axis=AX.X, op=ALU.add),
                                  [R["btI"]], [b_impp])
                        ops += [o4, o5, o6]
                        if g == 1 and p == 1:
                            ops += topk_ops(tl)
                    else:
                        def o4():
                            R["t4"], R["bt4"] = tmp4r.next()
                            Sx.op("dve", lambda e: e.tensor_tensor(out=R["t4"][:], in0=av[:, :, 0:64], in1=R["rv"][:, 4:8].unsqueeze(2).to_broadcast([128, 4, 64]), op=ALU.mult),
                                  [bacc, R["brv"]], [R["bt4"]])

                        def o5():
                            Sx.op("pool", lambda e: e.tensor_tensor(out=ov, in0=ov, in1=R["t4"][:], op=ALU.add), [R["bt4"], b_osb], [b_osb])
                        ops += [o4, o5]
                    return ops

                def topk_ops(tl):
                    qi = tl["qi"]
                    ops = []

                    def mk(eng, fn, rd, wr):
                        ops.append(lambda: Sx.op(eng, fn, rd, wr))
                    ab_ = lambda: tl["ab"]
                    ops.append(lambda: Sx.op("pool", lambda e: e.tensor_tensor(out=imp[:], in0=impp[:, 0:4:2, :], in1=impp[:, 1:4:2, :], op=ALU.add), [b_impp], [b_imp]))
                    ops.append(lambda: Sx.op("pool", lambda e: e.tensor_tensor(out=score[:], in0=imp[:], in1=ab_()[0][:, 0:2, :], op=ALU.mult), [b_imp, ab_()[1]], [b_score]))
                    ops.append(lambda: Sx.op("pool", lambda e: e.tensor_tensor(out=score[:], in0=score[:], in1=ab_()[0][:, 2:4, :], op=ALU.add), [b_score, ab_()[1]], [b_score]))
                    for g in range(2):
                        mk("dve", lambda e, g=g: e.max(out=m8[:, g, :], in_=score[:, g, :]), [b_score], [b_m8])
                        mk("dve", lambda e, g=g: e.match_replace(out=s1[:, g, :], in_to_replace=m8[:, g, :], in_values=score[:, g, :], imm_value=-1e9), [b_score, b_m8], [b_s1])
                        mk("dve", lambda e, g=g: e.max(out=m8[:, g, :], in_=s1[:, g, :]), [b_s1], [b_m8])
                        mk("dve", lambda e, g=g: e.match_replace(out=s2[:, g, :], in_to_replace=m8[:, g, :], in_values=s1[:, g, :], imm_value=-1e9), [b_s1, b_m8], [b_s2])
                    mk("pool", lambda e: e.tensor_single_scalar(out=s2[:], in_=s2[:], scalar=-1e8, op=ALU.is_lt), [b_s2], [b_s2])
                    mk("dve", lambda e: e.scalar_tensor_tensor(out=self32[:], in0=score[:], scalar=-0.5, in1=s2[:], op0=ALU.is_gt, op1=ALU.mult),
                       [b_score, b_s2], [b_self32])
                    mk("pool", lambda e: e.tensor_scalar(out=selq[:], in0=self32[:], scalar1=30000.0, scalar2=-30000.0, op0=ALU.mult, op1=ALU.add), [b_self32], [b_selq])
                    if dbg:
                        ops.append(lambda: Sx.dma("sp", dbg_imp[qi * 128:(qi + 1) * 128], imp[:], [b_imp], []))
                        ops.append(lambda: Sx.dma("sp", dbg_sel[qi * 128:(qi + 1) * 128], self32[:], [b_self32], []))
                    ops += topk_T_ops(tl)
                    return ops

                def topk_T_ops(tl):
                    R = {}

                    def pe_part():
                        R["sT"], R["bsT"] = selTr.next()
                        R["px"], R["pbx"] = rotX.next()
                        pxv = R["px"][:].bitcast(BF16)
                        for g in range(2):
                            Sx.op("pe", lambda e, g=g: e.transpose(out=pxv[0:64, g * 128:(g + 1) * 128], in_=selq[:, g, :], identity=ident[:]), [b_selq, b_ident], [R["pbx"]])

                    def act_part():
                        pxv = R["px"][:].bitcast(BF16)
                        Sx.op("act", lambda e: e.activation(out=R["sT"][0:64], in_=pxv[0:64, 0:256].rearrange("p (g q) -> p g q", g=2).unsqueeze(2).to_broadcast([64, 2, 4, 128]), func=AF.Copy),
                              [R["pbx"]], [R["bsT"]])
                        tl["selT"] = (R["sT"], R["bsT"])
                    nop = lambda: None
                    return [pe_part, nop, nop, nop, act_part]

                def build_selmask(tl, k0):
                    qi = tl["qi"]
                    if "selT" not in tl:
                        side_drain_all()
                    sT, bsT = tl["selT"]
                    nk = min(2, qi + 1 - k0)
                    px, pbx = rotI.next()
                    side_drain_group(bank_owner.get(id(pbx)))
                    bsm = b_smp[k0 // 2]
                    for kk_ in range(nk):
                        kc = k0 + kk_
                        Sx.op("pe", lambda e, kc=kc, kk_=kk_: e.matmul(px[:, kk_ * 256:(kk_ + 1) * 256], lhsT=Esb[:, kc * 128:(kc + 1) * 128],
                                                                      rhs=sT[:].rearrange("p g q -> p (g q)"), start=True, stop=True), [b_E, bsT], [pbx])

                    def act_part():
                        Sx.op("act", lambda e: e.activation(out=selmask[:, k0:k0 + nk].rearrange("p k g q -> p (k g q)"), in_=px[:, 0:nk * 256], func=AF.Copy),
                              [pbx], [bsm])
                        if k0 <= qi < k0 + nk:
                            Sx.op("pool", lambda e: e.tensor_tensor(out=selmask[:, qi], in0=selmask[:, qi], in1=mdiag[:].unsqueeze(1).to_broadcast([128, 2, 128]), op=ALU.mult),
                                  [bsm, b_md], [bsm])
                    return act_part

                def tile_pre(tl):
                    qi = tl["qi"]
                    if qi % 4 == 0:
                        qcur[0] = qbr.next()
                        qb, bq = qcur[0]
                        Sx.dma("sp", qb[:], qT.rearrange("(c p) s -> p c s", p=128)[:, :, (qi // 4) * 512:(qi // 4 + 1) * 512], [b_q[qi // 4]], [bq])
                    tl["qb"] = qcur[0]
                    tl["ab"] = abr.next()
                    tl["cm"] = cmr.next()
                    tl["osb"] = osbr.next()
                    Sx.dma("sp", tl["ab"][0][:], c_tkAB[qi], [], [tl["ab"][1]])
                    Sx.dma("sp", tl["cm"][0][:], c_cmask[qi], [], [tl["cm"][1]])

                def tile_tail_a(tl):
                    osb, b_osb = tl["osb"]
                    Sx.op("act", lambda e: e.activation(out=obf[:], in_=osb[:], func=AF.Copy), [b_osb], [b_obf])

                def tile_tail_b_ops(tl):
                    qi = tl["qi"]
                    q0 = (qi % 4) * 128
                    ops = []
                    nop = lambda: None
                    for c2 in range(2):
                        R = {}

                        def pe_part(c2=c2, R=R):
                            R["px"], R["pbx"] = rotX.next()
                            pxv = R["px"][:].bitcast(BF16)
                            for c in range(4):
                                cc = c2 * 4 + c
                                Sx.op("pe", lambda e, c=c, cc=cc: e.transpose(out=pxv[:, c * 128:(c + 1) * 128], in_=obf[:, cc * 128:(cc + 1) * 128], identity=ident[:]),
                                      [b_obf, b_ident], [R["pbx"]])

                        def act_part(c2=c2, R=R):
                            pxv = R["px"][:].bitcast(BF16)
                            Sx.op("act", lambda e: e.activation(out=ynst[:, c2 * 4:(c2 + 1) * 4, q0:q0 + 128], in_=pxv[:, 0:512].rearrange("p (c q) -> p c q", c=4), func=AF.Copy),
                                  [R["pbx"]], [b_ynst])
                            if c2 == 1 and qi % 4 == 3:
                                Sx.dma("sp", ynT.rearrange("(c p) s -> p c s", p=128)[:, :, (qi // 4) * 512:(qi // 4 + 1) * 512], ynst[:], [b_ynst], [b_yn[qi // 4]])
                        ops += [pe_part, nop, nop, nop, act_part]
                    return ops

                def emit_S(st):
                    grp = st["grp"]
                    tl = grp["tile"]
                    if st["tile_first"]:
                        tile_pre(tl)
                    b, g, p, kc = grp["b"], grp["g"], grp["p"], st["kc"]
                    qb, bq = tl["qb"]
                    q0 = (tl["qi"] % 4) * 128
                    rhs_q = qb[:, 4 * g:4 * g + 4, q0:q0 + 128]
                    ps, pbs = rotS.next()
                    if b == 0:
                        lhs, blhs = kcm[:, g, p, kc * 128:(kc + 1) * 128], b_kcm
                    elif b == 1:
                        lhs, blhs = ks2[:, g, p, kc * 128:(kc + 1) * 128], b_ks2
                    else:
                        lhs, blhs = kw2[:, g, p, kc * 128:(kc + 1) * 128], b_kw2
                    if b != 1:
                        Sx.op("pe", lambda e: e.matmul(ps[:, 0:512], lhsT=lhs, rhs=rhs_q, start=True, stop=True), [blhs, bq], [pbs])
                    else:
                        if "selT" not in tl:
                            side_drain_all()
                        nT, bnT = tl["selT"]
                        qi = tl["qi"]
                        Sx.op("pe", lambda e: e.matmul(ps[:, 0:512], lhsT=lhs, rhs=rhs_q, start=True, stop=False), [blhs, bq], [pbs])
                        Sx.op("pe", lambda e: e.matmul(ps[:, 0:512], lhsT=Esb[:, kc * 128:(kc + 1) * 128], rhs=nT[:, g, :, :], start=False, stop=(kc != qi)), [b_E, bnT], [pbs])
                        if kc == qi:
                            Sx.op("pe", lambda e: e.matmul(ps[:, 0:512], lhsT=ident[:], rhs=negdiag4[:], start=False, stop=True), [b_ident, b_nd4], [pbs])
                    st["ps"], st["pbs"] = ps, pbs

                def emit_mid(st):
                    grp = st["grp"]
                    tl = grp["tile"]
                    qi = tl["qi"]
                    b, g, kc = grp["b"], grp["g"], st["kc"]
                    ps, pbs = st["ps"], st["pbs"]
                    pt, bpt = ptr.next()
                    Sx.op("act", lambda e: e.activation(out=pt[:], in_=ps[:, 0:512], func=AF.Exp, scale=0.125), [pbs], [bpt])
                    ptv = pt[:].rearrange("p (c q) -> p c q", c=4)
                    mk = None
                    if b == 0:
                        mk, bmk = tl["cm"][0][:, kc, :], tl["cm"][1]
                    elif b == 1:
                        mk = None
                    elif kc == qi:
                        mk, bmk = mdiag[:], b_md
                    elif kc == qi - 4:
                        mk, bmk = mfar[:], b_mf
                    if mk is not None:
                        Sx.op("dve", lambda e: e.tensor_tensor(out=ptv, in0=ptv, in1=mk.unsqueeze(1).to_broadcast([128, 4, 128]), op=ALU.mult), [bpt, bmk], [bpt])
                    st["pt"], st["bpt"] = pt, bpt

                def emit_PV(st):
                    grp = st["grp"]
                    b, g, kc = grp["b"], grp["g"], st["kc"]
                    pt, bpt = st["pt"], st["bpt"]
                    if st["first"]:
                        grp["acc"] = rotAcc.next()
                        side_drain_group(bank_owner.get(id(grp["acc"][1])))
                        bank_owner[id(grp["acc"][1])] = grp
                        if b == 0:
                            grp["accI"] = rotI.next()
                            side_drain_group(bank_owner.get(id(grp["accI"][1])))
                            bank_owner[id(grp["accI"][1])] = grp
                    acc, bacc = grp["acc"]
                    for c in range(4):
                        if b == 0:
                            accI, baccI = grp["accI"]
                            Sx.op("pe", lambda e, c=c: e.matmul(acc[:, c * 65:(c + 1) * 65], lhsT=pt[:, c * 128:(c + 1) * 128], rhs=vcm[:, kc, g, 0:65],
                                                                start=(st["first"] and c == 0), stop=False, skip_group_check=True), [bpt, b_vcm], [bacc])
                            Sx.op("pe", lambda e, c=c: e.matmul(accI[:, c * 64:(c + 1) * 64], lhsT=pt[:, c * 128:(c + 1) * 128], rhs=vcm[:, kc, g, 65:129],
                                                                start=(st["first"] and c == 0), stop=False, skip_group_check=True), [bpt, b_vcm], [baccI])
                        else:
                            vT, bvT = (vs1, b_vs1) if b == 1 else (vw1, b_vw1)
                            Sx.op("pe", lambda e, c=c: e.matmul(acc[:, c * 65:(c + 1) * 65], lhsT=pt[:, c * 128:(c + 1) * 128], rhs=vT[:, kc, g, :],
                                                                start=(st["first"] and c == 0), stop=False, skip_group_check=True), [bpt, bvT], [bacc])

                LA = 3
                EV_DELAY = 2
                qcur = [None]
                osbr = Ring(T, 3, [128, D], F32, "osb")
                selTr = Ring(T, 2, [128, 2, 4, 128], BF16, "negT4")
                for sT_ in selTr.t:
                    Sx.op("pool", lambda e, sT_=sT_: e.memset(sT_[:], 0.0), [], [selTr.b[selTr.t.index(sT_)]])
                b_smp = [Buf() for _ in range((NQ + 1) // 2)]
                steps = []
                tiles = [{"qi": qi} for qi in range(NQ)]

                def add_group(tl, b, g, p, first_of_tile=False, tile_last=False):
                    qi = tl["qi"]
                    if b == 0:
                        kcs = list(range(NCH))
                    elif b == 1:
                        kcs = list(range(qi + 1))
                    else:
                        kcs = list(range(max(0, qi - 4), qi + 1))
                    grp = {"b": b, "g": g, "p": p, "acc": None, "tile": tl, "tile_last": tile_last}
                    for ii, kc in enumerate(kcs):
                        steps.append({"grp": grp, "kc": kc, "first": ii == 0, "last": ii == len(kcs) - 1, "tile_first": first_of_tile and ii == 0})

                def add_sel(tl):
                    for g in range(2):
                        for p in range(2):
                            add_group(tl, 1, g, p, tile_last=(g == 1 and p == 1))

                for qi in range(NQ):
                    tl = tiles[qi]
                    for g in range(2):
                        for p in range(2):
                            add_group(tl, 0, g, p, first_of_tile=(g == 0 and p == 0))
                            add_group(tl, 2, g, p)
                    if qi >= 1:
                        add_sel(tiles[qi - 1])
                add_sel(tiles[NQ - 1])
                n = len(steps)
                SIDE_RATE = 2

                for i in range(min(LA, n)):
                    emit_S(steps[i])
                for i in range(n):
                    st = steps[i]
                    for st_ in steps[i:i + 2]:
                        f_ = st_.pop("sm_act", None)
                        if f_ is not None:
                            f_()
                    emit_mid(st)
                    side_pop(SIDE_RATE)
                    emit_PV(st)
                    if st["last"]:
                        grp = st["grp"]
                        side_push(evac_ops(grp), grp)
                        if grp["tile_last"]:
                            side_push([lambda tl=grp["tile"]: tile_tail_a(tl), lambda: None, lambda: None, lambda: None] + tile_tail_b_ops(grp["tile"]), None)
                    if i + LA < n:
                        emit_S(steps[i + LA])
                side_drain_all()
                Sx.barrier()

        def phase_D(l, xsrc, after_weights):
            with ExitStack() as es:
                T = mkT(es)
                rotP = Rot([0, 1, 2, 3])
                rotO = Rot([4, 5, 6, 7])
                wpp = T([128, 4, D], BF16, "wpp"); b_wpp = Buf()
                wpn = T([128, 8, D], BF16, "wpn"); b_wpn = Buf()
                wo = T([128, 8, D], BF16, "wo"); b_wo = Buf()
                for k in range(4):
                    Sx.dma("pool", wpp[:, k, :], w_pp[l, k * 128:(k + 1) * 128, :], [], [b_wpp])
                for k in range(8):
                    Sx.dma("pool", wpn[:, k, :], w_pn[l, k * 128:(k + 1) * 128, :], [], [b_wpn])
                    Sx.dma("pool", wo[:, k, :], w_out[l, k * 128:(k + 1) * 128, :], [], [b_wo])
                after_weights()
                ypt = T([128, 4, 512], BF16, "ypt"); b_ypt = Buf()
                ynt = T([128, 8, 512], BF16, "ynt"); b_ynt = Buf()
                gmt = T([128, 16, 512], BF16, "gmt"); b_gmt = Buf()
                xtr = Ring(T, 2, [128, 4, D], F32, "xtD")
                mg = T([128, 8, 512], BF16, "mg"); b_mg = Buf()
                t1r = Ring(T, 2, [128, 512], F32, "t1D")
                t2r = Ring(T, 2, [128, 512], F32, "t2D")

                def loadD(tt, which):
                    tsl_ = slice(tt * 512, tt * 512 + 512)
                    if which == 0:
                        Sx.dma("sp", ypt[:], ypT.rearrange("(c p) s -> p c s", p=128)[:, :, tsl_], [b_yp[tt]], [b_ypt])
                        Sx.dma("sp", ynt[:], ynT.rearrange("(c p) s -> p c s", p=128)[:, :, tsl_], [b_yn[tt]], [b_ynt])
                    elif which == 1:
                        Sx.dma("sp", gmt[:], gmT.rearrange("(c p) s -> p c s", p=128)[:, :, tsl_], [b_gm[tt]], [b_gmt])
                    else:
                        xt_, bxt_ = xtr.next()
                        Sx.dma("sp", xt_[:], xsrc[tsl_].rearrange("(j p) f -> p j f", p=128), xres_bufs(tt * 512, 512) if xsrc is xres else [], [bxt_])
                        return xt_, bxt_

                loadD(0, 0)
                loadD(0, 1)
                cur = loadD(0, 2)
                for tt in range(NT):
                    t0 = tt * 512
                    tsl = slice(t0, t0 + 512)
                    xt, b_xt = cur
                    for oc in range(8):
                        pa, pba = rotP.next()
                        for k in range(4):
                            Sx.op("pe", lambda e, k=k: e.matmul(pa[:, 0:512], lhsT=wpp[:, k, oc * 128:(oc + 1) * 128], rhs=ypt[:, k, :], start=(k == 0), stop=(k == 3)),
                                  [b_wpp, b_ypt], [pba])
                        pn, pbn = rotP.next()
                        for k in range(8):
                            Sx.op("pe", lambda e, k=k: e.matmul(pn[:, 0:512], lhsT=wpn[:, k, oc * 128:(oc + 1) * 128], rhs=ynt[:, k, :], start=(k == 0), stop=(k == 7)),
                                  [b_wpn, b_ynt], [pbn])
                        t1, bt1 = t1r.next()
                        t2, bt2 = t2r.next()
                        Sx.op("dve", lambda e: e.tensor_tensor(out=t1[:], in0=pa[:, 0:512], in1=gmt[:, oc, :], op=ALU.mult), [pba, b_gmt], [bt1])
                        Sx.op("dve", lambda e: e.tensor_tensor(out=t2[:], in0=pn[:, 0:512], in1=gmt[:, 8 + oc, :], op=ALU.mult), [pbn, b_gmt], [bt2])
                        Sx.op("dve", lambda e: e.tensor_tensor(out=mg[:, oc, :], in0=t1[:], in1=t2[:], op=ALU.add), [bt1, bt2], [b_mg])
                    if tt + 1 < NT:
                        loadD(tt + 1, 0)
                        loadD(tt + 1, 1)
                        cur = loadD(tt + 1, 2)
                    for j in range(4):
                        for hf in range(2):
                            po, pbo = rotO.next()
                            for k in range(8):
                                Sx.op("pe", lambda e, k=k: e.matmul(po[:, 0:512], lhsT=mg[:, k, j * 128:(j + 1) * 128], rhs=wo[:, k, hf * 512:(hf + 1) * 512], start=(k == 0), stop=(k == 7)),
                                      [b_mg, b_wo], [pbo])
                            Sx.op("dve", lambda e: e.tensor_tensor(out=xt[:, j, hf * 512:(hf + 1) * 512], in0=po[:, 0:512], in1=xt[:, j, hf * 512:(hf + 1) * 512], op=ALU.add),
                                  [pbo, b_xt], [b_xt])
                    Sx.dma("sp", xres[tsl].rearrange("(j p) f -> p j f", p=128), xt[:], [b_xt], xres_bufs(t0, 512))
                Sx.barrier()

        def phase_E(l, last, wf1, b_wf1):
            with ExitStack() as es:
                T = mkT(es)
                rotF = Rot([0, 1, 2, 3])
                rotO = Rot([4, 5, 6, 7])
                wf2 = T([128, 32, D], BF16, "wf2"); b_wf2 = Buf()
                for k8 in range(4):
                    Sx.dma("pool", wf2[:, k8 * 8:(k8 + 1) * 8, :], w_ff2[l, k8 * 1024:(k8 + 1) * 1024, :].rearrange("(k p) f -> p k f", p=128), [], [b_wf2])
                xtr = Ring(T, 2, [128, 2, D], F32, "xtE")
                hn = T([128, 2, D], BF16, "hnE"); b_hn = Buf()
                hT = T([128, 8, 256], BF16, "hTE"); b_hT = Buf()
                ss = T([128, 8], F32, "ssE"); b_ss = Buf()
                actT = T([128, 32, 256], BF16, "actT"); b_act = Buf()
                rlr = Ring(T, 3, [128, 256], F32, "rl")
                if last:
                    nf = T([128, D], F32, "nf"); b_nf = Buf()
                    Sx.dma("sp", nf[:], norm_final.partition_broadcast(128), [], [b_nf])
                    junk = T([128, D], BF16, "junk"); b_junk = Buf()
                    s2 = T([128, 8], F32, "ssF"); b_s2 = Buf()

                def prepE(tt):
                    xt, b_xt = xtr.next()
                    Sx.dma("sp", xt[:], xres[tt * 256:(tt + 1) * 256].rearrange("(j p) f -> p j f", p=128), xres_bufs(tt * 256, 256), [b_xt])
                    norm_stats(xt, b_xt, 2, hn, b_hn, ss, b_ss)
                    return xt, b_xt

                cur = prepE(0)
                for tt in range(NT2):
                    t0 = tt * 256
                    tsl = slice(t0, t0 + 256)
                    xt, b_xt = cur
                    if tt == 0:
                        norm_tr(2, gmlp[:, l, :], b_gmlp, hn, b_hn, hT, b_hT, rotF)
                    for fc in range(32):
                        pt, pb = rotF.next()
                        for k in range(8):
                            Sx.op("pe", lambda e, k=k: e.matmul(pt[:, 0:256], lhsT=wf1[:, k, fc * 128:(fc + 1) * 128], rhs=hT[:, k, :], start=(k == 0), stop=(k == 7)),
                                  [b_wf1, b_hT], [pb])
                        rl, brl = rlr.next()
                        Sx.op("act", lambda e: e.activation(out=rl[:], in_=pt[:, 0:256], func=AF.Relu), [pb], [brl])
                        Sx.op("dve", lambda e: e.tensor_tensor(out=actT[:, fc, :], in0=rl[:], in1=rl[:], op=ALU.mult), [brl], [b_act])
                        if fc == 6 and tt + 1 < NT2:
                            cur = prepE(tt + 1)
                    if tt + 1 < NT2:
                        norm_tr(2, gmlp[:, l, :], b_gmlp, hn, b_hn, hT, b_hT, rotF)
                    for j in range(2):
                        for hf in range(2):
                            po, pbo = rotO.next()
                            for k in range(32):
                                Sx.op("pe", lambda e, k=k: e.matmul(po[:, 0:512], lhsT=actT[:, k, j * 128:(j + 1) * 128], rhs=wf2[:, k, hf * 512:(hf + 1) * 512], start=(k == 0), stop=(k == 31)),
                                      [b_act, b_wf2], [pbo])
                            Sx.op("dve", lambda e: e.tensor_tensor(out=xt[:, j, hf * 512:(hf + 1) * 512], in0=po[:, 0:512], in1=xt[:, j, hf * 512:(hf + 1) * 512], op=ALU.add),
                                  [pbo, b_xt], [b_xt])
                    if not last:
                        Sx.dma("sp", xres[tsl].rearrange("(j p) f -> p j f", p=128), xt[:], [b_xt], xres_bufs(t0, 256))
                    else:
                        for j in range(2):
                            Sx.op("act", lambda e, j=j: e.activation(out=junk[:], in_=xt[:, j, :], func=AF.Square, accum_out=s2[:, j:j + 1]), [b_xt], [b_junk, b_s2])
                        Sx.op("act", lambda e: e.activation(out=s2[:, 4:6], in_=s2[:, 0:2], func=AF.Sqrt, scale=1.0 / D, bias=epst[:, 0:1]), [b_s2, b_eps], [b_s2])
                        Sx.op("dve", lambda e: e.reciprocal(out=s2[:, 4:6], in_=s2[:, 4:6]), [b_s2], [b_s2])
                        for j in range(2):
                            Sx.op("dve", lambda e, j=j: e.tensor_scalar(out=xt[:, j, :], in0=xt[:, j, :], scalar1=s2[:, 4 + j:5 + j], scalar2=None, op0=ALU.mult), [b_xt, b_s2], [b_xt])
                            Sx.op("pool", lambda e, j=j: e.tensor_tensor(out=xt[:, j, :], in0=xt[:, j, :], in1=nf[:], op=ALU.mult), [b_xt, b_nf], [b_xt])
                        Sx.dma("sp", out[tsl].rearrange("(j p) f -> p j f", p=128), xt[:], [b_xt], [])
                Sx.barrier()

        for l in range(depth):
            xsrc = x_in if l == 0 else xres
            phase_A(l, xsrc)
            if stop_after == "A":
                break
            Sx.new_epoch()
            phase_BC(l)
            if stop_after in ("B", "C"):
                break
            Sx.new_epoch()
            with ExitStack() as esw:
                wf1 = mkT(esw)([128, 8, 4096], BF16, "wf1"); b_wf1 = Buf()

                def load_wf1():
                    for k2 in range(4):
                        Sx.dma("pool", wf1[:, k2 * 2:(k2 + 1) * 2, :], w_ff1[l, k2 * 256:(k2 + 1) * 256, :].rearrange("(k p) f -> p k f", p=128), [], [b_wf1])
                phase_D(l, xsrc, load_wf1)
                if stop_after == "D":
                    break
                phase_E(l, l == depth - 1, wf1, b_wf1)
            if l < depth - 1:
                Sx.new_epoch()
        Sx.barrier()
    return nc


WKEYS = ["norm_mix", "w_in", "w_pool", "pool_scale", "pe_k", "pe_v", "w_ck1", "w_ck2", "w_cv1", "w_cv2",
         "w_proj_pool", "w_proj_nsa", "w_out", "norm_mlp", "w_ff1", "w_ff2", "norm_final"]


def kernel(**inputs):
    S, depth, B = 4096, 4, 8
    x = np.asarray(inputs["x"], dtype=np.float32)
    wts = {k: np.ascontiguousarray(np.asarray(inputs[k], dtype=np.float32)) for k in WKEYS}
    consts = host_consts(S)
    nc = build(S, depth)
    in_maps = []
    for b in range(B):
        m = dict(wts)
        m.update(consts)
        m["x"] = np.ascontiguousarray(x[b])
        in_maps.append(m)
    res = run_bass_kernel_spmd(nc, in_maps, core_ids=list(range(B)))
    return np.stack([np.asarray(r["out"], dtype=np.float32) for r in res.results], axis=0)
```

```python
import numpy as np
import collections
from contextlib import ExitStack
import ml_dtypes
import concourse.bass as bass
import concourse.mybir as mybir
from concourse.bass_utils import run_bass_kernel_spmd

F32 = mybir.dt.float32
BF16 = mybir.dt.bfloat16
ALU = mybir.AluOpType
AF = mybir.ActivationFunctionType
AX = mybir.AxisListType

NDS = 20
LIMIT = None
NDS_SW = 6


class Buf:
    __slots__ = ("w", "r", "excl")

    def __init__(self, excl=False):
        self.w = None
        self.r = {}
        self.excl = excl


class Sched:
    def __init__(self, nc, es, same_engine_sync=True):
        self.nc = nc
        self.es = es
        self.eng = {"pe": nc.tensor, "act": nc.scalar, "dve": nc.vector, "pool": nc.gpsimd, "sp": nc.sync}
        self.same = same_engine_sync
        self.gen = 0
        self.dsem = [es.enter_context(nc.semaphore(f"dq{i}")) for i in range(NDS)]
        self.dcnt = [0] * NDS
        self.dnext = {False: 0, True: 0}
        self.seen = {k: {} for k in self.eng}
        self.total = 0
        self.limit = None
        self.new_epoch()

    def new_epoch(self):
        self.sem = {k: self.es.enter_context(self.nc.semaphore(f"e{self.gen}_{k}")) for k in self.eng}
        self.cnt = {k: 0 for k in self.eng}
        self.gen += 1

    def _wait(self, e, ev):
        sem, val = ev
        if sem is self.sem[e] and (e == "pe" or not self.same):
            return
        if self.seen[e].get(sem, 0) >= val:
            return
        self.seen[e][sem] = val
        self.eng[e].wait_ge(sem, val)

    def _deps(self, e, reads, writes):
        for b in reads:
            if b.w is not None:
                self._wait(e, b.w)
        for b in writes:
            if b.w is not None:
                self._wait(e, b.w)
            for sem, val in b.r.items():
                self._wait(e, (sem, val))

    def _commit(self, ev, reads, writes):
        for b in reads:
            if b.r.get(ev[0], 0) < ev[1]:
                b.r[ev[0]] = ev[1]
        for b in writes:
            b.w = ev
            b.r = {}

    def op(self, e, fn, reads=(), writes=()):
        self.total += 1
        if self.limit is not None and self.total > self.limit:
            return
        if any(b.excl for b in reads):
            writes = list(writes) + [b for b in reads if b.excl]
            reads = [b for b in reads if not b.excl]
        self._deps(e, reads, writes)
        ins = fn(self.eng[e])
        self.cnt[e] += 1
        ins.then_inc(self.sem[e], 1)
        self._commit((self.sem[e], self.cnt[e]), reads, writes)

    def dma(self, e, out, in_, reads=(), writes=(), **kw):
        self.total += 1
        if self.limit is not None and self.total > self.limit:
            return
        lo, n = (0, NDS_SW) if e == "pool" else (NDS_SW, NDS - NDS_SW)
        k = lo + self.dnext[e == "pool"] % n
        self.dnext[e == "pool"] += 1
        if self.dcnt[k] > 0:
            self._wait(e, (self.dsem[k], 16 * self.dcnt[k]))
        self._deps(e, reads, writes)
        self.eng[e].dma_start(out=out, in_=in_, **kw).then_inc(self.dsem[k], 16)
        self.dcnt[k] += 1
        self._commit((self.dsem[k], 16 * self.dcnt[k]), reads, writes)

    def barrier(self):
        evs = [(self.sem[k], self.cnt[k]) for k in self.eng if self.cnt[k] > 0]
        evs += [(self.dsem[i], 16 * self.dcnt[i]) for i in range(NDS) if self.dcnt[i] > 0]
        for e in self.eng:
            for ev in evs:
                if ev[0] is self.sem[e]:
                    continue
                self._wait(e, ev)


D = 1024
NIN = 4400
C_Q = 512
C_KC, C_VC, C_KS, C_VS, C_KW, C_VW, C_GN, C_GM = 1536, 1664, 1792, 1920, 2048, 2176, 2304, 2352
POOLW = (2, 4, 8, 16)
GELU_C = 1.5957691216057308


def host_consts(S):
    NQ, NCP, NJ = S // 128, S // 16, S // 64
    NCH = NCP // 128
    bf = ml_dtypes.bfloat16
    c = {}
    c["ident"] = np.eye(128, dtype=np.float32).astype(bf)
    Rm = np.zeros((128, 128), np.float32)
    for hb in (0, 64):
        for d in range(32):
            Rm[hb + d + 32, hb + d] = -1.0
            Rm[hb + d, hb + d + 32] = 1.0
    c["Rm"] = Rm.astype(bf)
    inv = (10000.0 ** (-np.arange(0, 64, 2, dtype=np.float32) / 64)).astype(np.float32)
    inv64 = np.concatenate([inv, inv])
    inv128 = np.concatenate([inv64, inv64])
    pos = np.arange(S, dtype=np.float32)
    ang = (pos[None, :] * inv128[:, None]).astype(np.float32)
    c["cosT"] = np.cos(ang).astype(np.float32)
    c["sinT"] = np.sin(ang).astype(np.float32)
    cpos = (np.arange(NCP, dtype=np.float32) * 16 + 31)
    cang = (cpos[None, :] * inv128[:, None]).astype(np.float32)
    c["ccos"] = np.cos(cang).astype(np.float32)
    c["csin"] = np.sin(cang).astype(np.float32)
    key = np.arange(S)
    c["Emat"] = (key[None, :] // 64 == np.arange(64)[:, None]).astype(np.float32).astype(bf)
    r = np.arange(128)
    c["mdiag"] = (r[:, None] <= r[None, :]).astype(np.float32).astype(bf)
    c["mfar"] = (r[:, None] > r[None, :]).astype(np.float32).astype(bf)
    nd = np.where(r[:, None] > r[None, :], -30000.0, 0.0).astype(np.float32)
    c["negdiag4"] = np.repeat(nd[:, None, :], 4, axis=1).astype(bf)
    nfar = np.where(r[:, None] > r[None, :], 0.0, -30000.0).astype(np.float32)
    c["negfar4"] = np.repeat(nfar[:, None, :], 4, axis=1).astype(bf)
    n = np.arange(NCP)
    j = np.arange(64)
    ovl = ((16 * n[:, None] <= 64 * j[None, :] + 63) & (16 * n[:, None] + 31 >= 64 * j[None, :])).astype(np.float32)
    ovl[NCP - 1, :] = 0.0
    c["ovl"] = ovl.reshape(NCH, 128, 64).transpose(1, 0, 2).copy().astype(bf)
    t = np.arange(S)
    cm = (16 * n[:, None] + 31 <= t[None, :]).astype(np.float32)
    cm[NCP - 1, :] = 0.0
    ncm = ((cm - 1.0) * 30000.0).reshape(NCH, 128, NQ, 128).transpose(2, 1, 0, 3)
    c["cmask"] = np.repeat(ncm[:, :, :, None, :], 4, axis=3).copy().astype(bf)
    cur = t // 64
    causal = (64 * j[None, :] <= t[:, None])
    forced = causal & ((j[None, :] == 0) | (j[None, :] >= cur[:, None] - 1))
    A = (causal & ~forced).astype(np.float32)
    Bt = np.where(forced, 100.0 + j[None, :], np.where(causal, 0.0, -1.0 - j[None, :])).astype(np.float32)
    AB = np.stack([A, A, Bt, Bt], axis=1)
    c["tkAB"] = AB.reshape(NQ, 128, 4, 64).astype(np.float32)
    corr = np.ones((128, 4, 16), np.float32)
    for g, w in enumerate(POOLW):
        for tt in range(16):
            corr[:, g, tt] = w / min(tt + 1, w)
    c["pcorr"] = corr
    return c


def build(S=4096, depth=4, dbg=False, stop_after=None, same_sync=True):
    nc = bass.Bass("TRN2", target_bir_lowering=False)
    NT, NQ, NCP, NJ = S // 512, S // 128, S // 16, S // 64
    NCMP = NCP - 1
    NCH = NCP // 128
    NT2 = S // 256

    def din(name, shape, dt=F32):
        return nc.dram_tensor(name, shape, dt, kind="ExternalInput").ap()

    def dscr(name, shape, dt):
        return nc.dram_tensor(name, shape, dt, kind=("ExternalOutput" if dbg else "Internal")).ap()

    x_in = din("x", [S, D])
    norm_mix = din("norm_mix", [depth, D])
    w_in = din("w_in", [depth, D, NIN])
    w_pool = din("w_pool", [depth, 4, 128, 128])
    pool_scale = din("pool_scale", [depth, 512])
    pe_k = din("pe_k", [depth, 32, 64])
    pe_v = din("pe_v", [depth, 32, 64])
    w_ck1 = din("w_ck1", [depth, 2048, 256])
    w_ck2 = din("w_ck2", [depth, 256, 64])
    w_cv1 = din("w_cv1", [depth, 2048, 256])
    w_cv2 = din("w_cv2", [depth, 256, 64])
    w_pp = din("w_proj_pool", [depth, 512, D])
    w_pn = din("w_proj_nsa", [depth, D, D])
    w_out = din("w_out", [depth, D, D])
    norm_mlp = din("norm_mlp", [depth, D])
    w_ff1 = din("w_ff1", [depth, D, 4096])
    w_ff2 = din("w_ff2", [depth, 4096, D])
    norm_final = din("norm_final", [D])
    c_ident = din("ident", [128, 128], BF16)
    c_Rm = din("Rm", [128, 128], BF16)
    c_cos = din("cosT", [128, S])
    c_sin = din("sinT", [128, S])
    c_ccos = din("ccos", [128, NCP])
    c_csin = din("csin", [128, NCP])
    c_E = din("Emat", [64, S], BF16)
    c_mdiag = din("mdiag", [128, 128], BF16)
    c_mfar = din("mfar", [128, 128], BF16)
    c_negdiag4 = din("negdiag4", [128, 4, 128], BF16)
    c_negfar4 = din("negfar4", [128, 4, 128], BF16)
    c_ovl = din("ovl", [128, NCH, 64], BF16)
    c_cmask = din("cmask", [NQ, 128, NCH, 4, 128], BF16)
    c_tkAB = din("tkAB", [NQ, 128, 4, 64])
    c_pcorr = din("pcorr", [128, 4, 16])
    out = nc.dram_tensor("out", [S, D], F32, kind="ExternalOutput").ap()

    xres = dscr("xres", [S, D], F32)
    qT = dscr("qT", [D, S], BF16)
    kk = dscr("kk", [4, 128, S], BF16)
    kcvc = dscr("kcvc", [2, 128, S], BF16)
    vtok = dscr("vtok", [2, S, 130], BF16)
    gatesd = dscr("gatesd", [S, 48], F32)
    gmT = dscr("gmT", [2048, S], BF16)
    ypT = dscr("ypT", [512, S], BF16)
    ynT = dscr("ynT", [D, S], BF16)
    dbg_imp = dscr("dbg_imp", [S, 2, 64], F32) if dbg else None
    dbg_sel = dscr("dbg_sel", [S, 2, 64], F32) if dbg else None
    dbg_kcm = dscr("dbg_kcm", [128, 2, NCP], F32) if dbg else None
    dbg_vcm = dscr("dbg_vcm", [128, NCH, 2, 129], F32) if dbg else None

    top = ExitStack()
    with top:
        Sx = Sched(nc, top, same_engine_sync=same_sync)
        Sx.limit = LIMIT

        gcnt = [0]

        def mkT(es):
            cnt = gcnt

            def T(shape, dt, name=None):
                cnt[0] += 1
                nm = f"{name or 't'}_{cnt[0]}"
                return es.enter_context(nc.sbuf_tensor(nm, shape, dt))
            return T

        T0 = mkT(top)
        pbank = [top.enter_context(nc.psum_tensor(f"pb{i}", [128, 512], F32)) for i in range(8)]
        pbuf = [Buf(True) for _ in range(8)]

        class Rot:
            def __init__(self, idxs):
                self.idxs = list(idxs)
                self.i = 0

            def next(self):
                k = self.idxs[self.i % len(self.idxs)]
                self.i += 1
                return pbank[k], pbuf[k]

        class Ring:
            def __init__(self, T, n, shape, dt, name):
                self.t = [T(shape, dt, name) for _ in range(n)]
                self.b = [Buf() for _ in range(n)]
                self.i = 0

            def next(self):
                k = self.i % len(self.t)
                self.i += 1
                return self.t[k], self.b[k]

        ident = T0([128, 128], BF16, "ident"); b_ident = Buf()
        Rm = T0([128, 128], BF16, "Rm"); b_Rm = Buf()
        mdiag = T0([128, 128], BF16, "mdiag"); b_md = Buf()
        mfar = T0([128, 128], BF16, "mfar"); b_mf = Buf()
        epst = T0([128, 1], F32, "eps"); b_eps = Buf()
        gmix = T0([128, depth, 8], F32, "gmix"); b_gmix = Buf()
        gmlp = T0([128, depth, 8], F32, "gmlp"); b_gmlp = Buf()
        pscl = T0([128, depth, 4], F32, "pscl"); b_pscl = Buf()
        Sx.dma("sp", ident[:], c_ident, [], [b_ident])
        Sx.dma("sp", Rm[:], c_Rm, [], [b_Rm])
        Sx.dma("sp", mdiag[:], c_mdiag, [], [b_md])
        Sx.dma("sp", mfar[:], c_mfar, [], [b_mf])
        negdiag4 = T0([128, 4, 128], BF16, "negdiag4"); b_nd4 = Buf()
        Sx.dma("sp", negdiag4[:], c_negdiag4, [], [b_nd4])
        negfar4 = T0([128, 4, 128], BF16, "negfar4"); b_nf4 = Buf()
        Sx.dma("sp", negfar4[:], c_negfar4, [], [b_nf4])
        Sx.op("pool", lambda e: e.memset(epst[:], 1e-6), [], [b_eps])
        for l in range(depth):
            Sx.dma("sp", gmix[:, l, :], norm_mix[l].rearrange("(c p) -> p c", p=128), [], [b_gmix], allow_slow_non_contiguous=True)
            Sx.dma("sp", gmlp[:, l, :], norm_mlp[l].rearrange("(c p) -> p c", p=128), [], [b_gmlp], allow_slow_non_contiguous=True)
            Sx.dma("sp", pscl[:, l, :], pool_scale[l].rearrange("(c p) -> p c", p=128), [], [b_pscl], allow_slow_non_contiguous=True)

        b_xres = [Buf() for _ in range(NT2)]
        b_q = [Buf() for _ in range(NT)]
        b_kk = [Buf() for _ in range(NT)]
        b_kcvc = [Buf() for _ in range(NT)]
        b_vtok = [Buf() for _ in range(NT)]
        b_gates = [Buf() for _ in range(NT)]
        b_gm = [Buf() for _ in range(NT)]
        b_yp = [Buf() for _ in range(NT)]
        b_yn = [Buf() for _ in range(NT)]

        def xres_bufs(t0, n):
            return b_xres[t0 // 256:(t0 + n) // 256]

        def norm_stats(xt, bxt, ntj, hn, bhn, ss, bss):
            for j in range(ntj):
                Sx.op("act", lambda e, j=j: e.activation(out=hn[:, j, :], in_=xt[:, j, :], func=AF.Square, accum_out=ss[:, j:j + 1]),
                      [bxt], [bhn, bss])
            Sx.op("act", lambda e: e.activation(out=ss[:, 4:4 + ntj], in_=ss[:, 0:ntj], func=AF.Sqrt, scale=1.0 / D, bias=epst[:, 0:1]),
                  [bss, b_eps], [bss])
            Sx.op("dve", lambda e: e.reciprocal(out=ss[:, 4:4 + ntj], in_=ss[:, 4:4 + ntj]), [bss], [bss])
            for j in range(ntj):
                Sx.op("dve", lambda e, j=j: e.tensor_scalar(out=hn[:, j, :], in0=xt[:, j, :], scalar1=ss[:, 4 + j:5 + j], scalar2=None, op0=ALU.mult),
                      [bxt, bss], [bhn])

        def norm_tr(ntj, gcol, bg, hn, bhn, hT, bhT, rot):
            for c in range(8):
                pt, pb = rot.next()
                pv = pt[:].bitcast(BF16)
                for j in range(ntj):
                    Sx.op("pe", lambda e, j=j, c=c, pv=pv: e.transpose(out=pv[:, j * 128:(j + 1) * 128], in_=hn[:, j, c * 128:(c + 1) * 128], identity=ident[:]),
                          [bhn, b_ident], [pb])
                Sx.op("dve", lambda e, c=c, pv=pv: e.tensor_scalar(out=hT[:, c, :], in0=pv[:, 0:ntj * 128], scalar1=gcol[:, c:c + 1], scalar2=None, op0=ALU.mult),
                      [pb, bg], [bhT])

        def phase_A(l, xsrc):
            with ExitStack() as es:
                T = mkT(es)
                rotA = Rot([0, 1, 2, 3])
                rotR = Rot([4, 5])
                rotM = Rot([6, 7])
                wA = T([128, 8, NIN], BF16, "wA"); b_w = [Buf() for _ in range(8)]
                for k in range(8):
                    Sx.dma("pool", wA[:, k, :], w_in[l, k * 128:(k + 1) * 128, :], [], [b_w[k]])
                wdup = T([128, 8, 4, 128], BF16, "wdup"); b_wdup = Buf()
                for i, cb in enumerate((C_KS, C_KS + 64, C_KW, C_KW + 64)):
                    Sx.op("pool", lambda e, i=i, cb=cb: e.tensor_copy(
                        out=wdup[:, :, i, :].rearrange("p k (two d) -> p k two d", two=2),
                        in_=wA[:, :, cb:cb + 64].unsqueeze(2).to_broadcast([128, 8, 2, 64])), b_w, [b_wdup])
                wpl = T([128, 4, 128], BF16, "wpl"); b_wpl = Buf()
                Sx.dma("pool", wpl[:], w_pool[l].rearrange("g c d -> c g d"), [], [b_wpl])
                pcorr = T([128, 4, 16], F32, "pcorr"); b_pc = Buf()
                Sx.dma("sp", pcorr[:], c_pcorr, [], [b_pc])
                xt = T([128, 4, D], F32, "xt"); b_xt = Buf()
                hn = T([128, 4, D], BF16, "hn"); b_hn = Buf()
                hT = T([128, 8, 512], BF16, "hT"); b_hT = Buf()
                ss = T([128, 8], F32, "ss"); b_ss = Buf()
                csr = Ring(T, 2, [128, 2, 512], F32, "cs")
                ubuf = T([128, 4, 528], F32, "ubuf"); b_u = [Buf() for _ in range(4)]
                ptmp = Ring(T, 2, [128, 528], F32, "ptmp")
                dTr = Ring(T, 4, [128, 512], BF16, "dT")
                xbr = Ring(T, 2, [128, 512], BF16, "xb")
                t1r = Ring(T, 2, [128, 512], F32, "t1")
                t2r = Ring(T, 2, [128, 512], F32, "t2")
                qst = T([128, 8, 512], BF16, "qst"); b_qst = Buf()
                kst = T([128, 4, 512], BF16, "kst"); b_kst = Buf()
                cst = T([128, 2, 512], BF16, "cst"); b_cst = Buf()
                gst = T([128, 16, 512], BF16, "gst"); b_gst = Buf()
                yst = T([128, 4, 512], BF16, "yst"); b_yst = Buf()
                vst = T([128, 4, 2, 130], BF16, "vst"); b_vst = Buf()
                gts = T([128, 4, 48], F32, "gts"); b_gts = Buf()
                Sx.op("pool", lambda e: e.memset(vst[:], 1.0), [], [b_vst])
                Sx.op("pool", lambda e: e.memset(ubuf[:], 0.0), [], b_u)

                def rope_to(pt, pb, dest, bdest, cs, b_cs):
                    xb, bxb = xbr.next()
                    Sx.op("act", lambda e: e.activation(out=xb[:], in_=pt[:, 0:512], func=AF.Copy), [pb], [bxb])

                    def part2():
                        p2, pb2 = rotR.next()
                        Sx.op("pe", lambda e: e.matmul(p2[:, 0:512], lhsT=Rm[:], rhs=xb[:], start=True, stop=True), [bxb, b_Rm], [pb2])
                        t1, bt1 = t1r.next()
                        t2, bt2 = t2r.next()
                        Sx.op("dve", lambda e: e.tensor_tensor(out=t1[:], in0=pt[:, 0:512], in1=cs[:, 0, :], op=ALU.mult), [pb, b_cs], [bt1])
                        Sx.op("dve", lambda e: e.tensor_tensor(out=t2[:], in0=p2[:, 0:512], in1=cs[:, 1, :], op=ALU.mult), [pb2, b_cs], [bt2])
                        Sx.op("pool", lambda e: e.tensor_tensor(out=dest, in0=t1[:], in1=t2[:], op=ALU.add), [bt1, bt2], [bdest])
                    return part2

                def prepA(tt):
                    tsl_ = slice(tt * 512, tt * 512 + 512)
                    Sx.dma("sp", xt[:], xsrc[tsl_].rearrange("(j p) f -> p j f", p=128), xres_bufs(tt * 512, 512) if xsrc is xres else [], [b_xt])
                    norm_stats(xt, b_xt, 4, hn, b_hn, ss, b_ss)

                prepA(0)
                for tt in range(NT):
                    t0 = tt * 512
                    tsl = slice(t0, t0 + 512)
                    cs, b_cs = csr.next()
                    Sx.dma("sp", cs[:, 0, :], c_cos[:, tsl], [], [b_cs])
                    Sx.dma("sp", cs[:, 1, :], c_sin[:, tsl], [], [b_cs])
                    norm_tr(4, gmix[:, l, :], b_gmix, hn, b_hn, hT, b_hT, rotM)

                    def fm_chunk(lhs_fn, wbufs):
                        pt, pb = rotA.next()
                        for k in range(8):
                            Sx.op("pe", lambda e, k=k: e.matmul(pt[:, 0:512], lhsT=lhs_fn(k), rhs=hT[:, k, :], start=(k == 0), stop=(k == 7)),
                                  [b_hT] + wbufs, [pb])
                        return pt, pb

                    for g in range(4):
                        pt, pb = fm_chunk(lambda k, g=g: wA[:, k, g * 128:(g + 1) * 128], b_w)
                        Sx.op("act", lambda e, g=g, pt=pt: e.activation(out=ubuf[:, g, 16:528], in_=pt[:, 0:512], func=AF.Copy), [pb], [b_u[g]])
                    rp = None
                    for c in range(8):
                        pt, pb = fm_chunk(lambda k, c=c: wA[:, k, C_Q + c * 128:C_Q + (c + 1) * 128], b_w)
                        if rp is not None:
                            rp()
                        rp = rope_to(pt, pb, qst[:, c, :], b_qst, cs, b_cs)
                    if tt + 1 < NT:
                        prepA(tt + 1)
                    for i in range(4):
                        pt, pb = fm_chunk(lambda k, i=i: wdup[:, k, i, :], [b_wdup])
                        rp()
                        rp = rope_to(pt, pb, kst[:, i, :], b_kst, cs, b_cs)
                    for i, cb in enumerate((C_KC, C_VC)):
                        pt, pb = fm_chunk(lambda k, cb=cb: wA[:, k, cb:cb + 128], b_w)
                        if rp is not None:
                            rp()
                            rp = None
                        Sx.op("act", lambda e, i=i, pt=pt: e.activation(out=cst[:, i, :], in_=pt[:, 0:512], func=AF.Copy), [pb], [b_cst])
                    pool_mms = []
                    for g in range(4):
                        cur = ubuf[:, g, :]
                        curb = b_u[g]
                        sh = 1
                        v0 = 0
                        for step in range(g + 1):
                            nx, bnx = ptmp.next()
                            Sx.op("pool", lambda e, cur=cur, nx=nx, sh=sh, v0=v0: e.tensor_tensor(out=nx[:, v0 + sh:528], in0=cur[:, v0 + sh:528], in1=cur[:, v0:528 - sh], op=ALU.add),
                                  [curb], [bnx])
                            cur, curb = nx, bnx
                            v0 += sh
                            sh *= 2
                        if tt == 0:
                            Sx.op("pool", lambda e, cur=cur, g=g: e.tensor_tensor(out=cur[:, 16:32], in0=cur[:, 16:32], in1=pcorr[:, g, :], op=ALU.mult),
                                  [curb, b_pc], [curb])
                        dT, bdT = dTr.next()
                        Sx.op("dve", lambda e, cur=cur, g=g, dT=dT: e.scalar_tensor_tensor(out=dT[:], in0=cur[:, 16:528], scalar=1.0 / POOLW[g], in1=ubuf[:, g, 16:528],
                                                                                         op0=ALU.mult, op1=ALU.subtract), [curb, b_u[g]], [bdT])
                        def pool_mm(g=g, dT=dT, bdT=bdT):
                            pt, pb = rotA.next()
                            Sx.op("pe", lambda e: e.matmul(pt[:, 0:512], lhsT=wpl[:, g, :], rhs=dT[:], start=True, stop=True), [bdT, b_wpl], [pb])
                            Sx.op("act", lambda e: e.activation(out=yst[:, g, :], in_=pt[:, 0:512], func=AF.Copy, scale=pscl[:, l, g:g + 1]), [pb, b_pscl], [b_yst])
                        pool_mms.append(pool_mm)
                        Sx.op("pool", lambda e, g=g: e.tensor_copy(out=ubuf[:, g, 0:16], in_=ubuf[:, g, 512:528]), [b_u[g]], [b_u[g]])
                    for c in range(16):
                        pt, pb = fm_chunk(lambda k, c=c: wA[:, k, C_GM + c * 128:C_GM + (c + 1) * 128], b_w)
                        Sx.op("act", lambda e, c=c, pt=pt: e.activation(out=gst[:, c, :], in_=pt[:, 0:512], func=AF.Sigmoid), [pb], [b_gst])
                        if c >= 8 and c % 2 == 0 and pool_mms:
                            pool_mms.pop(0)()
                    while pool_mms:
                        pool_mms.pop(0)()
                    for j in range(4):
                        pt, pb = rotA.next()
                        for (cb, wdt, o0) in ((C_VS, 128, 0), (C_VW, 128, 128), (C_GN, 48, 256)):
                            for k in range(8):
                                Sx.op("pe", lambda e, k=k, j=j, cb=cb, wdt=wdt, o0=o0, pt=pt: e.matmul(pt[:, o0:o0 + wdt], lhsT=hT[:, k, j * 128:(j + 1) * 128],
                                                                                                     rhs=wA[:, k, cb:cb + wdt], start=(k == 0), stop=(k == 7)),
                                      [b_hT] + b_w, [pb])
                        Sx.op("act", lambda e, j=j, pt=pt: e.activation(
                            out=vst[:, j, :, :].rearrange("p b (g c) -> p b g c", g=2)[:, :, :, 0:64],
                            in_=pt[:, 0:256].rearrange("p (b g d) -> p b g d", b=2, g=2), func=AF.Copy), [pb], [b_vst])
                        Sx.op("act", lambda e, j=j, pt=pt: e.activation(out=gts[:, j, :], in_=pt[:, 256:304], func=AF.Sigmoid), [pb], [b_gts])
                    Sx.dma("sp", qT.rearrange("(c p) s -> p c s", p=128)[:, :, tsl], qst[:], [b_qst], [b_q[tt]])
                    Sx.dma("sp", kk.rearrange("i p s -> p i s")[:, :, tsl], kst[:], [b_kst], [b_kk[tt]])
                    Sx.dma("sp", kcvc.rearrange("i p s -> p i s")[:, :, tsl], cst[:], [b_cst], [b_kcvc[tt]])
                    Sx.dma("sp", gmT.rearrange("(c p) s -> p c s", p=128)[:, :, tsl], gst[:], [b_gst], [b_gm[tt]])
                    Sx.dma("sp", ypT.rearrange("(c p) s -> p c s", p=128)[:, :, tsl], yst[:], [b_yst], [b_yp[tt]])
                    for bb in range(2):
                        Sx.dma("sp", vtok[bb].rearrange("(n p) c -> p n c", p=128)[:, tt * 4:(tt + 1) * 4], vst[:, :, bb, :], [b_vst], [b_vtok[tt]])
                    Sx.dma("sp", gatesd.rearrange("(n p) c -> p n c", p=128)[:, tt * 4:(tt + 1) * 4], gts[:], [b_gts], [b_gates[tt]])
                Sx.barrier()

        def phase_BC(l):
            with ExitStack() as es:
                T = mkT(es)
                kcm = T([128, 2, 2, NCP], BF16, "kcm"); b_kcm = Buf()
                vcm = T([128, NCH, 2, 129], BF16, "vcm"); b_vcm = Buf()
                ovl = T([128, NCH, 64], BF16, "ovl"); b_ovl = Buf()
                Sx.dma("sp", ovl[:], c_ovl, [], [b_ovl])
                Sx.op("pool", lambda e: e.memset(kcm[:], 0.0), [], [b_kcm])
                Sx.op("pool", lambda e: e.memset(vcm[:], 0.0), [], [b_vcm])
                with ExitStack() as esb:
                    Tb = mkT(esb)
                    rotB = Rot([0, 1, 2, 3])
                    kv = Tb([128, 2, S], BF16, "kv"); b_kv = Buf()
                    for i in range(2):
                        Sx.dma("sp", kv[:, i, :], kcvc[i], b_kcvc, [b_kv])
                    w1 = Tb([128, 2, 32, 256], BF16, "w1"); b_w1 = Buf()
                    for i, wsrc in enumerate((w_ck1, w_cv1)):
                        for hh in range(2):
                            for lq in range(4):
                                Sx.dma("pool", w1[hh * 64:(hh + 1) * 64, i, lq * 8:(lq + 1) * 8, :],
                                       wsrc[l, lq * 512:(lq + 1) * 512, :].rearrange("(l d) h -> d l h", d=64), [], [b_w1])
                    w2k = Tb([128, 2, 128], BF16, "w2k"); b_w2k = Buf()
                    w2v = Tb([128, 2, 64], BF16, "w2v"); b_w2v = Buf()
                    for hh in range(2):
                        Sx.dma("pool", w2k[:, :, hh * 64:(hh + 1) * 64], w_ck2[l].rearrange("(c p) d -> p c d", p=128), [], [b_w2k])
                    Sx.dma("pool", w2v[:], w_cv2[l].rearrange("(c p) d -> p c d", p=128), [], [b_w2v])
                    pef = Tb([64, 2, 32], F32, "pef"); b_pef = Buf()
                    peb = Tb([64, 2, 32], BF16, "peb"); b_peb = Buf()
                    Sx.dma("sp", pef[:, 0, :], pe_k[l].rearrange("l d -> d l"), [], [b_pef], allow_slow_non_contiguous=True)
                    Sx.dma("sp", pef[:, 1, :], pe_v[l].rearrange("l d -> d l"), [], [b_pef], allow_slow_non_contiguous=True)
                    Sx.op("act", lambda e: e.activation(out=peb[:], in_=pef[:], func=AF.Copy), [b_pef], [b_peb])
                    ccs = Tb([128, 2, NCP], F32, "ccs"); b_ccs = Buf()
                    Sx.dma("sp", ccs[:, 0, :], c_ccos, [], [b_ccs])
                    Sx.dma("sp", ccs[:, 1, :], c_csin, [], [b_ccs])
                    bias = Tb([128, 2, 2], F32, "bias"); b_bias = Buf()
                    for i in range(2):
                        for hc in range(2):
                            pt, pb = rotB.next()
                            for lq in range(32):
                                Sx.op("pe", lambda e, i=i, hc=hc, lq=lq, pt=pt: e.matmul(pt[:, 0:1], lhsT=w1[0:64, i, lq, hc * 128:(hc + 1) * 128], rhs=peb[:, i, lq:lq + 1],
                                                                                       start=(lq == 0), stop=(lq == 31)), [b_w1, b_peb], [pb])
                            Sx.op("act", lambda e, i=i, hc=hc, pt=pt: e.activation(out=bias[:, i, hc:hc + 1], in_=pt[:, 0:1], func=AF.Copy), [pb], [b_bias])
                    hT2 = Tb([128, 2, 2, 2, NCP], BF16, "hT2"); b_h2 = Buf()
                    xh = Ring(Tb, 2, [128, NCP], F32, "xh")
                    x2 = Ring(Tb, 2, [128, NCP], F32, "x2")
                    sg = Ring(Tb, 2, [128, NCP], F32, "sg")
                    for i in range(2):
                        for hc in range(2):
                            for g in range(2):
                                pt, pb = rotB.next()
                                for lq in range(32):
                                    Sx.op("pe", lambda e, i=i, hc=hc, g=g, lq=lq, pt=pt: e.matmul(
                                        pt[:, 0:NCMP], lhsT=w1[g * 64:(g + 1) * 64, i, lq, hc * 128:(hc + 1) * 128],
                                        rhs=kv[g * 64:(g + 1) * 64, i, lq:lq + 16 * (NCMP - 1) + 1:16], start=(lq == 0), stop=(lq == 31)), [b_w1, b_kv], [pb])
                                a, ba = xh.next()
                                b2, bb2 = x2.next()
                                c2, bc2 = sg.next()
                                N = NCMP
                                Sx.op("act", lambda e, a=a, pt=pt, i=i, hc=hc: e.activation(out=a[:, 0:N], in_=pt[:, 0:N], func=AF.Identity, bias=bias[:, i, hc:hc + 1]), [pb, b_bias], [ba])
                                Sx.op("dve", lambda e, a=a, b2=b2: e.tensor_tensor(out=b2[:, 0:N], in0=a[:, 0:N], in1=a[:, 0:N], op=ALU.mult), [ba], [bb2])
                                Sx.op("dve", lambda e, b2=b2: e.tensor_scalar(out=b2[:, 0:N], in0=b2[:, 0:N], scalar1=0.044715, scalar2=1.0, op0=ALU.mult, op1=ALU.add), [bb2], [bb2])
                                Sx.op("dve", lambda e, a=a, b2=b2: e.tensor_tensor(out=b2[:, 0:N], in0=b2[:, 0:N], in1=a[:, 0:N], op=ALU.mult), [bb2, ba], [bb2])
                                Sx.op("act", lambda e, b2=b2, c2=c2: e.activation(out=c2[:, 0:N], in_=b2[:, 0:N], func=AF.Sigmoid, scale=GELU_C), [bb2], [bc2])
                                Sx.op("dve", lambda e, a=a, c2=c2, i=i, hc=hc, g=g: e.tensor_tensor(out=hT2[:, i, hc, g, 0:N], in0=a[:, 0:N], in1=c2[:, 0:N], op=ALU.mult), [ba, bc2], [b_h2])
                    xbk = Tb([128, NCP], BF16, "xbk"); b_xbk = Buf()
                    tk1 = Tb([128, NCP], F32, "tk1"); b_tk1 = Buf()
                    tk2 = Tb([128, NCP], F32, "tk2"); b_tk2 = Buf()
                    for g in range(2):
                        pt, pb = rotB.next()
                        for hc in range(2):
                            Sx.op("pe", lambda e, g=g, hc=hc, pt=pt: e.matmul(pt[:, 0:NCMP], lhsT=w2k[:, hc, :], rhs=hT2[:, 0, hc, g, 0:NCMP], start=(hc == 0), stop=(hc == 1)),
                                  [b_w2k, b_h2], [pb])
                        Sx.op("act", lambda e, pt=pt: e.activation(out=xbk[:, 0:NCMP], in_=pt[:, 0:NCMP], func=AF.Copy), [pb], [b_xbk])
                        p2, pb2 = rotB.next()
                        Sx.op("pe", lambda e, p2=p2: e.matmul(p2[:, 0:NCMP], lhsT=Rm[:], rhs=xbk[:, 0:NCMP], start=True, stop=True), [b_xbk, b_Rm], [pb2])
                        Sx.op("dve", lambda e, pt=pt: e.tensor_tensor(out=tk1[:, 0:NCMP], in0=pt[:, 0:NCMP], in1=ccs[:, 0, 0:NCMP], op=ALU.mult), [pb, b_ccs], [b_tk1])
                        Sx.op("dve", lambda e, p2=p2: e.tensor_tensor(out=tk2[:, 0:NCMP], in0=p2[:, 0:NCMP], in1=ccs[:, 1, 0:NCMP], op=ALU.mult), [pb2, b_ccs], [b_tk2])
                        for hp in range(2):
                            Sx.op("dve", lambda e, g=g, hp=hp: e.tensor_tensor(out=kcm[hp * 64:(hp + 1) * 64, g, hp, 0:NCMP], in0=tk1[hp * 64:(hp + 1) * 64, 0:NCMP],
                                                                             in1=tk2[hp * 64:(hp + 1) * 64, 0:NCMP], op=ALU.add), [b_tk1, b_tk2], [b_kcm])
                    for nch in range(NCH):
                        nn = min(128, NCMP - nch * 128)
                        for g in range(2):
                            pt, pb = rotB.next()
                            for hc in range(2):
                                Sx.op("pe", lambda e, g=g, hc=hc, nch=nch, nn=nn, pt=pt: e.matmul(pt[0:nn, 0:64], lhsT=hT2[:, 1, hc, g, nch * 128:nch * 128 + nn], rhs=w2v[:, hc, :],
                                                                                                start=(hc == 0), stop=(hc == 1)), [b_w2v, b_h2], [pb])
                            Sx.op("act", lambda e, g=g, nch=nch, nn=nn, pt=pt: e.activation(out=vcm[0:nn, nch, g, 0:64], in_=pt[0:nn, 0:64], func=AF.Copy), [pb], [b_vcm])
                        for g in range(2):
                            Sx.op("pool", lambda e, g=g, nch=nch, nn=nn: e.memset(vcm[0:nn, nch, g, 64:65], 1.0), [], [b_vcm])
                            Sx.op("pool", lambda e, g=g, nch=nch: e.tensor_copy(out=vcm[:, nch, g, 65:129], in_=ovl[:, nch, :]), [b_ovl], [b_vcm])
                    if dbg:
                        dk = Tb([128, 2, NCP], F32, "dk"); b_dk = Buf()
                        dv = Tb([128, NCH, 2, 129], F32, "dv"); b_dv = Buf()
                        Sx.op("dve", lambda e: e.tensor_tensor(out=dk[:], in0=kcm[:, :, 0, :], in1=kcm[:, :, 1, :], op=ALU.add), [b_kcm], [b_dk])
                        Sx.op("dve", lambda e: e.tensor_copy(out=dv[:], in_=vcm[:]), [b_vcm], [b_dv])
                        Sx.dma("sp", dbg_kcm, dk[:], [b_dk], [])
                        Sx.dma("sp", dbg_vcm, dv[:], [b_dv], [])
                    Sx.barrier()
                if stop_after == "B":
                    return
                rotS = Rot([0, 1, 2, 3])
                rotAcc = Rot([4, 5])
                rotI = Rot([6])
                rotX = Rot([7])
                ks2 = T([128, 2, 2, S], BF16, "ks2"); b_ks2 = Buf()
                kw2 = T([128, 2, 2, S], BF16, "kw2"); b_kw2 = Buf()
                for hp in range(2):
                    oh = 1 - hp
                    Sx.op("pool", lambda e: e.memset(ks2[oh * 64:(oh + 1) * 64, :, hp, :], 0.0), [], [b_ks2])
                    Sx.op("pool", lambda e: e.memset(kw2[oh * 64:(oh + 1) * 64, :, hp, :], 0.0), [], [b_kw2])
                for g in range(2):
                    for hp in range(2):
                        Sx.dma("sp", ks2[hp * 64:(hp + 1) * 64, g, hp, :], kk[g, hp * 64:(hp + 1) * 64, :], b_kk, [b_ks2])
                        Sx.dma("sp", kw2[hp * 64:(hp + 1) * 64, g, hp, :], kk[2 + g, hp * 64:(hp + 1) * 64, :], b_kk, [b_kw2])
                vs1 = T([128, NQ, 2, 65], BF16, "vs1"); b_vs1 = Buf()
                vw1 = T([128, NQ, 2, 65], BF16, "vw1"); b_vw1 = Buf()
                for n0 in range(0, NQ, 8):
                    Sx.dma("sp", vs1[:, n0:n0 + 8].rearrange("p n g c -> p n (g c)"), vtok[0].rearrange("(n p) c -> p n c", p=128)[:, n0:n0 + 8], b_vtok, [b_vs1])
                    Sx.dma("sp", vw1[:, n0:n0 + 8].rearrange("p n g c -> p n (g c)"), vtok[1].rearrange("(n p) c -> p n c", p=128)[:, n0:n0 + 8], b_vtok, [b_vw1])
                gat = T([128, NQ, 48], F32, "gat"); b_gat = Buf()
                Sx.dma("sp", gat[:], gatesd.rearrange("(n p) c -> p n c", p=128), b_gates, [b_gat])
                Esb = T([128, S], BF16, "Esb"); b_E = Buf()
                Sx.op("pool", lambda e: e.memset(Esb[64:128, :], 0.0), [], [b_E])
                Sx.dma("sp", Esb[0:64, :], c_E, [], [b_E])
                qbr = Ring(T, 2, [128, 8, 512], BF16, "qb")
                ynst = T([128, 8, 512], BF16, "ynst"); b_ynst = Buf()
                ptr = Ring(T, 4, [128, 512], BF16, "pt")
                obf = T([128, D], BF16, "obf"); b_obf = Buf()
                tmp4r = Ring(T, 2, [128, 4, 64], F32, "tmp4")
                tmpIr = Ring(T, 2, [128, 4, 64], F32, "tmpI")
                rinv = Ring(T, 2, [128, 8], F32, "rinv")
                impp = T([128, 4, 64], F32, "impp"); b_impp = Buf()
                imp = T([128, 2, 64], F32, "imp"); b_imp = Buf()
                score = T([128, 2, 64], F32, "score"); b_score = Buf()
                m8 = T([128, 2, 8], F32, "m8"); b_m8 = Buf()
                s1 = T([128, 2, 64], F32, "s1"); b_s1 = Buf()
                s2 = T([128, 2, 64], F32, "s2"); b_s2 = Buf()
                selq = T([128, 2, 64], BF16, "selq"); b_selq = Buf()
                self32 = T([128, 2, 64], F32, "self32"); b_self32 = Buf()
                abr = Ring(T, 2, [128, 4, 64], F32, "ab")
                cmr = Ring(T, 2, [128, NCH, 4, 128], BF16, "cm")

                bank_owner = {}
                side = collections.deque()

                def side_push(fns, grp=None):
                    for fn in fns:
                        side.append((fn, grp))
                        if grp is not None:
                            grp["pend"] = grp.get("pend", 0) + 1

                def side_pop(k):
                    for _ in range(k):
                        if not side:
                            return
                        fn, grp = side.popleft()
                        fn()
                        if grp is not None:
                            grp["pend"] -= 1

                def side_drain_group(grp):
                    while grp is not None and grp.get("pend", 0) > 0:
                        side_pop(1)

                def side_drain_all():
                    while side:
                        side_pop(1)

                def evac_ops(grp):
                    tl = grp["tile"]
                    qi, g, p, b = tl["qi"], grp["g"], grp["p"], grp["b"]
                    acc, bacc = grp["acc"]
                    osb, b_osb = tl["osb"]
                    av = acc[:, 0:260].rearrange("p (c e) -> p c e", e=65)
                    h0 = (8 * g + p) * 3 + b
                    ov = osb[:].rearrange("p (h d) -> p h d", d=64)[:, 8 * g + p:8 * g + p + 7:2, :]
                    R = {}

                    def o1():
                        R["rv"], R["brv"] = rinv.next()
                        Sx.op("dve", lambda e: e.tensor_scalar(out=R["rv"][:, 0:4], in0=av[:, :, 64], scalar1=1e-30, scalar2=None, op0=ALU.max), [bacc], [R["brv"]])

                    def o2():
                        Sx.op("dve", lambda e: e.reciprocal(out=R["rv"][:, 0:4], in_=R["rv"][:, 0:4]), [R["brv"]], [R["brv"]])

                    def o3():
                        Sx.op("dve", lambda e: e.tensor_tensor(out=R["rv"][:, 4:8], in0=R["rv"][:, 0:4], in1=gat[:, qi, h0:h0 + 19:6], op=ALU.mult), [R["brv"], b_gat], [R["brv"]])
                    ops = [o1, o2, o3]
                    if b == 0:
                        accI, baccI = grp["accI"]

                        def o4():
                            Sx.op("dve", lambda e: e.tensor_tensor(out=ov, in0=av[:, :, 0:64], in1=R["rv"][:, 4:8].unsqueeze(2).to_broadcast([128, 4, 64]), op=ALU.mult),
                                  [bacc, R["brv"]], [b_osb])

                        def o5():
                            R["tI"], R["btI"] = tmpIr.next()
                            Sx.op("dve", lambda e: e.tensor_tensor(out=R["tI"][:], in0=accI[:, 0:256].rearrange("p (c j) -> p c j", j=64),
                                                                   in1=R["rv"][:, 0:4].unsqueeze(2).to_broadcast([128, 4, 64]), op=ALU.mult), [baccI, R["brv"]], [R["btI"]])

                        def o6():
                            Sx.op("dve", lambda e: e.tensor_reduce(out=impp[:, 2 * g + p, :], in_=R["tI"][:].rearrange("p c j -> p j c"), axis=AX.X, op=ALU.add),
                                  [R["btI"]], [b_impp])
                        ops += [o4, o5, o6]
                        if g == 1 and p == 1:
                            ops += topk_ops(tl)
                    else:
                        def o4():
                            R["t4"], R["bt4"] = tmp4r.next()
                            Sx.op("dve", lambda e: e.tensor_tensor(out=R["t4"][:], in0=av[:, :, 0:64], in1=R["rv"][:, 4:8].unsqueeze(2).to_broadcast([128, 4, 64]), op=ALU.mult),
                                  [bacc, R["brv"]], [R["bt4"]])

                        def o5():
                            Sx.op("pool", lambda e: e.tensor_tensor(out=ov, in0=ov, in1=R["t4"][:], op=ALU.add), [R["bt4"], b_osb], [b_osb])
                        ops += [o4, o5]
                    return ops

                def topk_ops(tl):
                    qi = tl["qi"]
                    ops = []

                    def mk(eng, fn, rd, wr):
                        ops.append(lambda: Sx.op(eng, fn, rd, wr))
                    ab_ = lambda: tl["ab"]
                    ops.append(lambda: Sx.op("pool", lambda e: e.tensor_tensor(out=imp[:], in0=impp[:, 0:4:2, :], in1=impp[:, 1:4:2, :], op=ALU.add), [b_impp], [b_imp]))
                    ops.append(lambda: Sx.op("pool", lambda e: e.tensor_tensor(out=score[:], in0=imp[:], in1=ab_()[0][:, 0:2, :], op=ALU.mult), [b_imp, ab_()[1]], [b_score]))
                    ops.append(lambda: Sx.op("pool", lambda e: e.tensor_tensor(out=score[:], in0=score[:], in1=ab_()[0][:, 2:4, :], op=ALU.add), [b_score, ab_()[1]], [b_score]))
                    for g in range(2):
                        mk("dve", lambda e, g=g: e.max(out=m8[:, g, :], in_=score[:, g, :]), [b_score], [b_m8])
                        mk("dve", lambda e, g=g: e.match_replace(out=s1[:, g, :], in_to_replace=m8[:, g, :], in_values=score[:, g, :], imm_value=-1e9), [b_score, b_m8], [b_s1])
                        mk("dve", lambda e, g=g: e.max(out=m8[:, g, :], in_=s1[:, g, :]), [b_s1], [b_m8])
                        mk("dve", lambda e, g=g: e.match_replace(out=s2[:, g, :], in_to_replace=m8[:, g, :], in_values=s1[:, g, :], imm_value=-1e9), [b_s1, b_m8], [b_s2])
                    mk("pool", lambda e: e.tensor_single_scalar(out=s2[:], in_=s2[:], scalar=-1e8, op=ALU.is_lt), [b_s2], [b_s2])
                    mk("dve", lambda e: e.scalar_tensor_tensor(out=self32[:], in0=score[:], scalar=-0.5, in1=s2[:], op0=ALU.is_gt, op1=ALU.mult),
                       [b_score, b_s2], [b_self32])
                    mk("pool", lambda e: e.tensor_scalar(out=selq[:], in0=self32[:], scalar1=30000.0, scalar2=-30000.0, op0=ALU.mult, op1=ALU.add), [b_self32], [b_selq])
                    if dbg:
                        ops.append(lambda: Sx.dma("sp", dbg_imp[qi * 128:(qi + 1) * 128], imp[:], [b_imp], []))
                        ops.append(lambda: Sx.dma("sp", dbg_sel[qi * 128:(qi + 1) * 128], self32[:], [b_self32], []))
                    ops += topk_T_ops(tl)
                    return ops

                def topk_T_ops(tl):
                    R = {}

                    def pe_part():
                        R["sT"], R["bsT"] = selTr.next()
                        R["px"], R["pbx"] = rotX.next()
                        pxv = R["px"][:].bitcast(BF16)
                        for g in range(2):
                            Sx.op("pe", lambda e, g=g: e.transpose(out=pxv[0:64, g * 128:(g + 1) * 128], in_=selq[:, g, :], identity=ident[:]), [b_selq, b_ident], [R["pbx"]])

                    def act_part():
                        pxv = R["px"][:].bitcast(BF16)
                        Sx.op("act", lambda e: e.activation(out=R["sT"][0:64], in_=pxv[0:64, 0:256].rearrange("p (g q) -> p g q", g=2).unsqueeze(2).to_broadcast([64, 2, 4, 128]), func=AF.Copy),
                              [R["pbx"]], [R["bsT"]])
                        tl["selT"] = (R["sT"], R["bsT"])
                    nop = lambda: None
                    return [pe_part, nop, nop, nop, act_part]

                def build_selmask(tl, k0):
                    qi = tl["qi"]
                    if "selT" not in tl:
                        side_drain_all()
                    sT, bsT = tl["selT"]
                    nk = min(2, qi + 1 - k0)
                    px, pbx = rotI.next()
                    side_drain_group(bank_owner.get(id(pbx)))
                    bsm = b_smp[k0 // 2]
                    for kk_ in range(nk):
                        kc = k0 + kk_
                        Sx.op("pe", lambda e, kc=kc, kk_=kk_: e.matmul(px[:, kk_ * 256:(kk_ + 1) * 256], lhsT=Esb[:, kc * 128:(kc + 1) * 128],
                                                                      rhs=sT[:].rearrange("p g q -> p (g q)"), start=True, stop=True), [b_E, bsT], [pbx])

                    def act_part():
                        Sx.op("act", lambda e: e.activation(out=selmask[:, k0:k0 + nk].rearrange("p k g q -> p (k g q)"), in_=px[:, 0:nk * 256], func=AF.Copy),
                              [pbx], [bsm])
                        if k0 <= qi < k0 + nk:
                            Sx.op("pool", lambda e: e.tensor_tensor(out=selmask[:, qi], in0=selmask[:, qi], in1=mdiag[:].unsqueeze(1).to_broadcast([128, 2, 128]), op=ALU.mult),
                                  [bsm, b_md], [bsm])
                    return act_part

                def tile_pre(tl):
                    qi = tl["qi"]
                    if qi % 4 == 0:
                        qcur[0] = qbr.next()
                        qb, bq = qcur[0]
                        Sx.dma("sp", qb[:], qT.rearrange("(c p) s -> p c s", p=128)[:, :, (qi // 4) * 512:(qi // 4 + 1) * 512], [b_q[qi // 4]], [bq])
                    tl["qb"] = qcur[0]
                    tl["ab"] = abr.next()
                    tl["cm"] = cmr.next()
                    tl["osb"] = osbr.next()
                    Sx.dma("sp", tl["ab"][0][:], c_tkAB[qi], [], [tl["ab"][1]])
                    Sx.dma("sp", tl["cm"][0][:], c_cmask[qi], [], [tl["cm"][1]])

                def tile_tail_a(tl):
                    osb, b_osb = tl["osb"]
                    Sx.op("act", lambda e: e.activation(out=obf[:], in_=osb[:], func=AF.Copy), [b_osb], [b_obf])

                def tile_tail_b_ops(tl):
                    qi = tl["qi"]
                    q0 = (qi % 4) * 128
                    ops = []
                    nop = lambda: None
                    for c2 in range(2):
                        R = {}

                        def pe_part(c2=c2, R=R):
                            R["px"], R["pbx"] = rotX.next()
                            pxv = R["px"][:].bitcast(BF16)
                            for c in range(4):
                                cc = c2 * 4 + c
                                Sx.op("pe", lambda e, c=c, cc=cc: e.transpose(out=pxv[:, c * 128:(c + 1) * 128], in_=obf[:, cc * 128:(cc + 1) * 128], identity=ident[:]),
                                      [b_obf, b_ident], [R["pbx"]])

                        def act_part(c2=c2, R=R):
                            pxv = R["px"][:].bitcast(BF16)
                            Sx.op("act", lambda e: e.activation(out=ynst[:, c2 * 4:(c2 + 1) * 4, q0:q0 + 128], in_=pxv[:, 0:512].rearrange("p (c q) -> p c q", c=4), func=AF.Copy),
                                  [R["pbx"]], [b_ynst])
                            if c2 == 1 and qi % 4 == 3:
                                Sx.dma("sp", ynT.rearrange("(c p) s -> p c s", p=128)[:, :, (qi // 4) * 512:(qi // 4 + 1) * 512], ynst[:], [b_ynst], [b_yn[qi // 4]])
                        ops += [pe_part, nop, nop, nop, act_part]
                    return ops

                def emit_S(st):
                    grp = st["grp"]
                    tl = grp["tile"]
                    if st["tile_first"]:
                        tile_pre(tl)
                    b, g, p, kc = grp["b"], grp["g"], grp["p"], st["kc"]
                    qb, bq = tl["qb"]
                    q0 = (tl["qi"] % 4) * 128
                    rhs_q = qb[:, 4 * g:4 * g + 4, q0:q0 + 128]
                    ps, pbs = rotS.next()
                    if b == 0:
                        lhs, blhs = kcm[:, g, p, kc * 128:(kc + 1) * 128], b_kcm
                    elif b == 1:
                        lhs, blhs = ks2[:, g, p, kc * 128:(kc + 1) * 128], b_ks2
                    else:
                        lhs, blhs = kw2[:, g, p, kc * 128:(kc + 1) * 128], b_kw2
                    if b == 0:
                        cm_, bcm_ = tl["cm"]
                        Sx.op("pe", lambda e: e.matmul(ps[:, 0:512], lhsT=lhs, rhs=rhs_q, start=True, stop=False), [blhs, bq], [pbs])
                        Sx.op("pe", lambda e: e.matmul(ps[:, 0:512], lhsT=ident[:], rhs=cm_[:, kc, :, :], start=False, stop=True), [b_ident, bcm_], [pbs])
                    elif b == 2:
                        qi_ = tl["qi"]
                        edge = (negdiag4, b_nd4) if kc == qi_ else ((negfar4, b_nf4) if kc == qi_ - 4 else None)
                        Sx.op("pe", lambda e: e.matmul(ps[:, 0:512], lhsT=lhs, rhs=rhs_q, start=True, stop=(edge is None)), [blhs, bq], [pbs])
                        if edge is not None:
                            Sx.op("pe", lambda e: e.matmul(ps[:, 0:512], lhsT=ident[:], rhs=edge[0][:], start=False, stop=True), [b_ident, edge[1]], [pbs])
                    else:
                        if "selT" not in tl:
                            side_drain_all()
                        nT, bnT = tl["selT"]
                        qi = tl["qi"]
                        Sx.op("pe", lambda e: e.matmul(ps[:, 0:512], lhsT=lhs, rhs=rhs_q, start=True, stop=False), [blhs, bq], [pbs])
                        Sx.op("pe", lambda e: e.matmul(ps[:, 0:512], lhsT=Esb[:, kc * 128:(kc + 1) * 128], rhs=nT[:, g, :, :], start=False, stop=(kc != qi)), [b_E, bnT], [pbs])
                        if kc == qi:
                            Sx.op("pe", lambda e: e.matmul(ps[:, 0:512], lhsT=ident[:], rhs=negdiag4[:], start=False, stop=True), [b_ident, b_nd4], [pbs])
                    st["ps"], st["pbs"] = ps, pbs

                def emit_mid(st):
                    grp = st["grp"]
                    tl = grp["tile"]
                    qi = tl["qi"]
                    b, g, kc = grp["b"], grp["g"], st["kc"]
                    ps, pbs = st["ps"], st["pbs"]
                    pt, bpt = ptr.next()
                    Sx.op("act", lambda e: e.activation(out=pt[:], in_=ps[:, 0:512], func=AF.Exp, scale=0.125), [pbs], [bpt])
                    ptv = pt[:].rearrange("p (c q) -> p c q", c=4)
                    mk = None
                    if mk is not None:
                        Sx.op("dve", lambda e: e.tensor_tensor(out=ptv, in0=ptv, in1=mk.unsqueeze(1).to_broadcast([128, 4, 128]), op=ALU.mult), [bpt, bmk], [bpt])
                    st["pt"], st["bpt"] = pt, bpt

                def emit_PV(st):
                    grp = st["grp"]
                    b, g, kc = grp["b"], grp["g"], st["kc"]
                    pt, bpt = st["pt"], st["bpt"]
                    if st["first"]:
                        grp["acc"] = rotAcc.next()
                        side_drain_group(bank_owner.get(id(grp["acc"][1])))
                        bank_owner[id(grp["acc"][1])] = grp
                        if b == 0:
                            grp["accI"] = rotI.next()
                            side_drain_group(bank_owner.get(id(grp["accI"][1])))
                            bank_owner[id(grp["accI"][1])] = grp
                    acc, bacc = grp["acc"]
                    for c in range(4):
                        if b == 0:
                            accI, baccI = grp["accI"]
                            Sx.op("pe", lambda e, c=c: e.matmul(acc[:, c * 65:(c + 1) * 65], lhsT=pt[:, c * 128:(c + 1) * 128], rhs=vcm[:, kc, g, 0:65],
                                                                start=(st["first"] and c == 0), stop=False, skip_group_check=True), [bpt, b_vcm], [bacc])
                            Sx.op("pe", lambda e, c=c: e.matmul(accI[:, c * 64:(c + 1) * 64], lhsT=pt[:, c * 128:(c + 1) * 128], rhs=vcm[:, kc, g, 65:129],
                                                                start=(st["first"] and c == 0), stop=False, skip_group_check=True), [bpt, b_vcm], [baccI])
                        else:
                            vT, bvT = (vs1, b_vs1) if b == 1 else (vw1, b_vw1)
                            Sx.op("pe", lambda e, c=c: e.matmul(acc[:, c * 65:(c + 1) * 65], lhsT=pt[:, c * 128:(c + 1) * 128], rhs=vT[:, kc, g, :],
                                                                start=(st["first"] and c == 0), stop=False, skip_group_check=True), [bpt, bvT], [bacc])

                LA = 3
                EV_DELAY = 2
                qcur = [None]
                osbr = Ring(T, 3, [128, D], F32, "osb")
                selTr = Ring(T, 2, [128, 2, 4, 128], BF16, "negT4")
                for sT_ in selTr.t:
                    Sx.op("pool", lambda e, sT_=sT_: e.memset(sT_[:], 0.0), [], [selTr.b[selTr.t.index(sT_)]])
                b_smp = [Buf() for _ in range((NQ + 1) // 2)]
                steps = []
                tiles = [{"qi": qi} for qi in range(NQ)]

                def add_group(tl, b, g, p, first_of_tile=False, tile_last=False):
                    qi = tl["qi"]
                    if b == 0:
                        kcs = list(range(NCH))
                    elif b == 1:
                        kcs = list(range(qi + 1))
                    else:
                        kcs = list(range(max(0, qi - 4), qi + 1))
                    grp = {"b": b, "g": g, "p": p, "acc": None, "tile": tl, "tile_last": tile_last}
                    for ii, kc in enumerate(kcs):
                        steps.append({"grp": grp, "kc": kc, "first": ii == 0, "last": ii == len(kcs) - 1, "tile_first": first_of_tile and ii == 0})

                def add_sel(tl):
                    for g in range(2):
                        for p in range(2):
                            add_group(tl, 1, g, p, tile_last=(g == 1 and p == 1))

                for qi in range(NQ):
                    tl = tiles[qi]
                    for g in range(2):
                        for p in range(2):
                            add_group(tl, 0, g, p, first_of_tile=(g == 0 and p == 0))
                            add_group(tl, 2, g, p)
                    if qi >= 1:
                        add_sel(tiles[qi - 1])
                add_sel(tiles[NQ - 1])
                n = len(steps)
                SIDE_RATE = 2

                for i in range(min(LA, n)):
                    emit_S(steps[i])
                for i in range(n):
                    st = steps[i]
                    for st_ in steps[i:i + 2]:
                        f_ = st_.pop("sm_act", None)
                        if f_ is not None:
                            f_()
                    emit_mid(st)
                    side_pop(SIDE_RATE)
                    emit_PV(st)
                    if st["last"]:
                        grp = st["grp"]
                        side_push(evac_ops(grp), grp)
                        if grp["tile_last"]:
                            side_push([lambda tl=grp["tile"]: tile_tail_a(tl), lambda: None, lambda: None, lambda: None] + tile_tail_b_ops(grp["tile"]), None)
                    if i + LA < n:
                        emit_S(steps[i + LA])
                side_drain_all()
                Sx.barrier()

        def phase_D(l, xsrc, after_weights):
            with ExitStack() as es:
                T = mkT(es)
                rotP = Rot([0, 1, 2, 3])
                rotO = Rot([4, 5, 6, 7])
                wpp = T([128, 4, D], BF16, "wpp"); b_wpp = Buf()
                wpn = T([128, 8, D], BF16, "wpn"); b_wpn = Buf()
                wo = T([128, 8, D], BF16, "wo"); b_wo = Buf()
                for k in range(4):
                    Sx.dma("pool", wpp[:, k, :], w_pp[l, k * 128:(k + 1) * 128, :], [], [b_wpp])
                for k in range(8):
                    Sx.dma("pool", wpn[:, k, :], w_pn[l, k * 128:(k + 1) * 128, :], [], [b_wpn])
                    Sx.dma("pool", wo[:, k, :], w_out[l, k * 128:(k + 1) * 128, :], [], [b_wo])
                after_weights()
                ypt = T([128, 4, 512], BF16, "ypt"); b_ypt = Buf()
                ynt = T([128, 8, 512], BF16, "ynt"); b_ynt = Buf()
                gmt = T([128, 16, 512], BF16, "gmt"); b_gmt = Buf()
                xtr = Ring(T, 2, [128, 4, D], F32, "xtD")
                mg = T([128, 8, 512], BF16, "mg"); b_mg = Buf()
                t1r = Ring(T, 2, [128, 512], F32, "t1D")
                t2r = Ring(T, 2, [128, 512], F32, "t2D")

                def loadD(tt, which):
                    tsl_ = slice(tt * 512, tt * 512 + 512)
                    if which == 0:
                        Sx.dma("sp", ypt[:], ypT.rearrange("(c p) s -> p c s", p=128)[:, :, tsl_], [b_yp[tt]], [b_ypt])
                        Sx.dma("sp", ynt[:], ynT.rearrange("(c p) s -> p c s", p=128)[:, :, tsl_], [b_yn[tt]], [b_ynt])
                    elif which == 1:
                        Sx.dma("sp", gmt[:], gmT.rearrange("(c p) s -> p c s", p=128)[:, :, tsl_], [b_gm[tt]], [b_gmt])
                    else:
                        xt_, bxt_ = xtr.next()
                        Sx.dma("sp", xt_[:], xsrc[tsl_].rearrange("(j p) f -> p j f", p=128), xres_bufs(tt * 512, 512) if xsrc is xres else [], [bxt_])
                        return xt_, bxt_

                loadD(0, 0)
                loadD(0, 1)
                cur = loadD(0, 2)
                for tt in range(NT):
                    t0 = tt * 512
                    tsl = slice(t0, t0 + 512)
                    xt, b_xt = cur
                    for oc in range(8):
                        pa, pba = rotP.next()
                        for k in range(4):
                            Sx.op("pe", lambda e, k=k: e.matmul(pa[:, 0:512], lhsT=wpp[:, k, oc * 128:(oc + 1) * 128], rhs=ypt[:, k, :], start=(k == 0), stop=(k == 3)),
                                  [b_wpp, b_ypt], [pba])
                        pn, pbn = rotP.next()
                        for k in range(8):
                            Sx.op("pe", lambda e, k=k: e.matmul(pn[:, 0:512], lhsT=wpn[:, k, oc * 128:(oc + 1) * 128], rhs=ynt[:, k, :], start=(k == 0), stop=(k == 7)),
                                  [b_wpn, b_ynt], [pbn])
                        t1, bt1 = t1r.next()
                        t2, bt2 = t2r.next()
                        Sx.op("dve", lambda e: e.tensor_tensor(out=t1[:], in0=pa[:, 0:512], in1=gmt[:, oc, :], op=ALU.mult), [pba, b_gmt], [bt1])
                        Sx.op("dve", lambda e: e.tensor_tensor(out=t2[:], in0=pn[:, 0:512], in1=gmt[:, 8 + oc, :], op=ALU.mult), [pbn, b_gmt], [bt2])
                        Sx.op("dve", lambda e: e.tensor_tensor(out=mg[:, oc, :], in0=t1[:], in1=t2[:], op=ALU.add), [bt1, bt2], [b_mg])
                    if tt + 1 < NT:
                        loadD(tt + 1, 0)
                        loadD(tt + 1, 1)
                        cur = loadD(tt + 1, 2)
                    for j in range(4):
                        for hf in range(2):
                            po, pbo = rotO.next()
                            for k in range(8):
                                Sx.op("pe", lambda e, k=k: e.matmul(po[:, 0:512], lhsT=mg[:, k, j * 128:(j + 1) * 128], rhs=wo[:, k, hf * 512:(hf + 1) * 512], start=(k == 0), stop=(k == 7)),
                                      [b_mg, b_wo], [pbo])
                            Sx.op("dve", lambda e: e.tensor_tensor(out=xt[:, j, hf * 512:(hf + 1) * 512], in0=po[:, 0:512], in1=xt[:, j, hf * 512:(hf + 1) * 512], op=ALU.add),
                                  [pbo, b_xt], [b_xt])
                    Sx.dma("sp", xres[tsl].rearrange("(j p) f -> p j f", p=128), xt[:], [b_xt], xres_bufs(t0, 512))
                Sx.barrier()

        def phase_E(l, last, wf1, b_wf1):
            with ExitStack() as es:
                T = mkT(es)
                rotF = Rot([0, 1, 2, 3])
                rotO = Rot([4, 5, 6, 7])
                wf2 = T([128, 32, D], BF16, "wf2"); b_wf2 = Buf()
                for k8 in range(4):
                    Sx.dma("pool", wf2[:, k8 * 8:(k8 + 1) * 8, :], w_ff2[l, k8 * 1024:(k8 + 1) * 1024, :].rearrange("(k p) f -> p k f", p=128), [], [b_wf2])
                xtr = Ring(T, 2, [128, 2, D], F32, "xtE")
                hn = T([128, 2, D], BF16, "hnE"); b_hn = Buf()
                hT = T([128, 8, 256], BF16, "hTE"); b_hT = Buf()
                ss = T([128, 8], F32, "ssE"); b_ss = Buf()
                actT = T([128, 32, 256], BF16, "actT"); b_act = Buf()
                rlr = Ring(T, 3, [128, 256], F32, "rl")
                if last:
                    nf = T([128, D], F32, "nf"); b_nf = Buf()
                    Sx.dma("sp", nf[:], norm_final.partition_broadcast(128), [], [b_nf])
                    junk = T([128, D], BF16, "junk"); b_junk = Buf()
                    s2 = T([128, 8], F32, "ssF"); b_s2 = Buf()

                def prepE(tt):
                    xt, b_xt = xtr.next()
                    Sx.dma("sp", xt[:], xres[tt * 256:(tt + 1) * 256].rearrange("(j p) f -> p j f", p=128), xres_bufs(tt * 256, 256), [b_xt])
                    norm_stats(xt, b_xt, 2, hn, b_hn, ss, b_ss)
                    return xt, b_xt

                cur = prepE(0)
                for tt in range(NT2):
                    t0 = tt * 256
                    tsl = slice(t0, t0 + 256)
                    xt, b_xt = cur
                    if tt == 0:
                        norm_tr(2, gmlp[:, l, :], b_gmlp, hn, b_hn, hT, b_hT, rotF)
                    for fc in range(32):
                        pt, pb = rotF.next()
                        for k in range(8):
                            Sx.op("pe", lambda e, k=k: e.matmul(pt[:, 0:256], lhsT=wf1[:, k, fc * 128:(fc + 1) * 128], rhs=hT[:, k, :], start=(k == 0), stop=(k == 7)),
                                  [b_wf1, b_hT], [pb])
                        rl, brl = rlr.next()
                        Sx.op("act", lambda e: e.activation(out=rl[:], in_=pt[:, 0:256], func=AF.Relu), [pb], [brl])
                        Sx.op("dve", lambda e: e.tensor_tensor(out=actT[:, fc, :], in0=rl[:], in1=rl[:], op=ALU.mult), [brl], [b_act])
                        if fc == 6 and tt + 1 < NT2:
                            cur = prepE(tt + 1)
                    if tt + 1 < NT2:
                        norm_tr(2, gmlp[:, l, :], b_gmlp, hn, b_hn, hT, b_hT, rotF)
                    for j in range(2):
                        for hf in range(2):
                            po, pbo = rotO.next()
                            for k in range(32):
                                Sx.op("pe", lambda e, k=k: e.matmul(po[:, 0:512], lhsT=actT[:, k, j * 128:(j + 1) * 128], rhs=wf2[:, k, hf * 512:(hf + 1) * 512], start=(k == 0), stop=(k == 31)),
                                      [b_act, b_wf2], [pbo])
                            Sx.op("dve", lambda e: e.tensor_tensor(out=xt[:, j, hf * 512:(hf + 1) * 512], in0=po[:, 0:512], in1=xt[:, j, hf * 512:(hf + 1) * 512], op=ALU.add),
                                  [pbo, b_xt], [b_xt])
                    if not last:
                        Sx.dma("sp", xres[tsl].rearrange("(j p) f -> p j f", p=128), xt[:], [b_xt], xres_bufs(t0, 256))
                    else:
                        for j in range(2):
                            Sx.op("act", lambda e, j=j: e.activation(out=junk[:], in_=xt[:, j, :], func=AF.Square, accum_out=s2[:, j:j + 1]), [b_xt], [b_junk, b_s2])
                        Sx.op("act", lambda e: e.activation(out=s2[:, 4:6], in_=s2[:, 0:2], func=AF.Sqrt, scale=1.0 / D, bias=epst[:, 0:1]), [b_s2, b_eps], [b_s2])
                        Sx.op("dve", lambda e: e.reciprocal(out=s2[:, 4:6], in_=s2[:, 4:6]), [b_s2], [b_s2])
                        for j in range(2):
                            Sx.op("dve", lambda e, j=j: e.tensor_scalar(out=xt[:, j, :], in0=xt[:, j, :], scalar1=s2[:, 4 + j:5 + j], scalar2=None, op0=ALU.mult), [b_xt, b_s2], [b_xt])
                            Sx.op("pool", lambda e, j=j: e.tensor_tensor(out=xt[:, j, :], in0=xt[:, j, :], in1=nf[:], op=ALU.mult), [b_xt, b_nf], [b_xt])
                        Sx.dma("sp", out[tsl].rearrange("(j p) f -> p j f", p=128), xt[:], [b_xt], [])
                Sx.barrier()

        for l in range(depth):
            xsrc = x_in if l == 0 else xres
            phase_A(l, xsrc)
            if stop_after == "A":
                break
            Sx.new_epoch()
            phase_BC(l)
            if stop_after in ("B", "C"):
                break
            Sx.new_epoch()
            with ExitStack() as esw:
                wf1 = mkT(esw)([128, 8, 4096], BF16, "wf1"); b_wf1 = Buf()

                def load_wf1():
                    for k2 in range(4):
                        Sx.dma("pool", wf1[:, k2 * 2:(k2 + 1) * 2, :], w_ff1[l, k2 * 256:(k2 + 1) * 256, :].rearrange("(k p) f -> p k f", p=128), [], [b_wf1])
                phase_D(l, xsrc, load_wf1)
                if stop_after == "D":
                    break
                phase_E(l, l == depth - 1, wf1, b_wf1)
            if l < depth - 1:
                Sx.new_epoch()
        Sx.barrier()
    return nc


WKEYS = ["norm_mix", "w_in", "w_pool", "pool_scale", "pe_k", "pe_v", "w_ck1", "w_ck2", "w_cv1", "w_cv2",
         "w_proj_pool", "w_proj_nsa", "w_out", "norm_mlp", "w_ff1", "w_ff2", "norm_final"]


def kernel(**inputs):
    S, depth, B = 4096, 4, 8
    x = np.asarray(inputs["x"], dtype=np.float32)
    wts = {k: np.ascontiguousarray(np.asarray(inputs[k], dtype=np.float32)) for k in WKEYS}
    consts = host_consts(S)
    nc = build(S, depth)
    in_maps = []
    for b in range(B):
        m = dict(wts)
        m.update(consts)
        m["x"] = np.ascontiguousarray(x[b])
        in_maps.append(m)
    res = run_bass_kernel_spmd(nc, in_maps, core_ids=list(range(B)))
    return np.stack([np.asarray(r["out"], dtype=np.float32) for r in res.results], axis=0)
```
